# Optimizing a Trainium2 kernel written in Bass

```python
import math
import jax, jax.numpy as jnp
from jax import lax
import numpy as np

D_MODEL = 1024
BATCH = 32
SEQ = 2048
DEPTH = 1

D_MIX = D_MODEL
ATTN_WIDTH = D_MIX // 2
SSM_WIDTH = D_MIX - ATTN_WIDTH
QK_NOPE_DIM = 128
QK_ROPE_DIM = 64
V_HEAD_DIM = 128
N_ATTN_HEADS = ATTN_WIDTH // V_HEAD_DIM
Q_RANK = D_MODEL // 4
KV_RANK = D_MODEL // 8
ROPE_THETA = 10000.0
Q_BLOCK = 128
SSM_GROUP_CH = 16
SSM_GROUPS = SSM_WIDTH // SSM_GROUP_CH
SSM_STATE = 64
DT_MIN = 1e-3
DT_MAX = 1e-1
EPS = 1e-6
SPLITS = (Q_RANK, KV_RANK, QK_ROPE_DIM, ATTN_WIDTH, SSM_WIDTH, SSM_WIDTH)
D_IN = sum(SPLITS)

kernel_name = "hymba_mla_s5_sandwich_encoder"


def _rms_norm(x, g):
    xf = x.astype(jnp.float32)
    y = xf * lax.rsqrt(jnp.mean(xf * xf, axis=-1, keepdims=True) + EPS)
    return (y * g.astype(jnp.float32)).astype(x.dtype)


def _rope_tables(positions, dim):
    inv_freq = ROPE_THETA ** (-jnp.arange(0, dim, 2, dtype=jnp.float32) / dim)
    ang = positions.astype(jnp.float32)[..., None] * inv_freq
    return jnp.cos(ang), jnp.sin(ang)


def _apply_rope(t, cos, sin):
    tf = t.astype(jnp.float32)
    t1, t2 = jnp.split(tf, 2, axis=-1)
    return jnp.concatenate([t1 * cos - t2 * sin, t1 * sin + t2 * cos], axis=-1).astype(t.dtype)


def _mla_attention(z_q, z_kv, z_kr, positions, q_norm_g, w_uq, kv_norm_g, w_ukv):
    bsz, seq, _ = z_q.shape
    q = (_rms_norm(z_q, q_norm_g) @ w_uq).reshape(bsz, seq, N_ATTN_HEADS, QK_NOPE_DIM + QK_ROPE_DIM)
    q_nope, q_rope = q[..., :QK_NOPE_DIM], q[..., QK_NOPE_DIM:]
    kv = (_rms_norm(z_kv, kv_norm_g) @ w_ukv).reshape(bsz, seq, N_ATTN_HEADS, QK_NOPE_DIM + V_HEAD_DIM)
    k_nope, v = kv[..., :QK_NOPE_DIM], kv[..., QK_NOPE_DIM:]
    cos, sin = _rope_tables(positions, QK_ROPE_DIM)
    q_rope = _apply_rope(q_rope, cos[:, :, None, :], sin[:, :, None, :])
    k_rope = _apply_rope(z_kr, cos, sin)
    n_blocks = seq // Q_BLOCK
    scale = (QK_NOPE_DIM + QK_ROPE_DIM) ** -0.5

    def to_blocks(t):
        return t.reshape(bsz, n_blocks, Q_BLOCK, *t.shape[2:]).swapaxes(0, 1)

    def attend_block(args):
        qn, qr = args
        s = (jnp.einsum('bqhd,bkhd->bhqk', qn, k_nope, preferred_element_type=jnp.float32)
             + jnp.einsum('bqhr,bkr->bhqk', qr, k_rope, preferred_element_type=jnp.float32))
        p = jax.nn.softmax(s * scale, axis=-1).astype(v.dtype)
        return jnp.einsum('bhqk,bkhd->bqhd', p, v)

    o = lax.map(attend_block, (to_blocks(q_nope), to_blocks(q_rope)))
    return o.swapaxes(0, 1).reshape(bsz, seq, N_ATTN_HEADS * V_HEAD_DIM)


def _diag_scan_op(e_i, e_j):
    a_i, b_i = e_i
    a_j, b_j = e_j
    return a_j * a_i, a_j * b_i + b_j


def _s5_bidirectional(u, lam_re, lam_im, log_dt, b_re, b_im, c_re, c_im, d_skip):
    bsz, seq, _ = u.shape
    u_t = u.astype(jnp.float32).reshape(bsz, seq, SSM_GROUPS, SSM_GROUP_CH).transpose(1, 0, 2, 3)
    u_c = u_t.astype(jnp.complex64)
    y = jnp.zeros_like(u_t)
    for direction, reverse in ((0, False), (1, True)):
        lam = lax.complex(jnp.minimum(lam_re[direction].astype(jnp.float32), -1e-4),
                          lam_im[direction].astype(jnp.float32))
        dt = jnp.exp(log_dt[direction].astype(jnp.float32))[:, None]
        lam_bar = jnp.exp(lam * dt)
        b = lax.complex(b_re[direction].astype(jnp.float32), b_im[direction].astype(jnp.float32))
        b_bar = ((lam_bar - 1.0) / lam)[..., None] * b
        c = lax.complex(c_re[direction].astype(jnp.float32), c_im[direction].astype(jnp.float32))
        bu = jnp.einsum('sbgh,gph->sbgp', u_c, b_bar)
        a = jnp.broadcast_to(lam_bar, (seq, 1, SSM_GROUPS, SSM_STATE))
        _, h = lax.associative_scan(_diag_scan_op, (a, bu), reverse=reverse, axis=0)
        y = y + jnp.real(jnp.einsum('sbgp,ghp->sbgh', h, c))
    y = y + d_skip.astype(jnp.float32).reshape(SSM_GROUPS, SSM_GROUP_CH) * u_t
    return y.transpose(1, 0, 2, 3).reshape(bsz, seq, SSM_WIDTH).astype(u.dtype)


def setup_inputs(seed: int = 0) -> dict:
    key = jax.random.key(seed)
    ks = jax.random.split(key, 24)
    f32 = jnp.float32
    L, G, P, H = DEPTH, SSM_GROUPS, SSM_STATE, SSM_GROUP_CH

    def nrm(k, shape, std):
        return jax.random.normal(k, shape, f32) * std

    def gain(k, n):
        return 1.0 + 0.05 * jax.random.normal(k, (L, n), f32)

    x = jax.random.normal(ks[0], (BATCH, SEQ, D_MODEL), f32)
    inc = jax.random.randint(ks[1], (BATCH, SEQ), 1, 3, dtype=jnp.int32)
    positions = jnp.cumsum(inc, axis=1, dtype=jnp.int32) - 1
    n_idx = jnp.arange(P, dtype=f32)
    lam_re = -0.5 + 0.01 * jax.random.normal(ks[2], (L, 2, G, P), f32)
    lam_im = math.pi * n_idx + 0.01 * jax.random.normal(ks[3], (L, 2, G, P), f32)
    log_dt = jax.random.uniform(ks[4], (L, 2, G), f32, math.log(DT_MIN), math.log(DT_MAX))
    return {
        "x": x,
        "positions": positions,
        "pre_norm_g": gain(ks[5], D_MODEL),
        "w_in": nrm(ks[6], (L, D_MODEL, D_IN), D_MODEL ** -0.5),
        "q_norm_g": gain(ks[7], Q_RANK),
        "w_uq": nrm(ks[8], (L, Q_RANK, N_ATTN_HEADS * (QK_NOPE_DIM + QK_ROPE_DIM)), Q_RANK ** -0.5),
        "kv_norm_g": gain(ks[9], KV_RANK),
        "w_ukv": nrm(ks[10], (L, KV_RANK, N_ATTN_HEADS * (QK_NOPE_DIM + V_HEAD_DIM)), KV_RANK ** -0.5),
        "attn_out_g": gain(ks[11], ATTN_WIDTH),
        "s5_lam_re": lam_re,
        "s5_lam_im": lam_im,
        "s5_log_dt": log_dt,
        "s5_b_re": nrm(ks[12], (L, 2, G, P, H), (2.0 * H) ** -0.5),
        "s5_b_im": nrm(ks[13], (L, 2, G, P, H), (2.0 * H) ** -0.5),
        "s5_c_re": nrm(ks[14], (L, 2, G, H, P), (2.0 * P) ** -0.5),
        "s5_c_im": nrm(ks[15], (L, 2, G, H, P), (2.0 * P) ** -0.5),
        "s5_d": nrm(ks[16], (L, SSM_WIDTH), 1.0),
        "w_glu": nrm(ks[17], (L, SSM_WIDTH, SSM_WIDTH), SSM_WIDTH ** -0.5),
        "b_glu": nrm(ks[18], (L, SSM_WIDTH), 0.02),
        "ssm_out_g": gain(ks[19], SSM_WIDTH),
        "w_out": nrm(ks[20], (L, D_MIX, D_MODEL), D_MIX ** -0.5),
        "post_norm_g": gain(ks[21], D_MODEL),
    }


def reference(x, positions, pre_norm_g, w_in, q_norm_g, w_uq, kv_norm_g, w_ukv, attn_out_g,
              s5_lam_re, s5_lam_im, s5_log_dt, s5_b_re, s5_b_im, s5_c_re, s5_c_im, s5_d,
              w_glu, b_glu, ssm_out_g, w_out, post_norm_g):
    split_points = list(np.cumsum(SPLITS)[:-1])
    for l in range(DEPTH):
        h = _rms_norm(x, pre_norm_g[l])
        z = h @ w_in[l]
        z_q, z_kv, z_kr, g_attn, u_ssm, g_ssm = jnp.split(z, split_points, axis=-1)
        attn = _mla_attention(z_q, z_kv, z_kr, positions, q_norm_g[l], w_uq[l], kv_norm_g[l], w_ukv[l])
        attn = _rms_norm(attn, attn_out_g[l]) * jax.nn.silu(g_attn)
        ssm = _s5_bidirectional(u_ssm, s5_lam_re[l], s5_lam_im[l], s5_log_dt[l], s5_b_re[l], s5_b_im[l],
                                s5_c_re[l], s5_c_im[l], s5_d[l])
        ssm = jax.nn.gelu(ssm)
        ssm = ssm * jax.nn.sigmoid(ssm @ w_glu[l] + b_glu[l])
        ssm = _rms_norm(ssm, ssm_out_g[l]) * jax.nn.silu(g_ssm)
        mixed = jnp.concatenate([attn, ssm], axis=-1) @ w_out[l]
        x = x + _rms_norm(mixed, post_norm_g[l])
    return x
```

```python
import numpy as np
import concourse.bass as bass
import concourse.mybir as mybir

F32 = mybir.dt.float32
BF16 = mybir.dt.bfloat16
I32 = mybir.dt.int32
AF = mybir.ActivationFunctionType
ALU = mybir.AluOpType
AX = mybir.AxisListType

_DTSIZE = {}


def dtsize(dt):
    s = str(dt)
    if s in _DTSIZE:
        return _DTSIZE[s]
    if '32' in s:
        v = 4
    elif '16' in s:
        v = 2
    elif '64' in s:
        v = 8
    else:
        v = 1
    _DTSIZE[s] = v
    return v


GRAN = 16


def region_of(ap):
    t = ap.tensor
    name = t.name
    es = dtsize(ap.dtype)
    dims = ap.ap
    off = ap.offset
    space = str(ap.space)
    if 'DRAM' in space.upper() or 'HBM' in space.upper() or 'dram' in space.lower():
        lo = off
        hi = off
        for st, cnt in dims:
            if st >= 0:
                hi += st * (cnt - 1)
            else:
                lo += st * (cnt - 1)
        return (name, 'dram', 0, 1, lo * es, (hi + 1) * es)
    pstep, pcnt = dims[0]
    p0 = off // pstep if pstep > 0 else 0
    rem = off - p0 * pstep if pstep > 0 else off
    lo = rem
    hi = rem
    for st, cnt in dims[1:]:
        if st >= 0:
            hi += st * (cnt - 1)
        else:
            lo += st * (cnt - 1)
    return (name, 'sb', p0, p0 + pcnt, lo * es, (hi + 1) * es)


class Op:
    __slots__ = ('eng', 'fn', 'reads', 'writes', 'idx', 'deps', 'has_dep', 'ord', 'sem', 'semval', 'is_dma', 'waits', 'pre_wait')

    def __init__(self, eng, fn, reads, writes, is_dma):
        self.eng = eng
        self.fn = fn
        self.reads = reads
        self.writes = writes
        self.is_dma = is_dma
        self.deps = set()
        self.has_dep = False
        self.ord = None
        self.sem = None
        self.semval = None
        self.waits = []
        self.pre_wait = None


ENGS = ['pe', 'act', 'dve', 'pool', 'sp']


class Sched:
    def __init__(self, nc, n_dma_sems=12, sem_limit=4000):
        self.nc = nc
        self.ops = []
        self.n_dma_sems = n_dma_sems
        self.sem_limit = sem_limit
        self.track = {}

    def add(self, eng, fn, reads=(), writes=(), dma=False):
        rr = [region_of(a) if not isinstance(a, tuple) else a for a in reads]
        ww = [region_of(a) if not isinstance(a, tuple) else a for a in writes]
        op = Op(eng, fn, rr, ww, dma)
        op.idx = len(self.ops)
        self.ops.append(op)
        return op

    def _arr(self, name, b1):
        ng = (b1 + GRAN - 1) // GRAN + 1
        t = self.track.get(name)
        if t is None:
            t = {'w': np.full((4, ng), -1, np.int64), 'r': np.full((len(ENGS) + 1, 4, ng), -1, np.int64)}
            self.track[name] = t
        elif t['w'].shape[1] < ng:
            old = t['w'].shape[1]
            w = np.full((4, ng), -1, np.int64)
            w[:, :old] = t['w']
            r = np.full((len(ENGS) + 1, 4, ng), -1, np.int64)
            r[:, :, :old] = t['r']
            t['w'] = w
            t['r'] = r
        return t

    def analyze(self):
        ops = self.ops
        for op in ops:
            slot = len(ENGS) if op.is_dma else ENGS.index(op.eng)
            deps = set()
            for (name, sp, p0, p1, b0, b1) in op.reads:
                t = self._arr(name, b1)
                q0, q1 = p0 // 32, (p1 - 1) // 32 + 1
                g0, g1 = b0 // GRAN, (b1 - 1) // GRAN + 1
                w = t['w'][q0:q1, g0:g1]
                for d in np.unique(w):
                    if d >= 0:
                        deps.add(int(d))
            for (name, sp, p0, p1, b0, b1) in op.writes:
                t = self._arr(name, b1)
                q0, q1 = p0 // 32, (p1 - 1) // 32 + 1
                g0, g1 = b0 // GRAN, (b1 - 1) // GRAN + 1
                w = t['w'][q0:q1, g0:g1]
                for d in np.unique(w):
                    if d >= 0:
                        deps.add(int(d))
                r = t['r'][:, q0:q1, g0:g1]
                for d in np.unique(r):
                    if d >= 0:
                        deps.add(int(d))
            for (name, sp, p0, p1, b0, b1) in op.reads:
                t = self.track[name]
                q0, q1 = p0 // 32, (p1 - 1) // 32 + 1
                g0, g1 = b0 // GRAN, (b1 - 1) // GRAN + 1
                if op.is_dma:
                    prev = t['r'][slot, q0:q1, g0:g1]
                    for d in np.unique(prev):
                        if d >= 0:
                            deps.add(int(d))
                t['r'][slot, q0:q1, g0:g1] = op.idx
            for (name, sp, p0, p1, b0, b1) in op.writes:
                t = self.track[name]
                q0, q1 = p0 // 32, (p1 - 1) // 32 + 1
                g0, g1 = b0 // GRAN, (b1 - 1) // GRAN + 1
                t['w'][q0:q1, g0:g1] = op.idx
                t['r'][:, q0:q1, g0:g1] = -1
            deps.discard(op.idx)
            fdeps = set()
            for d in deps:
                o = ops[d]
                if (not o.is_dma) and (not op.is_dma) and o.eng == op.eng:
                    if op.eng == 'pe':
                        continue
                    raw = False
                    for (n1, _, p0, p1, b0, b1) in op.reads:
                        for (n2, _, P0, P1, B0, B1) in o.writes:
                            if n1 == n2 and p0 < P1 and P0 < p1 and b0 < B1 and B0 < b1:
                                raw = True
                    if not raw:
                        continue
                fdeps.add(d)
            op.deps = fdeps
            for d in fdeps:
                ops[d].has_dep = True

    def emit(self, final_wait_ops=()):
        nc = self.nc
        ops = self.ops
        for o in final_wait_ops:
            o.has_dep = True
        for o in ops:
            if o.is_dma:
                o.has_dep = True
        counts = {e: 0 for e in ENGS}
        for op in ops:
            if op.has_dep and not op.is_dma:
                counts[op.eng] += 1
        nsem_eng = {e: max(1, (counts[e] + self.sem_limit - 1) // self.sem_limit) for e in ENGS}
        import contextlib
        with contextlib.ExitStack() as es:
            eng_sems = {e: [es.enter_context(nc.semaphore(f"c_{e}_{i}")) for i in range(nsem_eng[e])] for e in ENGS}
            dma_sems = [es.enter_context(nc.semaphore(f"d_{i}")) for i in range(self.n_dma_sems)]
            cnt = {e: 0 for e in ENGS}
            dma_cnt = [0] * self.n_dma_sems
            dma_rr = 0
            for op in ops:
                if not op.has_dep:
                    continue
                if op.is_dma:
                    s = dma_rr % self.n_dma_sems
                    dma_rr += 1
                    if dma_cnt[s] > 0:
                        op.pre_wait = (dma_sems[s], 16 * dma_cnt[s])
                    dma_cnt[s] += 1
                    op.sem = dma_sems[s]
                    op.semval = 16 * dma_cnt[s]
                else:
                    k = cnt[op.eng]
                    cnt[op.eng] += 1
                    op.sem = eng_sems[op.eng][k // self.sem_limit]
                    op.semval = (k % self.sem_limit) + 1
            waited = {e: {} for e in ENGS}
            for op in ops:
                need = {}
                for d in op.deps:
                    o = ops[d]
                    key = id(o.sem)
                    if key not in need or need[key][1] < o.semval:
                        need[key] = (o.sem, o.semval)
                if op.pre_wait is not None:
                    key = id(op.pre_wait[0])
                    if key not in need or need[key][1] < op.pre_wait[1]:
                        need[key] = op.pre_wait
                wl = []
                for key, (s, v) in need.items():
                    if waited[op.eng].get(key, 0) >= v:
                        continue
                    waited[op.eng][key] = v
                    wl.append((s, v))
                op.waits = wl
            self.n_waits = sum(len(o.waits) for o in ops)
            print('SCHED ops', len(ops), 'incs', dict(cnt), 'dma', sum(dma_cnt), 'waits', self.n_waits, flush=True)
            finals = {}
            for o in final_wait_ops:
                finals[id(o.sem)] = (o.sem, max(o.semval, finals.get(id(o.sem), (None, 0))[1]))
            with nc.Block() as block:
                def run(engname, eng):
                    for op in ops:
                        if op.eng != engname:
                            continue
                        for (s, v) in op.waits:
                            eng.wait_ge(s, v)
                        ins = op.fn(eng)
                        if op.has_dep:
                            ins.then_inc(op.sem, 16 if op.is_dma else 1)
                    if engname == 'sp':
                        for (s, v) in finals.values():
                            eng.wait_ge(s, v)

                @block.tensor
                def _(eng):
                    run('pe', eng)

                @block.scalar
                def _(eng):
                    run('act', eng)

                @block.vector
                def _(eng):
                    run('dve', eng)

                @block.gpsimd
                def _(eng):
                    run('pool', eng)

                @block.sync
                def _(eng):
                    run('sp', eng)

import contextlib
import math
from concourse.bass_utils import run_bass_kernel_spmd

D = 1024
S = 2048
DIN = 1984
C_ZQ, C_ZKV, C_ZKR, C_GA, C_U, C_GS = 0, 256, 384, 448, 960, 1472
EPS = 1e-6
TWO_PI = 2.0 * math.pi
SCALE = 192.0 ** -0.5
NSEQ_CORE = 4


def build(nseq, stop_after=99, dbg=None):
    nc = bass.Bass("TRN2", target_bir_lowering=False)

    def din(name, shape, dt=F32):
        return nc.dram_tensor(name, shape, dt, kind="ExternalInput").ap()

    x = din("x", [nseq, S, D])
    pos = din("pos", [nseq, S], I32)
    w_in = din("w_in", [D, DIN])
    w_uq = din("w_uq", [256, 768])
    w_ukv = din("w_ukv", [128, 1024])
    w_glu = din("w_glu", [512, 512])
    w_out = din("w_out", [D, D])
    smallv = din("smallv", [128, 64])
    gpost = din("gpost", [1, D])
    s5p = din("s5p", [128, 4, 32])
    s5bc = din("s5bc", [128, 4, 512])
    dvec = din("dvec", [128, 32])
    cst = din("cst", [128, 3, 128])
    kvals = din("kvals", [128, 16])
    mvals = din("mvals", [128, 256])
    y = nc.dram_tensor("y", [nseq, S, D], F32, kind="ExternalOutput").ap()
    s5w = nc.dram_tensor("s5w", [16, 128, 1280], BF16, kind="Internal").ap()
    s5t = nc.dram_tensor("s5t", [32, 128, 1536], F32, kind="Internal").ap()

    es = contextlib.ExitStack()
    with es:
        def sb(name, shape, dt):
            return es.enter_context(nc.sbuf_tensor(name, shape, dt))

        Win_b = sb("Win_b", [128, 8, DIN], BF16)
        Wkrrot = sb("Wkrrot", [128, 8, 64], BF16)
        Wuq_b = sb("Wuq_b", [128, 2, 768], BF16)
        Wuqrot = sb("Wuqrot", [128, 2, 256], BF16)
        Wukv_b = sb("Wukv_b", [128, 1024], BF16)
        Wglu_b = sb("Wglu_b", [128, 4, 512], BF16)
        Wv_b = sb("Wv_b", [128, 512], BF16)
        Wout_b = sb("Wout_b", [128, 8, D], BF16)
        identb = sb("identb", [128, 128], BF16)
        onesb = sb("onesb", [128, 128], BF16)
        cstf = sb("cstf", [128, 3, 128], F32)
        gpost_b = sb("gpost_b", [128, D], F32)
        sv = sb("sv", [128, 64], F32)
        RHO8 = sb("RHO8", [128, 32], F32)
        cosT = sb("cosT", [64, S], BF16)
        sinT = sb("sinT", [64, S], BF16)
        stat = sb("stat", [128, 64], F32)
        Xs = sb("Xs", [128, 2, 2, 2, 256], BF16)
        BIG = sb("BIG", [128, 33024], F32)
        PS = es.enter_context(nc.psum_tensor("PS", [128, 4096], F32))

        identf = cstf[:, 0, :]
        maskf = cstf[:, 1, :]
        maskb = cstf[:, 2, :]
        gpre = sv[:, 0:8]
        gq = sv[:, 8:10]
        gkv = sv[:, 10:11]
        gout = sv[:, 11:19]
        bglu = sv[:, 19:23]
        invf = sv[0:64, 23:24]
        epsc = sv[:, 24:25]

        Sd = Sched(nc)
        A = Sd.add
        FINAL_OPS = []

        class Arena:
            def __init__(self):
                self.off = 0

            def reset(self, off=0):
                self.off = off

            def f32(self, n):
                a = BIG[:, self.off:self.off + n]
                self.off += n
                assert self.off <= 33024, self.off
                return a

            def bf(self, n):
                assert n % 2 == 0
                return self.f32(n // 2).bitcast(BF16)

            def i32(self, n):
                return self.f32(n).bitcast(I32)

        ar = Arena()
        ps_rr = {}

        def psb(n=1, lo=0, hi=8):
            k = ps_rr.get((lo, hi), lo)
            if k + n > hi:
                k = lo
            ps_rr[(lo, hi)] = k + n
            return PS[:, k * 512:(k + n) * 512]

        def scan(out, rho, in_):
            return A('dve', lambda e: e.tensor_tensor_scan(out=out, data0=rho, data1=in_, initial=0.0, op0=ALU.mult, op1=ALU.add),
                     [rho, in_], [out])

        DBG = {}

        def dump(name, ap):
            if dbg is None or name not in dbg:
                return
            t = nc.dram_tensor("dbg_" + name, list(ap.shape), ap.dtype, kind="ExternalOutput").ap()
            FINAL_OPS.append(dma(t, ap))

        def dma(out, in_, eng='sp'):
            return A(eng, lambda e: e.dma_start(out=out, in_=in_), [in_], [out], dma=True)

        def tt(eng, out, in0, in1, op):
            return A(eng, lambda e: e.tensor_tensor(out=out, in0=in0, in1=in1, op=op), [in0, in1], [out])

        def ts(eng, out, in0, s1, op0, s2=None, op1=None):
            rd = [in0] + [s for s in (s1, s2) if not isinstance(s, (int, float, type(None)))]
            if op1 is None:
                return A(eng, lambda e: e.tensor_scalar(out=out, in0=in0, scalar1=s1, scalar2=None, op0=op0), rd, [out])
            return A(eng, lambda e: e.tensor_scalar(out=out, in0=in0, scalar1=s1, scalar2=s2, op0=op0, op1=op1), rd, [out])

        def stt(eng, out, in0, sc, in1, op0, op1):
            rd = [in0, in1] + ([] if isinstance(sc, (int, float)) else [sc])
            return A('dve', lambda e: e.scalar_tensor_tensor(out=out, in0=in0, scalar=sc, in1=in1, op0=op0, op1=op1), rd, [out])

        def cp(eng, out, in_):
            if eng == 'act':
                return A('act', lambda e: e.copy(out=out, in_=in_), [in_], [out])
            return A(eng, lambda e: e.tensor_copy(out=out, in_=in_), [in_], [out])

        def act(out, in_, func, scale=1.0, bias=None, accum=None):
            rd = [in_] + ([] if bias is None else [bias]) + ([] if isinstance(scale, (int, float)) else [scale])
            wr = [out] + ([] if accum is None else [accum])
            kw = {}
            if bias is not None:
                kw['bias'] = bias
            if accum is not None:
                kw['accum_out'] = accum
            return A('act', lambda e: e.activation(out=out, in_=in_, func=func, scale=scale, **kw), rd, wr)

        def recip(eng, out, in_):
            return A('dve', lambda e: e.reciprocal(out=out, in_=in_), [in_], [out])

        def mms(lst, extra_reads=()):
            def fn(e):
                ins = None
                for (o, l, r, st, sp_, tp) in lst:
                    if tp is None:
                        ins = e.matmul(o, l, r, start=st, stop=sp_)
                    else:
                        ins = e.matmul(o, l, r, start=st, stop=sp_, tile_position=tp)
                return ins
            rd = []
            wr = []
            for (o, l, r, st, sp_, tp) in lst:
                rd += [l, r]
                wr.append(o)
            return A('pe', fn, rd + list(extra_reads), wr)

        def trs(lst):
            def fn(e):
                ins = None
                for (o, i_, idn) in lst:
                    ins = e.transpose(o, i_, idn)
                return ins
            rd = []
            wr = []
            for (o, i_, idn) in lst:
                rd += [i_, idn]
                wr.append(o)
            return A('pe', fn, rd, wr)

        def memset(eng, out, val):
            return A(eng, lambda e: e.memset(out, val), [], [out])

        dma(sv[:], smallv)
        dma(cstf[:], cst)
        dma(gpost_b[:], gpost.to_broadcast([128, D]))
        cp('dve', identb[:], identf)
        memset('pool', onesb[:], 1.0)
        memset('pool', Xs[:], 0.0)

        ar.reset()
        wst = [ar.f32(2048), ar.f32(2048)]
        engs3 = ['dve', 'pool', 'dve']

        def scale_cast(eng, out, in_, scal):
            if eng == 'act':
                return A('act', lambda e: e.activation(out=out, in_=in_, func=AF.Copy, scale=scal), [in_, scal], [out])
            return ts(eng, out, in_, scal, ALU.mult)

        k = 0
        for c in range(8):
            st_ = wst[k % 2]
            dma(st_[:, 0:DIN], w_in[c * 128:(c + 1) * 128, :])
            scale_cast(engs3[k % 3], Win_b[:, c, :], st_[:, 0:DIN], gpre[:, c:c + 1])
            k += 1
        for c in range(8):
            st_ = wst[k % 2]
            dma(st_[:, 0:D], w_out[c * 128:(c + 1) * 128, :])
            scale_cast(engs3[k % 3], Wout_b[:, c, :], st_[:, 0:D], gout[:, c:c + 1])
            k += 1
        for c in range(2):
            st_ = wst[k % 2]
            dma(st_[:, 0:768], w_uq[c * 128:(c + 1) * 128, :])
            scale_cast(engs3[k % 3], Wuq_b[:, c, :], st_[:, 0:768], gq[:, c:c + 1])
            k += 1
        st_ = wst[k % 2]
        dma(st_[:, 0:1024], w_ukv)
        scale_cast(engs3[k % 3], Wukv_b[:], st_[:, 0:1024], gkv)
        k += 1
        st_ = wst[k % 2]
        dma(st_[:].rearrange("p (c n) -> p c n", c=4), w_glu.rearrange("(c p) n -> p c n", p=128))
        cp(engs3[k % 3], Wglu_b[:].rearrange("p c n -> p (c n)"), st_[:])
        k += 1
        cp('pool', Wv_b[:].rearrange("p (h f) -> p h f", h=4), Wukv_b[:].rearrange("p (h f) -> p h f", h=4)[:, :, 128:256])
        ts('dve', Wkrrot[:, :, 0:32], Win_b[:, :, C_ZKR + 32:C_ZKR + 64], -1.0, ALU.mult)
        cp('dve', Wkrrot[:, :, 32:64], Win_b[:, :, C_ZKR:C_ZKR + 32])
        wq4 = Wuq_b[:].rearrange("p c (h f) -> p c h f", h=4)
        wr4 = Wuqrot[:].rearrange("p c (h f) -> p c h f", h=4)
        for c in range(2):
            ts('dve', wr4[:, c, :, 0:32], wq4[:, c, :, 160:192], -1.0, ALU.mult)
            cp('dve', wr4[:, c, :, 32:64], wq4[:, c, :, 128:160])

        def sincos(tu_ap, it_ap, itf_ap, fr_ap, s_out, c_out, s_scale=TWO_PI, eng='dve', s_out2=None):
            cp(eng, it_ap, tu_ap)
            cp(eng, itf_ap, it_ap)
            tt(eng, fr_ap, tu_ap, itf_ap, ALU.subtract)
            act(s_out, fr_ap, AF.Sin, scale=s_scale)
            if s_out2 is not None:
                act(s_out2, fr_ap, AF.Sin, scale=-s_scale)
            ts(eng, fr_ap, tu_ap, 0.25, ALU.add)
            cp(eng, it_ap, fr_ap)
            cp(eng, itf_ap, it_ap)
            tt(eng, fr_ap, fr_ap, itf_ap, ALU.subtract)
            act(c_out, fr_ap, AF.Sin, scale=TWO_PI)


        import os as _os
        S5CUT = int(_os.environ.get('S5CUT', '99'))

        def s5_setup():
            ar.reset(4096)
            P5 = ar.f32(128).rearrange("p (a c) -> p a c", a=4)
            BC = ar.f32(2048).rearrange("p (a c h) -> p a c h", a=4, c=32)
            KV = ar.f32(16)
            MV = ar.f32(256)
            DV = ar.f32(32)
            dma(P5, s5p)
            dma(BC.rearrange("p a c h -> p a (c h)"), s5bc)
            dma(KV, kvals)
            dma(MV, mvals)
            dma(DV, dvec)
            LAMRE, LAMIM, LOGDT, DSGN = P5[:, 0, :], P5[:, 1, :], P5[:, 2, :], P5[:, 3, :]
            BRE, BIM, CRE, CIM = BC[:, 0], BC[:, 1], BC[:, 2], BC[:, 3]

            def t32():
                return ar.f32(32)

            def t512():
                return ar.f32(512).rearrange("p (c k) -> p c k", c=32)

            lr, dt_, lrdt, th, den, rden, ca, cb, cr, ci, q1, q2, ff = [t32() for _ in range(13)]
            ts('dve', lr, LAMRE, -1e-4, ALU.min)
            act(dt_, LOGDT, AF.Exp)
            tt('dve', lrdt, lr, dt_, ALU.mult)
            tt('dve', th, LAMIM, dt_, ALU.mult)
            ang, lnm, mag, tu, itf, fr, sinv, cosv, PWr, PWi = [t512() for _ in range(10)]
            iti = ar.i32(512).rearrange("p (c k) -> p c k", c=32)
            thb = th.unsqueeze(2).to_broadcast([128, 32, 16])
            lrb = lrdt.unsqueeze(2).to_broadcast([128, 32, 16])
            kvb = KV.unsqueeze(1).to_broadcast([128, 32, 16])
            tt('dve', ang, thb, kvb, ALU.mult)
            tt('dve', lnm, lrb, kvb, ALU.mult)
            act(mag, lnm, AF.Exp)

            if S5CUT <= 1:
                return
            ts('dve', tu, ang, 1.0 / TWO_PI, ALU.mult)
            sincos(tu, iti, itf, fr, sinv, cosv)
            tt('dve', PWr, mag, cosv, ALU.mult)
            tt('dve', PWi, mag, sinv, ALU.mult)
            cp('dve', RHO8[:], mag[:, :, 15])
            if S5CUT <= 2:
                return
            ts('dve', ca, PWr[:, :, 8], -1.0, ALU.add)
            cp('dve', cb, PWi[:, :, 8])
            tt('dve', q1, lr, lr, ALU.mult)
            tt('dve', q2, LAMIM, LAMIM, ALU.mult)
            tt('dve', den, q1, q2, ALU.add)
            recip('dve', rden, den)
            tt('dve', q1, ca, lr, ALU.mult)
            tt('dve', q2, cb, LAMIM, ALU.mult)
            tt('dve', q1, q1, q2, ALU.add)
            tt('dve', cr, q1, rden, ALU.mult)
            tt('dve', q1, cb, lr, ALU.mult)
            tt('dve', q2, ca, LAMIM, ALU.mult)
            tt('dve', q1, q1, q2, ALU.subtract)
            tt('dve', ci, q1, rden, ALU.mult)
            Bbr, Bbi, w1, w2 = [t512() for _ in range(4)]
            crb = cr.unsqueeze(2).to_broadcast([128, 32, 16])
            cib = ci.unsqueeze(2).to_broadcast([128, 32, 16])
            tt('dve', w1, crb, BRE, ALU.mult)
            tt('dve', w2, cib, BIM, ALU.mult)
            tt('dve', Bbr, w1, w2, ALU.subtract)
            tt('dve', w1, crb, BIM, ALU.mult)
            tt('dve', w2, cib, BRE, ALU.mult)
            tt('dve', Bbi, w1, w2, ALU.add)
            ts('dve', ff, th, 8.0 / TWO_PI, ALU.mult)
            fi_ = ar.i32(32)
            fif = t32()
            cp('dve', fi_, ff)
            cp('dve', fif, fi_)
            tt('dve', ff, ff, fif, ALU.subtract)
            tt('dve', ff, ff, DSGN, ALU.mult)
            mark0 = ar.off

            if S5CUT <= 3:
                return
            TABST = ar.f32(4 * 1536).rearrange("p (c t x) -> p c t x", c=4, t=3)
            tq = ar.f32(1024).rearrange("p (c m) -> p c m", c=4)
            tqi = ar.i32(1024).rearrange("p (c m) -> p c m", c=4)
            tqf = ar.f32(1024).rearrange("p (c m) -> p c m", c=4)
            tqr = ar.f32(1024).rearrange("p (c m) -> p c m", c=4)
            mvb = MV.unsqueeze(1).to_broadcast([128, 4, 256])
            for bt in range(8):
                c0 = bt * 4
                eng = 'dve'
                fb = ff[:, c0:c0 + 4].unsqueeze(2).to_broadcast([128, 4, 256])
                tt(eng, tq, fb, mvb, ALU.mult)
                sincos(tq, tqi, tqf, tqr, TABST[:, :, 1, 0:256], TABST[:, :, 0, 0:256], eng=eng, s_out2=TABST[:, :, 1, 256:512])
                cp(eng, TABST[:, :, 0, 256:512], TABST[:, :, 0, 0:256])
                cp(eng, TABST[:, :, 2, 0:256], TABST[:, :, 1, 256:512])
                cp(eng, TABST[:, :, 2, 256:512], TABST[:, :, 1, 0:256])
                dma(s5t[c0:c0 + 4].rearrange("c p x -> p c x"), TABST.rearrange("p c t x -> p c (t x)"))

            if S5CUT <= 4:
                return
            ar.reset(mark0)
            TACC = ar.f32(32 * 128).rearrange("p (g x) -> p g x", g=32)
            TST = ar.bf(32 * 128).rearrange("p (g x) -> p g x", g=32)

            def q4():
                return ar.f32(8 * 128).rearrange("p (g i h) -> p g i h", g=8, i=8)

            def q4b():
                return ar.bf(8 * 128).rearrange("p (g i h) -> p g i h", g=8, i=8)

            Etr, Eti, Ttr, Tti, Gtr, Gti = [q4b() for _ in range(6)]
            v1, v2 = q4(), q4()
            EST = ar.bf(8 * 256).rearrange("p (g r x) -> p g r x", g=8, r=2)
            GST = ar.bf(8 * 256).rearrange("p (g r x) -> p g r x", g=8, r=2)

            def bc_pw(pw, c0, sl):
                return pw[:, c0:c0 + 8, sl].unsqueeze(3).to_broadcast([128, 8, 8, 16])

            def bc_x(xx, c0):
                return xx[:, c0:c0 + 8, :].unsqueeze(2).to_broadcast([128, 8, 8, 16])

            def cmul(eng, outr, outi, pr_, pi_, xr_, xi_, neg_im=False):
                tt(eng, v1, pr_, xr_, ALU.mult)
                tt(eng, v2, pi_, xi_, ALU.mult)
                tt(eng, outr, v1, v2, ALU.subtract)
                tt(eng, v1, pr_, xi_, ALU.mult)
                tt(eng, v2, pi_, xr_, ALU.mult)
                if neg_im:
                    stt(eng, outi, v1, -1.0, v2, ALU.mult, ALU.subtract)
                else:
                    tt(eng, outi, v1, v2, ALU.add)

            SL_E = [slice(14, 6, -1), slice(7, 15)]
            SL_G = [slice(8, 16), slice(15, 7, -1)]
            SL_TE = [slice(7, None, -1), slice(7, 15)]
            SL_TG = [slice(7, 15), slice(7, None, -1)]
            for d in range(2):
                for hf in range(2):
                    c0 = d * 16 + hf * 8
                    gp0 = hf * 8
                    eng = 'dve'
                    cmul(eng, Etr, Eti, bc_pw(PWr, c0, SL_E[d]), bc_pw(PWi, c0, SL_E[d]), bc_x(Bbr, c0), bc_x(Bbi, c0))
                    gr_v = GST[:, :, 0, :].rearrange("p g (j h) -> p g j h", j=8)
                    gi_v = GST[:, :, 1, :].rearrange("p g (j h) -> p g j h", j=8)
                    cmul(eng, gr_v, gi_v, bc_pw(PWr, c0, SL_G[d]), bc_pw(PWi, c0, SL_G[d]), bc_x(CRE, c0), bc_x(CIM, c0), neg_im=True)
                    dma(s5w[gp0:gp0 + 8, :, 512 + d * 256:512 + d * 256 + 256].rearrange("g p x -> p g x"),
                        GST.rearrange("p g r x -> p g (r x)"))
                    if d == 0:
                        cmul(eng, Ttr, Tti, bc_pw(PWr, c0, SL_TE[d]), bc_pw(PWi, c0, SL_TE[d]), bc_x(Bbr, c0), bc_x(Bbi, c0))
                        te_r, te_i = Ttr, Tti
                    else:
                        te_r, te_i = Etr, Eti
                    cmul(eng, Gtr, Gti, bc_pw(PWr, c0, SL_TG[d]), bc_pw(PWi, c0, SL_TG[d]), bc_x(CRE, c0), bc_x(CIM, c0), neg_im=True)
                    for gl in range(8 if S5CUT > 5 else 0):
                        pe_ = psb().bitcast(BF16)
                        trs([(pe_[:, 0:128], Etr[:, gl].rearrange("p i h -> p (i h)"), identb[:]),
                             (pe_[:, 128:256], Eti[:, gl].rearrange("p i h -> p (i h)"), identb[:])])
                        cp('act', EST[:, gl].rearrange("p r x -> p (r x)"), pe_[:, 0:256])
                        if S5CUT <= 6:
                            continue
                        pt_ = psb(2)
                        lst = []
                        for g2 in range(2):
                            lo, hi = g2 * 64, g2 * 64 + 64
                            lst.append((pt_[:, g2 * 512:g2 * 512 + 128], te_r[lo:hi, gl].rearrange("p i h -> p (i h)"),
                                        Gtr[lo:hi, gl].rearrange("p i h -> p (i h)"), True, False, None))
                            lst.append((pt_[:, g2 * 512:g2 * 512 + 128], te_i[lo:hi, gl].rearrange("p i h -> p (i h)"),
                                        Gti[lo:hi, gl].rearrange("p i h -> p (i h)"), False, True, None))
                        mms(lst)
                        gp = gp0 + gl
                        msk = (maskf if d == 0 else maskb).unsqueeze(1).to_broadcast([128, 2, 128])
                        pv = pt_.rearrange("p (a x) -> p a x", a=2)[:, :, 0:128]
                        if d == 0:
                            tt('dve', TACC[:, 2 * gp:2 * gp + 2, :], pv, msk, ALU.mult)
                        else:
                            tmp = v1[:, 0:2].rearrange("p a i h -> p a (i h)")
                            tt('dve', tmp, pv, msk, ALU.mult)
                            tt('pool', TACC[:, 2 * gp:2 * gp + 2, :], TACC[:, 2 * gp:2 * gp + 2, :], tmp, ALU.add)
                            for g2 in range(2):
                                g = 2 * gp + g2
                                stt('pool', TST[:, g, :], identf, DV[:, g:g + 1], TACC[:, g, :], ALU.mult, ALU.add)
                    dma(s5w[gp0:gp0 + 8, :, d * 256:d * 256 + 256].rearrange("g p x -> p g x"),
                        EST.rearrange("p g r x -> p g (r x)"))
            dma(s5w[:, :, 1024:1280].rearrange("g p x -> p g x"), TST.rearrange("p (g a) x -> p g (a x)", a=2))


        if stop_after >= 0:
            s5_setup()

        ar.reset(0)
        BUFA = ar.bf(8192)
        BUFB = ar.bf(8192)
        gS = ar.bf(8192).rearrange("p (c n) -> p c n", c=4)
        gA = ar.bf(8192).rearrange("p (c n) -> p c n", c=4)
        zqT = ar.bf(4096).rearrange("p (c n) -> p c n", c=2)
        zkvT = ar.bf(2048)
        krT = ar.bf(2048)
        TMP0 = ar.off
        U_g = BUFA.rearrange("p (g m) -> p g m", g=32)
        geluT = BUFA.rearrange("p (c n) -> p c n", c=4)
        knT = BUFA.rearrange("p (h n) -> p h n", h=4)
        Ytok = BUFB.rearrange("p (mt i c) -> p mt i c", mt=2, i=8)
        Vt = BUFB.rearrange("p (kt c) -> p kt c", kt=16)
        statk = [0]

        def stat3():
            k = statk[0] % 20
            statk[0] += 1
            return stat[:, 3 * k:3 * k + 1], stat[:, 3 * k + 1:3 * k + 2], stat[:, 3 * k + 2:3 * k + 3]

        def fm_norm(src_list, nfeat, sq, sd, rs, pl=(2, 8)):
            n = len(src_list)
            for j, sap in enumerate(src_list):
                act(sq[:, j, :], sap, AF.Square)
            pq = psb(lo=pl[0], hi=pl[1])
            mms([(pq, onesb[:], sq[:, j, :], j == 0, j == n - 1, None) for j in range(n)])
            act(sd, pq, AF.Sqrt, scale=1.0 / nfeat, bias=epsc)
            recip('dve', rs, sd)

        for s in range(nseq if stop_after >= 1 else 0):
            xv = x[s].rearrange("(mt m i) d -> mt i m d", mt=2, m=128, i=8)
            yv = y[s].rearrange("(mt m i) d -> mt i m d", mt=2, m=128, i=8)
            ar.reset(TMP0)
            posi = ar.i32(512)
            posf, rtu, ritf, rfr = [ar.f32(512) for _ in range(4)]
            riti = ar.i32(512)
            for cb in range(4):
                cols = slice(cb * 512, (cb + 1) * 512)
                dma(posi[0:64, :], pos[s:s + 1, cols].to_broadcast([64, 512]))
                cp('dve', posf[0:64, :], posi[0:64, :])
                ts('dve', rtu[0:64, :], posf[0:64, :], invf, ALU.mult, 1.0 / TWO_PI, ALU.mult)
                sincos(rtu[0:64, :], riti[0:64, :], ritf[0:64, :], rfr[0:64, :], sinT[:, cols], cosT[:, cols])

            ar.reset(TMP0)
            xt = [ar.f32(1024), ar.f32(1024)]
            hb = [ar.bf(1024), ar.bf(1024)]
            hT = [ar.bf(4096).rearrange("p (c n) -> p c n", c=8) for _ in range(2)]
            utile = ar.bf(4096).rearrange("p (g i h) -> p g i h", g=32, i=8)
            sq1 = ar.bf(1024).rearrange("p (c n) -> p c n", c=2)
            sd1 = ar.f32(512)
            rs1 = ar.f32(512)
            rt1 = ar.f32(512)
            rt2 = ar.f32(512)
            tcount = 0
            for b in range(4):
                mt, q = divmod(b, 2)
                hs = hT[b % 2]
                cols = slice(b * 512, (b + 1) * 512)
                for ii in range(4):
                    i = 4 * q + ii
                    sl = tcount % 2
                    tcount += 1
                    dma(xt[sl], xv[mt, i])
                    ss, sdv, rstd = stat3()
                    act(hb[sl], xt[sl], AF.Square, accum=ss)
                    act(sdv, ss, AF.Sqrt, scale=1.0 / D, bias=epsc)
                    recip('dve', rstd, sdv)
                    ts('dve', hb[sl], xt[sl], rstd, ALU.mult)
                    pt = psb(lo=0, hi=2).bitcast(BF16)
                    trs([(pt[:, c * 128:(c + 1) * 128], hb[sl][:, c * 128:(c + 1) * 128], identb[:]) for c in range(8)])
                    cp('act' if ii % 2 == 0 else 'dve', hs[:, :, ii * 128:(ii + 1) * 128], pt.rearrange("p (c m) -> p c m", c=8))

                def proj_fm(wcols, M=128, wsrc=None):
                    pp = psb(lo=2, hi=8)
                    if wsrc is None:
                        lst = [(pp[0:M, :], Win_b[:, c, wcols], hs[:, c, :], c == 0, c == 7, None) for c in range(8)]
                    else:
                        lst = [(pp[0:M, :], wsrc[:, c, :], hs[:, c, :], c == 0, c == 7, None) for c in range(8)]
                    mms(lst)
                    return pp

                pz = [proj_fm(slice(C_ZQ + j * 128, C_ZQ + (j + 1) * 128)) for j in range(2)]
                fm_norm(pz, 256, sq1, sd1, rs1)
                for j in range(2):
                    tt('dve', zqT[:, j, cols], pz[j], rs1, ALU.mult)
                pk = proj_fm(slice(C_ZKV, C_ZKV + 128))
                fm_norm([pk], 128, sq1, sd1, rs1)
                tt('dve', zkvT[:, cols], pk, rs1, ALU.mult)
                pr1 = proj_fm(slice(C_ZKR, C_ZKR + 64), M=64)
                pr2 = proj_fm(None, M=64, wsrc=Wkrrot)
                tt('dve', rt1[0:64, :], pr1[0:64, :], cosT[:, cols], ALU.mult)
                tt('dve', rt2[0:64, :], pr2[0:64, :], sinT[:, cols], ALU.mult)
                tt('pool', krT[0:64, cols], rt1[0:64, :], rt2[0:64, :], ALU.add)
                for j in range(4):
                    pg = proj_fm(slice(C_GA + j * 128, C_GA + (j + 1) * 128))
                    act(gA[:, j, cols], pg, AF.Silu)
                for j in range(4):
                    pg = proj_fm(slice(C_GS + j * 128, C_GS + (j + 1) * 128))
                    act(gS[:, j, cols], pg, AF.Silu)
                for ii in range(4):
                    i = 4 * q + ii
                    pu = psb(lo=2, hi=8)
                    mms([(pu, hs[:, c, ii * 128:(ii + 1) * 128], Win_b[:, c, C_U:C_U + 512], c == 0, c == 7, None) for c in range(8)])
                    cp('dve' if ii % 2 == 0 else 'act', utile[:, :, i, :], pu.rearrange("p (g h) -> p g h", g=32))
                if q == 1:
                    for g4 in range(8):
                        pt = psb(lo=0, hi=2).bitcast(BF16)
                        trs([(pt[:, k_ * 128:(k_ + 1) * 128], utile[:, g4 * 4 + k_].rearrange("p i h -> p (i h)"), identb[:]) for k_ in range(4)])
                        cp('dve' if g4 % 2 == 0 else 'act', U_g[:, g4 * 4:(g4 + 1) * 4, mt * 128:(mt + 1) * 128],
                           pt[:, 0:512].rearrange("p (k m) -> p k m", k=4))
            if s == 0:
                dump("U_g", U_g); dump("zqT", zqT); dump("zkvT", zkvT); dump("krT", krT[0:64, :]); dump("gA", gA); dump("gS", gS)
            if stop_after <= 1:
                continue

            ar.reset(TMP0)
            W5 = [ar.bf(1280) for _ in range(2)]
            TAB = [ar.f32(1536).rearrange("p (t x) -> p t x", t=3) for _ in range(2)]
            tA = [ar.f32(512) for _ in range(2)]
            tB = [ar.f32(512) for _ in range(2)]
            Xp = [ar.f32(512) for _ in range(2)]
            rr = [ar.f32(512) for _ in range(2)]
            a2 = [ar.f32(512) for _ in range(2)]
            b2 = [ar.f32(512) for _ in range(2)]

            def h2(ap):
                return ap.rearrange("p (a x) -> p a x", a=2)

            def sw(ap):
                return h2(ap)[:, ::-1, :]

            for gp in range(16):
                w5 = W5[gp % 2]
                dma(w5, s5w[gp])
                Ev = w5[:, 0:512].rearrange("p (d r x) -> p d r x", d=2, r=2)
                Gv = w5[:, 512:1024].rearrange("p (d r x) -> p d r x", d=2, r=2)
                Tv = w5[:, 1024:1280].rearrange("p (a x) -> p a x", a=2)
                xs = Xs[:, gp % 2]
                for d in range(2):
                    c = d * 16 + gp
                    k2 = (gp * 2 + d) % 2
                    tab = TAB[k2]
                    dma(tab.rearrange("p t x -> p (t x)"), s5t[c])
                    px = psb()
                    lst = []
                    for g2 in range(2):
                        for r_ in range(2):
                            lst.append((px[g2 * 64:(g2 + 1) * 64, r_ * 256:(r_ + 1) * 256], Ev[:, d, r_, g2 * 64:(g2 + 1) * 64],
                                        U_g[:, 2 * gp + g2, :], True, True, (0, 64 * g2) if g2 else None))
                    mms(lst)
                    tt('dve', h2(tA[k2]), h2(px), tab[:, 0, :].rearrange("p (a x) -> p a x", a=2), ALU.mult)
                    tt('dve', h2(tB[k2]), sw(px), tab[:, 1, :].rearrange("p (a x) -> p a x", a=2), ALU.mult)
                    tt('pool', Xp[k2], tA[k2], tB[k2], ALU.add)
                    rho = RHO8[:, c:c + 1].to_broadcast([128, 256])
                    for r_ in range(2):
                        o_ = rr[k2][:, r_ * 256:(r_ + 1) * 256]
                        i_ = Xp[k2][:, r_ * 256:(r_ + 1) * 256]
                        if d == 1:
                            o_ = o_[:, ::-1]
                            i_ = i_[:, ::-1]
                        scan(o_, rho, i_)
                    tt('pool', h2(a2[k2]), h2(rr[k2]), tab[:, 0, :].rearrange("p (a x) -> p a x", a=2), ALU.mult)
                    tt('pool', h2(b2[k2]), sw(rr[k2]), tab[:, 2, :].rearrange("p (a x) -> p a x", a=2), ALU.mult)
                    if d == 0:
                        tt('dve', xs[:, 0, :, 1:256], h2(a2[k2])[:, :, 0:255], h2(b2[k2])[:, :, 0:255], ALU.add)
                    else:
                        tt('dve', xs[:, 1, :, 0:255], h2(a2[k2])[:, :, 1:256], h2(b2[k2])[:, :, 1:256], ALU.add)
                for mt in range(2):
                    py = psb(2)
                    lst = []
                    for g2 in range(2):
                        oc = py[:, g2 * 512:g2 * 512 + 128]
                        lo, hi = g2 * 64, g2 * 64 + 64
                        lst.append((oc, U_g[:, 2 * gp + g2, mt * 128:(mt + 1) * 128], Tv[:, g2, :], True, False, None))
                        for d in range(2):
                            for r_ in range(2):
                                lst.append((oc, xs[lo:hi, d, r_, mt * 128:(mt + 1) * 128], Gv[lo:hi, d, r_, :], False, (d == 1 and r_ == 1), None))
                    mms(lst)
                    act(Ytok[:, mt, :, gp * 32:(gp + 1) * 32].rearrange("p j (a h) -> p a j h", a=2),
                        py.rearrange("p (a x) -> p a x", a=2)[:, :, 0:128].rearrange("p a (j h) -> p a j h", j=8), AF.Gelu)
            if s == 0:
                dump("Ytok", Ytok)
            if stop_after <= 2:
                continue

            ar.reset(TMP0)
            sg = ar.bf(2048).rearrange("p (c n) -> p c n", c=4)
            ssm2 = ar.bf(2048).rearrange("p (c n) -> p c n", c=4)
            sq2 = ar.bf(2048).rearrange("p (c n) -> p c n", c=4)
            tn2 = ar.bf(2048).rearrange("p (c n) -> p c n", c=4)
            sd2 = ar.f32(512)
            rs2 = ar.f32(512)
            for mt in range(2):
                for i in range(8):
                    pt = psb().bitcast(BF16)
                    trs([(pt[:, c * 128:(c + 1) * 128], Ytok[:, mt, i, c * 128:(c + 1) * 128], identb[:]) for c in range(4)])
                    c0 = mt * 1024 + i * 128
                    cp('dve' if i % 2 == 0 else 'act', geluT[:, :, c0:c0 + 128], pt[:, 0:512].rearrange("p (c m) -> p c m", c=4))
            for b in range(4):
                cols = slice(b * 512, (b + 1) * 512)
                for mo in range(4):
                    pg = psb()
                    mms([(pg, Wglu_b[:, kc, mo * 128:(mo + 1) * 128], geluT[:, kc, cols], kc == 0, kc == 3, None) for kc in range(4)])
                    act(sg[:, mo, :], pg, AF.Sigmoid, bias=bglu[:, mo:mo + 1])
                tt('dve', ssm2, geluT[:, :, cols], sg, ALU.mult)
                fm_norm([ssm2[:, j, :] for j in range(4)], 512, sq2, sd2, rs2)
                tt('pool', tn2, ssm2, rs2.unsqueeze(1).to_broadcast([128, 4, 512]), ALU.mult)
                tt('dve', gS[:, :, cols], tn2, gS[:, :, cols], ALU.mult)
            if s == 0:
                dump("ssm", gS)
            if stop_after <= 3:
                continue

            ar.reset(TMP0)
            qn = [ar.bf(2048).rearrange("p (h n) -> p h n", h=4) for _ in range(2)]
            qrT = [ar.bf(2048).rearrange("p (h n) -> p h n", h=4) for _ in range(2)]
            PT = [ar.bf(512) for _ in range(3)]
            rinv = [ar.f32(512) for _ in range(2)]
            attn = ar.bf(2048).rearrange("p (h n) -> p h n", h=4)
            sq3 = ar.bf(2048).rearrange("p (h n) -> p h n", h=4)
            tn3 = ar.bf(2048).rearrange("p (h n) -> p h n", h=4)
            sd3 = ar.f32(512)
            rs3 = ar.f32(512)
            qt1 = ar.f32(512)
            qt2 = ar.f32(512)
            wkv4 = Wukv_b[:].rearrange("p (h f) -> p h f", h=4)
            for h in range(4):
                for b in range(4):
                    cols = slice(b * 512, (b + 1) * 512)
                    pk = psb()
                    mms([(pk, wkv4[:, h, 0:128], zkvT[:, cols], True, True, None)])
                    cp('act' if b % 2 else 'dve', knT[:, h, cols], pk)
            for kt in range(16):
                pv = psb()
                mms([(pv, zkvT[:, kt * 128:(kt + 1) * 128], Wv_b[:], True, True, None)])
                cp('act' if kt % 2 else 'dve', Vt[:, kt, :], pv)
            for qb in range(4):
                cols = slice(qb * 512, (qb + 1) * 512)
                qn_ = qn[qb % 2]
                qr_ = qrT[qb % 2]
                for h in range(4):
                    pq = psb(lo=0, hi=4)
                    mms([(pq, Wuq_b[:, kc, h * 192:h * 192 + 128], zqT[:, kc, cols], kc == 0, kc == 1, None) for kc in range(2)])
                    cp('act' if h % 2 else 'dve', qn_[:, h, :], pq)
                    p1 = psb(lo=0, hi=4)
                    mms([(p1[0:64, :], Wuq_b[:, kc, h * 192 + 128:h * 192 + 192], zqT[:, kc, cols], kc == 0, kc == 1, None) for kc in range(2)])
                    p2 = psb(lo=0, hi=4)
                    mms([(p2[0:64, :], Wuqrot[:, kc, h * 64:(h + 1) * 64], zqT[:, kc, cols], kc == 0, kc == 1, None) for kc in range(2)])
                    tt('dve', qt1[0:64, :], p1[0:64, :], cosT[:, cols], ALU.mult)
                    tt('dve', qt2[0:64, :], p2[0:64, :], sinT[:, cols], ALU.mult)
                    tt('pool', qr_[0:64, h, :], qt1[0:64, :], qt2[0:64, :], ALU.add)
                for h in range(4):
                    po = PS[:, (4 + 2 * (h % 2)) * 512:(5 + 2 * (h % 2)) * 512]
                    pr = PS[:, (5 + 2 * (h % 2)) * 512:(6 + 2 * (h % 2)) * 512]
                    pend = None
                    nkt = 16
                    for kt in range(nkt + 1):
                        cur = None
                        if kt < nkt:
                            pst = psb(lo=0, hi=4)
                            ks = slice(kt * 128, (kt + 1) * 128)
                            mms([(pst, knT[:, h, ks], qn_[:, h, :], True, False, None),
                                 (pst, krT[0:64, ks], qr_[0:64, h, :], False, True, None)])
                            ptile = PT[kt % 3]
                            act(ptile, pst, AF.Exp, scale=SCALE)
                            cur = (kt, ptile)
                        if pend is not None:
                            k0, p0 = pend
                            mms([(po, Vt[:, k0, h * 128:(h + 1) * 128], p0, k0 == 0, k0 == nkt - 1, None),
                                 (pr, onesb[:], p0, k0 == 0, k0 == nkt - 1, None)])
                        pend = cur
                    ri = rinv[h % 2]
                    recip('dve', ri, pr)
                    tt('dve', attn[:, h, :], po, ri, ALU.mult)
                fm_norm([attn[:, j, :] for j in range(4)], 512, sq3, sd3, rs3, pl=(0, 4))
                tt('pool', tn3, attn, rs3.unsqueeze(1).to_broadcast([128, 4, 512]), ALU.mult)
                tt('dve', gA[:, :, cols], tn3, gA[:, :, cols], ALU.mult)
            if s == 0:
                dump("attn", gA)
            if stop_after <= 4:
                continue

            ar.reset(TMP0)
            xt4 = [ar.f32(1024), ar.f32(1024)]
            t4 = [ar.f32(1024), ar.f32(1024)]
            o4 = [ar.f32(1024), ar.f32(1024)]
            junk4 = ar.bf(1024)
            for mt in range(2):
                for i in range(8):
                    k_ = (mt * 8 + i) % 2
                    c0 = mt * 1024 + i * 128
                    dma(xt4[k_], xv[mt, i])
                    pm = psb(2)
                    lst = []
                    for nb in range(2):
                        for kc in range(8):
                            src = gA[:, kc, c0:c0 + 128] if kc < 4 else gS[:, kc - 4, c0:c0 + 128]
                            lst.append((pm[:, nb * 512:(nb + 1) * 512], src, Wout_b[:, kc, nb * 512:(nb + 1) * 512], kc == 0, kc == 7, None))
                    mms(lst)
                    ss, sdv, rstd = stat3()
                    ssb, _u1, _u2 = stat3()
                    act(junk4[:, 0:512], pm[:, 0:512], AF.Square, accum=ss)
                    act(junk4[:, 512:1024], pm[:, 512:1024], AF.Square, accum=ssb)
                    tt('dve', ss, ss, ssb, ALU.add)
                    act(sdv, ss, AF.Sqrt, scale=1.0 / D, bias=epsc)
                    recip('dve', rstd, sdv)
                    for nb in range(2):
                        stt('dve', t4[k_][:, nb * 512:(nb + 1) * 512], pm[:, nb * 512:(nb + 1) * 512], rstd, gpost_b[:, nb * 512:(nb + 1) * 512], ALU.mult, ALU.mult)
                    tt('pool', o4[k_], t4[k_], xt4[k_], ALU.add)
                    FINAL_OPS.append(dma(yv[mt, i], o4[k_]))

        Sd.analyze()
        Sd.emit(final_wait_ops=FINAL_OPS)
    return nc


def _consts():
    ident = np.eye(128, dtype=np.float32)
    ii = np.arange(128) // 16
    maskf = (ii[None, :] >= ii[:, None]).astype(np.float32)
    maskb = (ii[:, None] >= ii[None, :]).astype(np.float32)
    cst = np.stack([ident, maskf, maskb], axis=1).copy()
    kvals = np.tile(np.arange(-7, 9, dtype=np.float32)[None, :], (128, 1)).copy()
    mvals = np.tile(np.arange(256, dtype=np.float32)[None, :], (128, 1)).copy()
    invf = (np.float32(10000.0) ** (-np.arange(0, 64, 2, dtype=np.float32) / np.float32(64))).astype(np.float32)
    invf64 = np.concatenate([invf, invf])
    return cst, kvals, mvals, invf64


def _prep_shared(inp):
    f = np.float32
    cst, kvals, mvals, invf64 = _consts()
    smallv = np.zeros((128, 64), f)
    smallv[:, 0:8] = np.asarray(inp["pre_norm_g"], f).reshape(8, 128).T
    smallv[:, 8:10] = np.asarray(inp["q_norm_g"], f).reshape(2, 128).T
    smallv[:, 10:11] = np.asarray(inp["kv_norm_g"], f).reshape(1, 128).T
    gout = np.concatenate([np.asarray(inp["attn_out_g"], f).reshape(-1), np.asarray(inp["ssm_out_g"], f).reshape(-1)])
    smallv[:, 11:19] = gout.reshape(8, 128).T
    smallv[:, 19:23] = np.asarray(inp["b_glu"], f).reshape(4, 128).T
    smallv[0:64, 23] = invf64
    smallv[:, 24] = EPS

    def pm(a):
        a = np.asarray(a, f).reshape(2, 16, 2, 64)
        return a.transpose(2, 3, 0, 1).reshape(128, 32)

    logdt = np.asarray(inp["s5_log_dt"], f).reshape(2, 16, 2)
    logdt_t = np.broadcast_to(logdt.transpose(2, 0, 1)[:, None, :, :], (2, 64, 2, 16)).reshape(128, 32)
    dsgn = np.concatenate([np.ones((128, 16), f), -np.ones((128, 16), f)], axis=1)
    s5p = np.stack([pm(inp["s5_lam_re"][0]), pm(inp["s5_lam_im"][0]), logdt_t, dsgn], axis=1).astype(f).copy()

    def pb(a):
        a = np.asarray(a, f).reshape(2, 16, 2, 64, 16)
        return a.transpose(2, 3, 0, 1, 4).reshape(128, 512)

    def pc(a):
        a = np.asarray(a, f).reshape(2, 16, 2, 16, 64)
        return a.transpose(2, 4, 0, 1, 3).reshape(128, 512)

    s5bc = np.stack([pb(inp["s5_b_re"][0]), pb(inp["s5_b_im"][0]), pc(inp["s5_c_re"][0]), pc(inp["s5_c_im"][0])], axis=1).astype(f).copy()
    dsk = np.asarray(inp["s5_d"], f).reshape(32, 16)
    dvec = np.broadcast_to(dsk.T[None, :, :], (8, 16, 32)).reshape(128, 32).astype(f).copy()
    shared = {
        "w_in": np.ascontiguousarray(np.asarray(inp["w_in"], f)[0]),
        "w_uq": np.ascontiguousarray(np.asarray(inp["w_uq"], f)[0]),
        "w_ukv": np.ascontiguousarray(np.asarray(inp["w_ukv"], f)[0]),
        "w_glu": np.ascontiguousarray(np.asarray(inp["w_glu"], f)[0]),
        "w_out": np.ascontiguousarray(np.asarray(inp["w_out"], f)[0]),
        "smallv": smallv,
        "gpost": np.asarray(inp["post_norm_g"], f).reshape(1, D).copy(),
        "s5p": s5p, "s5bc": s5bc, "dvec": dvec, "cst": cst, "kvals": kvals, "mvals": mvals,
    }
    return shared


def _perm_pos(p):
    n = p.shape[0]
    return np.ascontiguousarray(p.reshape(n, 2, 128, 8).transpose(0, 1, 3, 2).reshape(n, S)).astype(np.int32)


_NC_CACHE = {}


def kernel(**inputs):
    x = np.asarray(inputs["x"], np.float32)
    positions = np.asarray(inputs["positions"], np.int32)
    B = x.shape[0]
    ncores = 8
    nseq = B // ncores
    shared = _prep_shared(inputs)
    if nseq not in _NC_CACHE:
        _NC_CACHE[nseq] = build(nseq)
    nc = _NC_CACHE[nseq]
    in_maps = []
    for c in range(ncores):
        m = dict(shared)
        m["x"] = np.ascontiguousarray(x[c * nseq:(c + 1) * nseq])
        m["pos"] = _perm_pos(positions[c * nseq:(c + 1) * nseq])
        in_maps.append(m)
    res = run_bass_kernel_spmd(nc, in_maps, core_ids=list(range(ncores)))
    out = np.concatenate([np.asarray(r["y"]) for r in res.results], axis=0)
    return out.astype(np.float32)
```

```python
import numpy as np
import concourse.bass as bass
import concourse.mybir as mybir

F32 = mybir.dt.float32
BF16 = mybir.dt.bfloat16
I32 = mybir.dt.int32
AF = mybir.ActivationFunctionType
ALU = mybir.AluOpType
AX = mybir.AxisListType

_DTSIZE = {}


def dtsize(dt):
    s = str(dt)
    if s in _DTSIZE:
        return _DTSIZE[s]
    if '32' in s:
        v = 4
    elif '16' in s:
        v = 2
    elif '64' in s:
        v = 8
    else:
        v = 1
    _DTSIZE[s] = v
    return v


GRAN = 16


def region_of(ap):
    t = ap.tensor
    name = t.name
    es = dtsize(ap.dtype)
    dims = ap.ap
    off = ap.offset
    space = str(ap.space)
    if 'DRAM' in space.upper() or 'HBM' in space.upper() or 'dram' in space.lower():
        lo = off
        hi = off
        for st, cnt in dims:
            if st >= 0:
                hi += st * (cnt - 1)
            else:
                lo += st * (cnt - 1)
        return (name, 'dram', 0, 1, lo * es, (hi + 1) * es)
    pstep, pcnt = dims[0]
    p0 = off // pstep if pstep > 0 else 0
    rem = off - p0 * pstep if pstep > 0 else off
    lo = rem
    hi = rem
    for st, cnt in dims[1:]:
        if st >= 0:
            hi += st * (cnt - 1)
        else:
            lo += st * (cnt - 1)
    return (name, 'sb', p0, p0 + pcnt, lo * es, (hi + 1) * es)


class Op:
    __slots__ = ('eng', 'fn', 'reads', 'writes', 'idx', 'deps', 'has_dep', 'ord', 'sem', 'semval', 'is_dma', 'waits', 'pre_wait')

    def __init__(self, eng, fn, reads, writes, is_dma):
        self.eng = eng
        self.fn = fn
        self.reads = reads
        self.writes = writes
        self.is_dma = is_dma
        self.deps = set()
        self.has_dep = False
        self.ord = None
        self.sem = None
        self.semval = None
        self.waits = []
        self.pre_wait = None


ENGS = ['pe', 'act', 'dve', 'pool', 'sp']


class Sched:
    def __init__(self, nc, n_dma_sems=12, sem_limit=4000):
        self.nc = nc
        self.ops = []
        self.n_dma_sems = n_dma_sems
        self.sem_limit = sem_limit
        self.track = {}

    def add(self, eng, fn, reads=(), writes=(), dma=False):
        rr = [region_of(a) if not isinstance(a, tuple) else a for a in reads]
        ww = [region_of(a) if not isinstance(a, tuple) else a for a in writes]
        op = Op(eng, fn, rr, ww, dma)
        op.idx = len(self.ops)
        self.ops.append(op)
        return op

    def _arr(self, name, b1):
        ng = (b1 + GRAN - 1) // GRAN + 1
        t = self.track.get(name)
        if t is None:
            t = {'w': np.full((4, ng), -1, np.int64), 'r': np.full((len(ENGS) + 1, 4, ng), -1, np.int64)}
            self.track[name] = t
        elif t['w'].shape[1] < ng:
            old = t['w'].shape[1]
            w = np.full((4, ng), -1, np.int64)
            w[:, :old] = t['w']
            r = np.full((len(ENGS) + 1, 4, ng), -1, np.int64)
            r[:, :, :old] = t['r']
            t['w'] = w
            t['r'] = r
        return t

    def analyze(self):
        ops = self.ops
        for op in ops:
            slot = len(ENGS) if op.is_dma else ENGS.index(op.eng)
            deps = set()
            for (name, sp, p0, p1, b0, b1) in op.reads:
                t = self._arr(name, b1)
                q0, q1 = p0 // 32, (p1 - 1) // 32 + 1
                g0, g1 = b0 // GRAN, (b1 - 1) // GRAN + 1
                w = t['w'][q0:q1, g0:g1]
                for d in np.unique(w):
                    if d >= 0:
                        deps.add(int(d))
            for (name, sp, p0, p1, b0, b1) in op.writes:
                t = self._arr(name, b1)
                q0, q1 = p0 // 32, (p1 - 1) // 32 + 1
                g0, g1 = b0 // GRAN, (b1 - 1) // GRAN + 1
                w = t['w'][q0:q1, g0:g1]
                for d in np.unique(w):
                    if d >= 0:
                        deps.add(int(d))
                r = t['r'][:, q0:q1, g0:g1]
                for d in np.unique(r):
                    if d >= 0:
                        deps.add(int(d))
            for (name, sp, p0, p1, b0, b1) in op.reads:
                t = self.track[name]
                q0, q1 = p0 // 32, (p1 - 1) // 32 + 1
                g0, g1 = b0 // GRAN, (b1 - 1) // GRAN + 1
                if op.is_dma:
                    prev = t['r'][slot, q0:q1, g0:g1]
                    for d in np.unique(prev):
                        if d >= 0:
                            deps.add(int(d))
                t['r'][slot, q0:q1, g0:g1] = op.idx
            for (name, sp, p0, p1, b0, b1) in op.writes:
                t = self.track[name]
                q0, q1 = p0 // 32, (p1 - 1) // 32 + 1
                g0, g1 = b0 // GRAN, (b1 - 1) // GRAN + 1
                t['w'][q0:q1, g0:g1] = op.idx
                t['r'][:, q0:q1, g0:g1] = -1
            deps.discard(op.idx)
            fdeps = set()
            for d in deps:
                o = ops[d]
                if (not o.is_dma) and (not op.is_dma) and o.eng == op.eng:
                    if op.eng == 'pe':
                        continue
                    raw = False
                    for (n1, _, p0, p1, b0, b1) in op.reads:
                        for (n2, _, P0, P1, B0, B1) in o.writes:
                            if n1 == n2 and p0 < P1 and P0 < p1 and b0 < B1 and B0 < b1:
                                raw = True
                    if not raw:
                        continue
                fdeps.add(d)
            op.deps = fdeps
            for d in fdeps:
                ops[d].has_dep = True

    def emit(self, final_wait_ops=()):
        nc = self.nc
        ops = self.ops
        for o in final_wait_ops:
            o.has_dep = True
        for o in ops:
            if o.is_dma:
                o.has_dep = True
        counts = {e: 0 for e in ENGS}
        for op in ops:
            if op.has_dep and not op.is_dma:
                counts[op.eng] += 1
        nsem_eng = {e: max(1, (counts[e] + self.sem_limit - 1) // self.sem_limit) for e in ENGS}
        import contextlib
        with contextlib.ExitStack() as es:
            eng_sems = {e: [es.enter_context(nc.semaphore(f"c_{e}_{i}")) for i in range(nsem_eng[e])] for e in ENGS}
            dma_sems = [es.enter_context(nc.semaphore(f"d_{i}")) for i in range(self.n_dma_sems)]
            cnt = {e: 0 for e in ENGS}
            dma_cnt = [0] * self.n_dma_sems
            dma_rr = 0
            for op in ops:
                if not op.has_dep:
                    continue
                if op.is_dma:
                    s = dma_rr % self.n_dma_sems
                    dma_rr += 1
                    if dma_cnt[s] > 0:
                        op.pre_wait = (dma_sems[s], 16 * dma_cnt[s])
                    dma_cnt[s] += 1
                    op.sem = dma_sems[s]
                    op.semval = 16 * dma_cnt[s]
                else:
                    k = cnt[op.eng]
                    cnt[op.eng] += 1
                    op.sem = eng_sems[op.eng][k // self.sem_limit]
                    op.semval = (k % self.sem_limit) + 1
            waited = {e: {} for e in ENGS}
            for op in ops:
                need = {}
                for d in op.deps:
                    o = ops[d]
                    key = id(o.sem)
                    if key not in need or need[key][1] < o.semval:
                        need[key] = (o.sem, o.semval)
                if op.pre_wait is not None:
                    key = id(op.pre_wait[0])
                    if key not in need or need[key][1] < op.pre_wait[1]:
                        need[key] = op.pre_wait
                wl = []
                for key, (s, v) in need.items():
                    if waited[op.eng].get(key, 0) >= v:
                        continue
                    waited[op.eng][key] = v
                    wl.append((s, v))
                op.waits = wl
            self.n_waits = sum(len(o.waits) for o in ops)
            print('SCHED ops', len(ops), 'incs', dict(cnt), 'dma', sum(dma_cnt), 'waits', self.n_waits, flush=True)
            finals = {}
            for o in final_wait_ops:
                finals[id(o.sem)] = (o.sem, max(o.semval, finals.get(id(o.sem), (None, 0))[1]))
            with nc.Block() as block:
                def run(engname, eng):
                    for op in ops:
                        if op.eng != engname:
                            continue
                        for (s, v) in op.waits:
                            eng.wait_ge(s, v)
                        ins = op.fn(eng)
                        if op.has_dep:
                            ins.then_inc(op.sem, 16 if op.is_dma else 1)
                    if engname == 'sp':
                        for (s, v) in finals.values():
                            eng.wait_ge(s, v)

                @block.tensor
                def _(eng):
                    run('pe', eng)

                @block.scalar
                def _(eng):
                    run('act', eng)

                @block.vector
                def _(eng):
                    run('dve', eng)

                @block.gpsimd
                def _(eng):
                    run('pool', eng)

                @block.sync
                def _(eng):
                    run('sp', eng)

import contextlib
import math
from concourse.bass_utils import run_bass_kernel_spmd

D = 1024
S = 2048
DIN = 1984
C_ZQ, C_ZKV, C_ZKR, C_GA, C_U, C_GS = 0, 256, 384, 448, 960, 1472
EPS = 1e-6
TWO_PI = 2.0 * math.pi
SCALE = 192.0 ** -0.5
NSEQ_CORE = 4


def build(nseq, stop_after=99, dbg=None):
    nc = bass.Bass("TRN2", target_bir_lowering=False)

    def din(name, shape, dt=F32):
        return nc.dram_tensor(name, shape, dt, kind="ExternalInput").ap()

    x = din("x", [nseq, S, D])
    pos = din("pos", [nseq, S], I32)
    w_in = din("w_in", [D, DIN])
    w_uq = din("w_uq", [256, 768])
    w_ukv = din("w_ukv", [128, 1024])
    w_glu = din("w_glu", [512, 512])
    w_out = din("w_out", [D, D])
    smallv = din("smallv", [128, 64])
    gpost = din("gpost", [1, D])
    s5p = din("s5p", [128, 4, 32])
    s5bc = din("s5bc", [128, 4, 512])
    dvec = din("dvec", [128, 32])
    cst = din("cst", [128, 3, 128])
    kvals = din("kvals", [128, 16])
    mvals = din("mvals", [128, 256])
    y = nc.dram_tensor("y", [nseq, S, D], F32, kind="ExternalOutput").ap()
    s5w = nc.dram_tensor("s5w", [16, 128, 1280], BF16, kind="Internal").ap()
    s5t = nc.dram_tensor("s5t", [32, 128, 1536], F32, kind="Internal").ap()

    es = contextlib.ExitStack()
    with es:
        def sb(name, shape, dt):
            return es.enter_context(nc.sbuf_tensor(name, shape, dt))

        Win_b = sb("Win_b", [128, 8, DIN], BF16)
        Wkrrot = sb("Wkrrot", [128, 8, 64], BF16)
        Wuq_b = sb("Wuq_b", [128, 2, 768], BF16)
        Wuqrot = sb("Wuqrot", [128, 2, 256], BF16)
        Wukv_b = sb("Wukv_b", [128, 1024], BF16)
        Wglu_b = sb("Wglu_b", [128, 4, 512], BF16)
        Wv_b = sb("Wv_b", [128, 512], BF16)
        Wout_b = sb("Wout_b", [128, 8, D], BF16)
        identb = sb("identb", [128, 128], BF16)
        onesb = sb("onesb", [128, 128], BF16)
        cstf = sb("cstf", [128, 3, 128], F32)
        gpost_b = sb("gpost_b", [128, D], F32)
        sv = sb("sv", [128, 64], F32)
        RHO8 = sb("RHO8", [128, 32], F32)
        cosT = sb("cosT", [64, S], BF16)
        sinT = sb("sinT", [64, S], BF16)
        stat = sb("stat", [128, 64], F32)
        Xs = sb("Xs", [128, 2, 2, 2, 256], BF16)
        BIG = sb("BIG", [128, 33024], F32)
        PS = es.enter_context(nc.psum_tensor("PS", [128, 4096], F32))

        identf = cstf[:, 0, :]
        maskf = cstf[:, 1, :]
        maskb = cstf[:, 2, :]
        gpre = sv[:, 0:8]
        gq = sv[:, 8:10]
        gkv = sv[:, 10:11]
        gout = sv[:, 11:19]
        bglu = sv[:, 19:23]
        invf = sv[0:64, 23:24]
        epsc = sv[:, 24:25]

        Sd = Sched(nc)
        A = Sd.add
        FINAL_OPS = []

        class Arena:
            def __init__(self):
                self.off = 0

            def reset(self, off=0):
                self.off = off

            def f32(self, n):
                a = BIG[:, self.off:self.off + n]
                self.off += n
                assert self.off <= 33024, self.off
                return a

            def bf(self, n):
                assert n % 2 == 0
                return self.f32(n // 2).bitcast(BF16)

            def i32(self, n):
                return self.f32(n).bitcast(I32)

        ar = Arena()
        ps_rr = {}

        def psb(n=1, lo=0, hi=8):
            k = ps_rr.get((lo, hi), lo)
            if k + n > hi:
                k = lo
            ps_rr[(lo, hi)] = k + n
            return PS[:, k * 512:(k + n) * 512]

        def scan(out, rho, in_):
            return A('dve', lambda e: e.tensor_tensor_scan(out=out, data0=rho, data1=in_, initial=0.0, op0=ALU.mult, op1=ALU.add),
                     [rho, in_], [out])

        DBG = {}

        def dump(name, ap):
            if dbg is None or name not in dbg:
                return
            t = nc.dram_tensor("dbg_" + name, list(ap.shape), ap.dtype, kind="ExternalOutput").ap()
            FINAL_OPS.append(dma(t, ap))

        def dma(out, in_, eng='sp'):
            return A(eng, lambda e: e.dma_start(out=out, in_=in_), [in_], [out], dma=True)

        def tt(eng, out, in0, in1, op):
            return A(eng, lambda e: e.tensor_tensor(out=out, in0=in0, in1=in1, op=op), [in0, in1], [out])

        def ts(eng, out, in0, s1, op0, s2=None, op1=None):
            rd = [in0] + [s for s in (s1, s2) if not isinstance(s, (int, float, type(None)))]
            if op1 is None:
                return A(eng, lambda e: e.tensor_scalar(out=out, in0=in0, scalar1=s1, scalar2=None, op0=op0), rd, [out])
            return A(eng, lambda e: e.tensor_scalar(out=out, in0=in0, scalar1=s1, scalar2=s2, op0=op0, op1=op1), rd, [out])

        def stt(eng, out, in0, sc, in1, op0, op1):
            rd = [in0, in1] + ([] if isinstance(sc, (int, float)) else [sc])
            return A('dve', lambda e: e.scalar_tensor_tensor(out=out, in0=in0, scalar=sc, in1=in1, op0=op0, op1=op1), rd, [out])

        def cp(eng, out, in_):
            if eng == 'act':
                return A('act', lambda e: e.copy(out=out, in_=in_), [in_], [out])
            return A(eng, lambda e: e.tensor_copy(out=out, in_=in_), [in_], [out])

        def act(out, in_, func, scale=1.0, bias=None, accum=None):
            rd = [in_] + ([] if bias is None else [bias]) + ([] if isinstance(scale, (int, float)) else [scale])
            wr = [out] + ([] if accum is None else [accum])
            kw = {}
            if bias is not None:
                kw['bias'] = bias
            if accum is not None:
                kw['accum_out'] = accum
            return A('act', lambda e: e.activation(out=out, in_=in_, func=func, scale=scale, **kw), rd, wr)

        def recip(eng, out, in_):
            return A('dve', lambda e: e.reciprocal(out=out, in_=in_), [in_], [out])

        def mms(lst, extra_reads=()):
            def fn(e):
                ins = None
                for (o, l, r, st, sp_, tp) in lst:
                    if tp is None:
                        ins = e.matmul(o, l, r, start=st, stop=sp_)
                    else:
                        ins = e.matmul(o, l, r, start=st, stop=sp_, tile_position=tp)
                return ins
            rd = []
            wr = []
            for (o, l, r, st, sp_, tp) in lst:
                rd += [l, r]
                wr.append(o)
            return A('pe', fn, rd + list(extra_reads), wr)

        def trs(lst):
            def fn(e):
                ins = None
                for (o, i_, idn) in lst:
                    ins = e.transpose(o, i_, idn)
                return ins
            rd = []
            wr = []
            for (o, i_, idn) in lst:
                rd += [i_, idn]
                wr.append(o)
            return A('pe', fn, rd, wr)

        def memset(eng, out, val):
            return A(eng, lambda e: e.memset(out, val), [], [out])

        dma(sv[:], smallv)
        dma(cstf[:], cst)
        dma(gpost_b[:], gpost.to_broadcast([128, D]))
        cp('dve', identb[:], identf)
        memset('pool', onesb[:], 1.0)
        memset('pool', Xs[:], 0.0)

        ar.reset()
        wst = [ar.f32(2048), ar.f32(2048)]
        engs3 = ['dve', 'pool', 'dve']

        def scale_cast(eng, out, in_, scal):
            if eng == 'act':
                return A('act', lambda e: e.activation(out=out, in_=in_, func=AF.Copy, scale=scal), [in_, scal], [out])
            return ts(eng, out, in_, scal, ALU.mult)

        k = 0
        for c in range(8):
            st_ = wst[k % 2]
            dma(st_[:, 0:DIN], w_in[c * 128:(c + 1) * 128, :])
            scale_cast(engs3[k % 3], Win_b[:, c, :], st_[:, 0:DIN], gpre[:, c:c + 1])
            k += 1
        for c in range(8):
            st_ = wst[k % 2]
            dma(st_[:, 0:D], w_out[c * 128:(c + 1) * 128, :])
            scale_cast(engs3[k % 3], Wout_b[:, c, :], st_[:, 0:D], gout[:, c:c + 1])
            k += 1
        for c in range(2):
            st_ = wst[k % 2]
            dma(st_[:, 0:768], w_uq[c * 128:(c + 1) * 128, :])
            scale_cast(engs3[k % 3], Wuq_b[:, c, :], st_[:, 0:768], gq[:, c:c + 1])
            k += 1
        st_ = wst[k % 2]
        dma(st_[:, 0:1024], w_ukv)
        scale_cast(engs3[k % 3], Wukv_b[:], st_[:, 0:1024], gkv)
        k += 1
        st_ = wst[k % 2]
        dma(st_[:].rearrange("p (c n) -> p c n", c=4), w_glu.rearrange("(c p) n -> p c n", p=128))
        cp(engs3[k % 3], Wglu_b[:].rearrange("p c n -> p (c n)"), st_[:])
        k += 1
        cp('pool', Wv_b[:].rearrange("p (h f) -> p h f", h=4), Wukv_b[:].rearrange("p (h f) -> p h f", h=4)[:, :, 128:256])
        ts('dve', Wkrrot[:, :, 0:32], Win_b[:, :, C_ZKR + 32:C_ZKR + 64], -1.0, ALU.mult)
        cp('dve', Wkrrot[:, :, 32:64], Win_b[:, :, C_ZKR:C_ZKR + 32])
        wq4 = Wuq_b[:].rearrange("p c (h f) -> p c h f", h=4)
        wr4 = Wuqrot[:].rearrange("p c (h f) -> p c h f", h=4)
        for c in range(2):
            ts('dve', wr4[:, c, :, 0:32], wq4[:, c, :, 160:192], -1.0, ALU.mult)
            cp('dve', wr4[:, c, :, 32:64], wq4[:, c, :, 128:160])

        def sincos(tu_ap, it_ap, itf_ap, fr_ap, s_out, c_out, s_scale=TWO_PI, eng='dve', s_out2=None):
            cp(eng, it_ap, tu_ap)
            cp(eng, itf_ap, it_ap)
            tt(eng, fr_ap, tu_ap, itf_ap, ALU.subtract)
            act(s_out, fr_ap, AF.Sin, scale=s_scale)
            if s_out2 is not None:
                act(s_out2, fr_ap, AF.Sin, scale=-s_scale)
            ts(eng, fr_ap, tu_ap, 0.25, ALU.add)
            cp(eng, it_ap, fr_ap)
            cp(eng, itf_ap, it_ap)
            tt(eng, fr_ap, fr_ap, itf_ap, ALU.subtract)
            act(c_out, fr_ap, AF.Sin, scale=TWO_PI)


        import os as _os
        S5CUT = int(_os.environ.get('S5CUT', '99'))

        def s5_setup():
            ar.reset(4096)
            P5 = ar.f32(128).rearrange("p (a c) -> p a c", a=4)
            BC = ar.f32(2048).rearrange("p (a c h) -> p a c h", a=4, c=32)
            KV = ar.f32(16)
            MV = ar.f32(256)
            DV = ar.f32(32)
            dma(P5, s5p)
            dma(BC.rearrange("p a c h -> p a (c h)"), s5bc)
            dma(KV, kvals)
            dma(MV, mvals)
            dma(DV, dvec)
            LAMRE, LAMIM, LOGDT, DSGN = P5[:, 0, :], P5[:, 1, :], P5[:, 2, :], P5[:, 3, :]
            BRE, BIM, CRE, CIM = BC[:, 0], BC[:, 1], BC[:, 2], BC[:, 3]

            def t32():
                return ar.f32(32)

            def t512():
                return ar.f32(512).rearrange("p (c k) -> p c k", c=32)

            lr, dt_, lrdt, th, den, rden, ca, cb, cr, ci, q1, q2, ff = [t32() for _ in range(13)]
            ts('dve', lr, LAMRE, -1e-4, ALU.min)
            act(dt_, LOGDT, AF.Exp)
            tt('dve', lrdt, lr, dt_, ALU.mult)
            tt('dve', th, LAMIM, dt_, ALU.mult)
            ang, lnm, mag, tu, itf, fr, sinv, cosv, PWr, PWi = [t512() for _ in range(10)]
            iti = ar.i32(512).rearrange("p (c k) -> p c k", c=32)
            thb = th.unsqueeze(2).to_broadcast([128, 32, 16])
            lrb = lrdt.unsqueeze(2).to_broadcast([128, 32, 16])
            kvb = KV.unsqueeze(1).to_broadcast([128, 32, 16])
            tt('dve', ang, thb, kvb, ALU.mult)
            tt('dve', lnm, lrb, kvb, ALU.mult)
            act(mag, lnm, AF.Exp)

            if S5CUT <= 1:
                return
            ts('dve', tu, ang, 1.0 / TWO_PI, ALU.mult)
            sincos(tu, iti, itf, fr, sinv, cosv)
            tt('dve', PWr, mag, cosv, ALU.mult)
            tt('dve', PWi, mag, sinv, ALU.mult)
            cp('dve', RHO8[:], mag[:, :, 15])
            if S5CUT <= 2:
                return
            ts('dve', ca, PWr[:, :, 8], -1.0, ALU.add)
            cp('dve', cb, PWi[:, :, 8])
            tt('dve', q1, lr, lr, ALU.mult)
            tt('dve', q2, LAMIM, LAMIM, ALU.mult)
            tt('dve', den, q1, q2, ALU.add)
            recip('dve', rden, den)
            tt('dve', q1, ca, lr, ALU.mult)
            tt('dve', q2, cb, LAMIM, ALU.mult)
            tt('dve', q1, q1, q2, ALU.add)
            tt('dve', cr, q1, rden, ALU.mult)
            tt('dve', q1, cb, lr, ALU.mult)
            tt('dve', q2, ca, LAMIM, ALU.mult)
            tt('dve', q1, q1, q2, ALU.subtract)
            tt('dve', ci, q1, rden, ALU.mult)
            Bbr, Bbi, w1, w2 = [t512() for _ in range(4)]
            crb = cr.unsqueeze(2).to_broadcast([128, 32, 16])
            cib = ci.unsqueeze(2).to_broadcast([128, 32, 16])
            tt('dve', w1, crb, BRE, ALU.mult)
            tt('dve', w2, cib, BIM, ALU.mult)
            tt('dve', Bbr, w1, w2, ALU.subtract)
            tt('dve', w1, crb, BIM, ALU.mult)
            tt('dve', w2, cib, BRE, ALU.mult)
            tt('dve', Bbi, w1, w2, ALU.add)
            ts('dve', ff, th, 8.0 / TWO_PI, ALU.mult)
            fi_ = ar.i32(32)
            fif = t32()
            cp('dve', fi_, ff)
            cp('dve', fif, fi_)
            tt('dve', ff, ff, fif, ALU.subtract)
            tt('dve', ff, ff, DSGN, ALU.mult)
            mark0 = ar.off

            if S5CUT <= 3:
                return
            TABST = ar.f32(4 * 1536).rearrange("p (c t x) -> p c t x", c=4, t=3)
            tq = ar.f32(1024).rearrange("p (c m) -> p c m", c=4)
            tqi = ar.i32(1024).rearrange("p (c m) -> p c m", c=4)
            tqf = ar.f32(1024).rearrange("p (c m) -> p c m", c=4)
            tqr = ar.f32(1024).rearrange("p (c m) -> p c m", c=4)
            mvb = MV.unsqueeze(1).to_broadcast([128, 4, 256])
            for bt in range(8):
                c0 = bt * 4
                eng = 'dve'
                fb = ff[:, c0:c0 + 4].unsqueeze(2).to_broadcast([128, 4, 256])
                tt(eng, tq, fb, mvb, ALU.mult)
                sincos(tq, tqi, tqf, tqr, TABST[:, :, 1, 0:256], TABST[:, :, 0, 0:256], eng=eng, s_out2=TABST[:, :, 1, 256:512])
                cp(eng, TABST[:, :, 0, 256:512], TABST[:, :, 0, 0:256])
                cp(eng, TABST[:, :, 2, 0:256], TABST[:, :, 1, 256:512])
                cp(eng, TABST[:, :, 2, 256:512], TABST[:, :, 1, 0:256])
                dma(s5t[c0:c0 + 4].rearrange("c p x -> p c x"), TABST.rearrange("p c t x -> p c (t x)"))

            if S5CUT <= 4:
                return
            ar.reset(mark0)
            TACC = ar.f32(32 * 128).rearrange("p (g x) -> p g x", g=32)
            TST = ar.bf(32 * 128).rearrange("p (g x) -> p g x", g=32)

            def q4():
                return ar.f32(8 * 128).rearrange("p (g i h) -> p g i h", g=8, i=8)

            def q4b():
                return ar.bf(8 * 128).rearrange("p (g i h) -> p g i h", g=8, i=8)

            Etr, Eti, Ttr, Tti, Gtr, Gti = [q4b() for _ in range(6)]
            v1, v2 = q4(), q4()
            EST = ar.bf(8 * 256).rearrange("p (g r x) -> p g r x", g=8, r=2)
            GST = ar.bf(8 * 256).rearrange("p (g r x) -> p g r x", g=8, r=2)

            def bc_pw(pw, c0, sl):
                return pw[:, c0:c0 + 8, sl].unsqueeze(3).to_broadcast([128, 8, 8, 16])

            def bc_x(xx, c0):
                return xx[:, c0:c0 + 8, :].unsqueeze(2).to_broadcast([128, 8, 8, 16])

            def cmul(eng, outr, outi, pr_, pi_, xr_, xi_, neg_im=False):
                tt(eng, v1, pr_, xr_, ALU.mult)
                tt(eng, v2, pi_, xi_, ALU.mult)
                tt(eng, outr, v1, v2, ALU.subtract)
                tt(eng, v1, pr_, xi_, ALU.mult)
                tt(eng, v2, pi_, xr_, ALU.mult)
                if neg_im:
                    stt(eng, outi, v1, -1.0, v2, ALU.mult, ALU.subtract)
                else:
                    tt(eng, outi, v1, v2, ALU.add)

            SL_E = [slice(14, 6, -1), slice(7, 15)]
            SL_G = [slice(8, 16), slice(15, 7, -1)]
            SL_TE = [slice(7, None, -1), slice(7, 15)]
            SL_TG = [slice(7, 15), slice(7, None, -1)]
            for d in range(2):
                for hf in range(2):
                    c0 = d * 16 + hf * 8
                    gp0 = hf * 8
                    eng = 'dve'
                    cmul(eng, Etr, Eti, bc_pw(PWr, c0, SL_E[d]), bc_pw(PWi, c0, SL_E[d]), bc_x(Bbr, c0), bc_x(Bbi, c0))
                    gr_v = GST[:, :, 0, :].rearrange("p g (j h) -> p g j h", j=8)
                    gi_v = GST[:, :, 1, :].rearrange("p g (j h) -> p g j h", j=8)
                    cmul(eng, gr_v, gi_v, bc_pw(PWr, c0, SL_G[d]), bc_pw(PWi, c0, SL_G[d]), bc_x(CRE, c0), bc_x(CIM, c0), neg_im=True)
                    dma(s5w[gp0:gp0 + 8, :, 512 + d * 256:512 + d * 256 + 256].rearrange("g p x -> p g x"),
                        GST.rearrange("p g r x -> p g (r x)"))
                    if d == 0:
                        cmul(eng, Ttr, Tti, bc_pw(PWr, c0, SL_TE[d]), bc_pw(PWi, c0, SL_TE[d]), bc_x(Bbr, c0), bc_x(Bbi, c0))
                        te_r, te_i = Ttr, Tti
                    else:
                        te_r, te_i = Etr, Eti
                    cmul(eng, Gtr, Gti, bc_pw(PWr, c0, SL_TG[d]), bc_pw(PWi, c0, SL_TG[d]), bc_x(CRE, c0), bc_x(CIM, c0), neg_im=True)
                    for gl in range(8 if S5CUT > 5 else 0):
                        pe_ = psb().bitcast(BF16)
                        trs([(pe_[:, 0:128], Etr[:, gl].rearrange("p i h -> p (i h)"), identb[:]),
                             (pe_[:, 128:256], Eti[:, gl].rearrange("p i h -> p (i h)"), identb[:])])
                        cp('act', EST[:, gl].rearrange("p r x -> p (r x)"), pe_[:, 0:256])
                        if S5CUT <= 6:
                            continue
                        pt_ = psb(2)
                        lst = []
                        for g2 in range(2):
                            lo, hi = g2 * 64, g2 * 64 + 64
                            lst.append((pt_[:, g2 * 512:g2 * 512 + 128], te_r[lo:hi, gl].rearrange("p i h -> p (i h)"),
                                        Gtr[lo:hi, gl].rearrange("p i h -> p (i h)"), True, False, None))
                            lst.append((pt_[:, g2 * 512:g2 * 512 + 128], te_i[lo:hi, gl].rearrange("p i h -> p (i h)"),
                                        Gti[lo:hi, gl].rearrange("p i h -> p (i h)"), False, True, None))
                        mms(lst)
                        gp = gp0 + gl
                        msk = (maskf if d == 0 else maskb).unsqueeze(1).to_broadcast([128, 2, 128])
                        pv = pt_.rearrange("p (a x) -> p a x", a=2)[:, :, 0:128]
                        if d == 0:
                            tt('dve', TACC[:, 2 * gp:2 * gp + 2, :], pv, msk, ALU.mult)
                        else:
                            tmp = v1[:, 0:2].rearrange("p a i h -> p a (i h)")
                            tt('dve', tmp, pv, msk, ALU.mult)
                            tt('pool', TACC[:, 2 * gp:2 * gp + 2, :], TACC[:, 2 * gp:2 * gp + 2, :], tmp, ALU.add)
                            for g2 in range(2):
                                g = 2 * gp + g2
                                stt('pool', TST[:, g, :], identf, DV[:, g:g + 1], TACC[:, g, :], ALU.mult, ALU.add)
                    dma(s5w[gp0:gp0 + 8, :, d * 256:d * 256 + 256].rearrange("g p x -> p g x"),
                        EST.rearrange("p g r x -> p g (r x)"))
            dma(s5w[:, :, 1024:1280].rearrange("g p x -> p g x"), TST.rearrange("p (g a) x -> p g (a x)", a=2))


        if stop_after >= 0:
            s5_setup()

        ar.reset(0)
        BUFA = ar.bf(8192)
        BUFB = ar.bf(8192)
        gS = ar.bf(8192).rearrange("p (c n) -> p c n", c=4)
        gA = ar.bf(8192).rearrange("p (c n) -> p c n", c=4)
        zqT = ar.bf(4096).rearrange("p (c n) -> p c n", c=2)
        zkvT = ar.bf(2048)
        krT = ar.bf(2048)
        TMP0 = ar.off
        U_g = BUFA.rearrange("p (g m) -> p g m", g=32)
        geluT = BUFA.rearrange("p (c n) -> p c n", c=4)
        knT = BUFA.rearrange("p (h n) -> p h n", h=4)
        Ytok = BUFB.rearrange("p (mt i c) -> p mt i c", mt=2, i=8)
        Vt = BUFB.rearrange("p (kt c) -> p kt c", kt=16)
        statk = [0]

        def stat3():
            k = statk[0] % 20
            statk[0] += 1
            return stat[:, 3 * k:3 * k + 1], stat[:, 3 * k + 1:3 * k + 2], stat[:, 3 * k + 2:3 * k + 3]

        def fm_norm(src_list, nfeat, sq, sd, rs, pl=(2, 8)):
            n = len(src_list)
            for j, sap in enumerate(src_list):
                act(sq[:, j, :], sap, AF.Square)
            pq = psb(lo=pl[0], hi=pl[1])
            mms([(pq, onesb[:], sq[:, j, :], j == 0, j == n - 1, None) for j in range(n)])
            act(sd, pq, AF.Sqrt, scale=1.0 / nfeat, bias=epsc)
            recip('dve', rs, sd)

        for s in range(nseq if stop_after >= 1 else 0):
            xv = x[s].rearrange("(mt m i) d -> mt i m d", mt=2, m=128, i=8)
            yv = y[s].rearrange("(mt m i) d -> mt i m d", mt=2, m=128, i=8)
            ar.reset(TMP0)
            posi = ar.i32(512)
            posf, rtu, ritf, rfr = [ar.f32(512) for _ in range(4)]
            riti = ar.i32(512)
            for cb in range(4):
                cols = slice(cb * 512, (cb + 1) * 512)
                dma(posi[0:64, :], pos[s:s + 1, cols].to_broadcast([64, 512]))
                cp('dve', posf[0:64, :], posi[0:64, :])
                ts('dve', rtu[0:64, :], posf[0:64, :], invf, ALU.mult, 1.0 / TWO_PI, ALU.mult)
                sincos(rtu[0:64, :], riti[0:64, :], ritf[0:64, :], rfr[0:64, :], sinT[:, cols], cosT[:, cols])

            ar.reset(TMP0)
            xt = [ar.f32(1024), ar.f32(1024)]
            hb = [ar.bf(1024) for _ in range(4)]
            hT = [ar.bf(4096).rearrange("p (c n) -> p c n", c=8) for _ in range(2)]
            utile = ar.bf(4096).rearrange("p (g i h) -> p g i h", g=32, i=8)
            sq1 = ar.bf(1024).rearrange("p (c n) -> p c n", c=2)
            sd1 = ar.f32(512)
            rs1 = ar.f32(512)
            rt1 = ar.bf(512)
            rt2 = ar.bf(512)
            memset('pool', krT[64:128, :], 0.0)
            tcnt = [0]

            def prepA(b):
                mt, q = divmod(b, 2)
                for ii in range(4):
                    i = 4 * q + ii
                    sl = tcnt[0] % 2
                    tcnt[0] += 1
                    dma(xt[sl], xv[mt, i])
                    ss, sdv, rstd = stat3()
                    act(hb[ii], xt[sl], AF.Square, accum=ss)
                    act(sdv, ss, AF.Sqrt, scale=1.0 / D, bias=epsc)
                    recip('dve', rstd, sdv)
                    ts('dve', hb[ii], xt[sl], rstd, ALU.mult)

            def prepB(b):
                hs = hT[b % 2]
                for ii in range(4):
                    pt = psb(lo=0, hi=2).bitcast(BF16)
                    trs([(pt[:, c * 128:(c + 1) * 128], hb[ii][:, c * 128:(c + 1) * 128], identb[:]) for c in range(8)])
                    cp('act' if ii % 2 == 0 else 'dve', hs[:, :, ii * 128:(ii + 1) * 128], pt.rearrange("p (c m) -> p c m", c=8))

            def proj(b):
                mt, q = divmod(b, 2)
                hs = hT[b % 2]
                cols = slice(b * 512, (b + 1) * 512)

                def proj_fm(wcols, M=128, wsrc=None):
                    pp = psb(lo=2, hi=8)
                    if wsrc is None:
                        lst = [(pp[0:M, :], Win_b[:, c, wcols], hs[:, c, :], c == 0, c == 7, None) for c in range(8)]
                    else:
                        lst = [(pp[0:M, :], wsrc[:, c, :], hs[:, c, :], c == 0, c == 7, None) for c in range(8)]
                    mms(lst)
                    return pp

                pz = [proj_fm(slice(C_ZQ + j * 128, C_ZQ + (j + 1) * 128)) for j in range(2)]
                fm_norm(pz, 256, sq1, sd1, rs1)
                for j in range(2):
                    tt('dve', zqT[:, j, cols], pz[j], rs1, ALU.mult)
                pk = proj_fm(slice(C_ZKV, C_ZKV + 128))
                fm_norm([pk], 128, sq1, sd1, rs1)
                tt('dve', zkvT[:, cols], pk, rs1, ALU.mult)
                pr1 = proj_fm(slice(C_ZKR, C_ZKR + 64), M=64)
                pr2 = proj_fm(None, M=64, wsrc=Wkrrot)
                tt('dve', rt1[0:64, :], pr1[0:64, :], cosT[:, cols], ALU.mult)
                tt('dve', rt2[0:64, :], pr2[0:64, :], sinT[:, cols], ALU.mult)
                tt('pool', krT[0:64, cols], rt1[0:64, :], rt2[0:64, :], ALU.add)
                for j in range(4):
                    pg = proj_fm(slice(C_GA + j * 128, C_GA + (j + 1) * 128))
                    act(gA[:, j, cols], pg, AF.Silu)
                for j in range(4):
                    pg = proj_fm(slice(C_GS + j * 128, C_GS + (j + 1) * 128))
                    act(gS[:, j, cols], pg, AF.Silu)
                for ii in range(4):
                    i = 4 * q + ii
                    pu = psb(lo=2, hi=8)
                    mms([(pu, hs[:, c, ii * 128:(ii + 1) * 128], Win_b[:, c, C_U:C_U + 512], c == 0, c == 7, None) for c in range(8)])
                    cp('dve' if ii % 2 == 0 else 'act', utile[:, :, i, :], pu.rearrange("p (g h) -> p g h", g=32))
                if q == 1:
                    for g4 in range(8):
                        pt = psb(lo=0, hi=2).bitcast(BF16)
                        trs([(pt[:, k_ * 128:(k_ + 1) * 128], utile[:, g4 * 4 + k_].rearrange("p i h -> p (i h)"), identb[:]) for k_ in range(4)])
                        cp('dve' if g4 % 2 == 0 else 'act', U_g[:, g4 * 4:(g4 + 1) * 4, mt * 128:(mt + 1) * 128],
                           pt[:, 0:512].rearrange("p (k m) -> p k m", k=4))

            prepA(0)
            prepB(0)
            for b in range(4):
                if b + 1 < 4:
                    prepA(b + 1)
                proj(b)
                if b + 1 < 4:
                    prepB(b + 1)
            if s == 0:
                dump("U_g", U_g); dump("zqT", zqT); dump("zkvT", zkvT); dump("krT", krT[0:64, :]); dump("gA", gA); dump("gS", gS)
            if stop_after <= 1:
                continue

            ar.reset(TMP0)
            W5 = [ar.bf(1280) for _ in range(2)]
            TAB = [ar.f32(1536).rearrange("p (t x) -> p t x", t=3) for _ in range(3)]
            tA = [ar.f32(512) for _ in range(3)]
            tB = [ar.f32(512) for _ in range(3)]
            Xp = [ar.f32(512) for _ in range(3)]
            rr = [ar.f32(512) for _ in range(3)]

            def h2(ap):
                return ap.rearrange("p (a x) -> p a x", a=2)

            def sw(ap):
                return h2(ap)[:, ::-1, :]

            PXS = {}

            def w5views(gp):
                w5 = W5[gp % 2]
                Ev = w5[:, 0:512].rearrange("p (d r x) -> p d r x", d=2, r=2)
                Gv = w5[:, 512:1024].rearrange("p (d r x) -> p d r x", d=2, r=2)
                Tv = w5[:, 1024:1280].rearrange("p (a x) -> p a x", a=2)
                return w5, Ev, Gv, Tv

            def stA(idx):
                gp, d = divmod(idx, 2)
                w5, Ev, Gv, Tv = w5views(gp)
                if d == 0:
                    dma(w5, s5w[gp])
                c = d * 16 + gp
                k3 = idx % 3
                tab = TAB[k3]
                dma(tab.rearrange("p t x -> p (t x)"), s5t[c])
                px = psb()
                lst = []
                for g2 in range(2):
                    for r_ in range(2):
                        lst.append((px[g2 * 64:(g2 + 1) * 64, r_ * 256:(r_ + 1) * 256], Ev[:, d, r_, g2 * 64:(g2 + 1) * 64],
                                    U_g[:, 2 * gp + g2, :], True, True, (0, 64 * g2) if g2 else None))
                mms(lst)
                tt('dve', h2(tA[k3]), h2(px), h2(tab[:, 0, :]), ALU.mult)
                tt('dve', h2(tB[k3]), sw(px), h2(tab[:, 1, :]), ALU.mult)

            def stB(idx):
                gp, d = divmod(idx, 2)
                c = d * 16 + gp
                k3 = idx % 3
                tt('pool', Xp[k3], tA[k3], tB[k3], ALU.add)
                rho = RHO8[:, c:c + 1].to_broadcast([128, 256])
                for r_ in range(2):
                    o_ = rr[k3][:, r_ * 256:(r_ + 1) * 256]
                    i_ = Xp[k3][:, r_ * 256:(r_ + 1) * 256]
                    if d == 1:
                        o_ = o_[:, ::-1]
                        i_ = i_[:, ::-1]
                    scan(o_, rho, i_)

            def stC(idx):
                gp, d = divmod(idx, 2)
                k3 = idx % 3
                tab = TAB[k3]
                xs = Xs[:, gp % 2]
                w5, Ev, Gv, Tv = w5views(gp)
                tt('pool', h2(tA[k3]), h2(rr[k3]), h2(tab[:, 0, :]), ALU.mult)
                tt('pool', h2(tB[k3]), sw(rr[k3]), h2(tab[:, 2, :]), ALU.mult)
                if d == 0:
                    tt('dve', xs[:, 0, :, 1:256], h2(tA[k3])[:, :, 0:255], h2(tB[k3])[:, :, 0:255], ALU.add)
                else:
                    tt('dve', xs[:, 1, :, 0:255], h2(tA[k3])[:, :, 1:256], h2(tB[k3])[:, :, 1:256], ALU.add)
                    for mt in range(2):
                        py = psb(2)
                        lst = []
                        for g2 in range(2):
                            oc = py[:, g2 * 512:g2 * 512 + 128]
                            lo, hi = g2 * 64, g2 * 64 + 64
                            lst.append((oc, U_g[:, 2 * gp + g2, mt * 128:(mt + 1) * 128], Tv[:, g2, :], True, False, None))
                            for d_ in range(2):
                                for r_ in range(2):
                                    lst.append((oc, xs[lo:hi, d_, r_, mt * 128:(mt + 1) * 128], Gv[lo:hi, d_, r_, :], False, (d_ == 1 and r_ == 1), None))
                        mms(lst)
                        act(Ytok[:, mt, :, gp * 32:(gp + 1) * 32].rearrange("p j (a h) -> p a j h", a=2),
                            py.rearrange("p (a x) -> p a x", a=2)[:, :, 0:128].rearrange("p a (j h) -> p a j h", j=8), AF.Gelu)

            for t_ in range(32 + 2):
                if t_ < 32:
                    stA(t_)
                if 0 <= t_ - 1 < 32:
                    stB(t_ - 1)
                if 0 <= t_ - 2 < 32:
                    stC(t_ - 2)
            if s == 0:
                dump("Ytok", Ytok)
            if stop_after <= 2:
                continue

            ar.reset(TMP0)
            sg = ar.bf(2048).rearrange("p (c n) -> p c n", c=4)
            ssm2 = ar.bf(2048).rearrange("p (c n) -> p c n", c=4)
            sq2 = ar.bf(2048).rearrange("p (c n) -> p c n", c=4)
            tn2 = ar.bf(2048).rearrange("p (c n) -> p c n", c=4)
            sd2 = ar.f32(512)
            rs2 = ar.f32(512)
            for mt in range(2):
                for i in range(8):
                    pt = psb().bitcast(BF16)
                    trs([(pt[:, c * 128:(c + 1) * 128], Ytok[:, mt, i, c * 128:(c + 1) * 128], identb[:]) for c in range(4)])
                    c0 = mt * 1024 + i * 128
                    cp('dve' if i % 2 == 0 else 'act', geluT[:, :, c0:c0 + 128], pt[:, 0:512].rearrange("p (c m) -> p c m", c=4))
            for b in range(4):
                cols = slice(b * 512, (b + 1) * 512)
                for mo in range(4):
                    pg = psb()
                    mms([(pg, Wglu_b[:, kc, mo * 128:(mo + 1) * 128], geluT[:, kc, cols], kc == 0, kc == 3, None) for kc in range(4)])
                    act(sg[:, mo, :], pg, AF.Sigmoid, bias=bglu[:, mo:mo + 1])
                tt('dve', ssm2, geluT[:, :, cols], sg, ALU.mult)
                fm_norm([ssm2[:, j, :] for j in range(4)], 512, sq2, sd2, rs2)
                tt('pool', tn2, ssm2, rs2.unsqueeze(1).to_broadcast([128, 4, 512]), ALU.mult)
                tt('dve', gS[:, :, cols], tn2, gS[:, :, cols], ALU.mult)
            if s == 0:
                dump("ssm", gS)
            if stop_after <= 3:
                continue

            ar.reset(TMP0)
            qn = [ar.bf(2048).rearrange("p (h n) -> p h n", h=4) for _ in range(2)]
            qrT = [ar.bf(2048).rearrange("p (h n) -> p h n", h=4) for _ in range(2)]
            PT = [ar.bf(512) for _ in range(3)]
            rinv = [ar.f32(512) for _ in range(2)]
            attn = [ar.bf(2048).rearrange("p (h n) -> p h n", h=4) for _ in range(2)]
            sq3 = ar.bf(2048).rearrange("p (h n) -> p h n", h=4)
            tn3 = ar.bf(2048).rearrange("p (h n) -> p h n", h=4)
            sd3 = ar.f32(512)
            rs3 = ar.f32(512)
            qt1 = ar.f32(512)
            qt2 = ar.f32(512)
            wkv4 = Wukv_b[:].rearrange("p (h f) -> p h f", h=4)
            for h in range(4):
                for b in range(4):
                    cols = slice(b * 512, (b + 1) * 512)
                    pk = psb()
                    mms([(pk, wkv4[:, h, 0:128], zkvT[:, cols], True, True, None)])
                    cp('act' if b % 2 else 'dve', knT[:, h, cols], pk)
            for kt in range(16):
                pv = psb()
                mms([(pv, zkvT[:, kt * 128:(kt + 1) * 128], Wv_b[:], True, True, None)])
                cp('act' if kt % 2 else 'dve', Vt[:, kt, :], pv)
            for q_ in qrT:
                memset('pool', q_[64:128], 0.0)

            def qproj(qb):
                cols = slice(qb * 512, (qb + 1) * 512)
                qn_ = qn[qb % 2]
                qr_ = qrT[qb % 2]
                for h in range(4):
                    pq = psb(lo=0, hi=4)
                    mms([(pq, Wuq_b[:, kc, h * 192:h * 192 + 128], zqT[:, kc, cols], kc == 0, kc == 1, None) for kc in range(2)])
                    cp('act' if h % 2 else 'dve', qn_[:, h, :], pq)
                    p1 = psb(lo=0, hi=4)
                    mms([(p1[0:64, :], Wuq_b[:, kc, h * 192 + 128:h * 192 + 192], zqT[:, kc, cols], kc == 0, kc == 1, None) for kc in range(2)])
                    p2 = psb(lo=0, hi=4)
                    mms([(p2[0:64, :], Wuqrot[:, kc, h * 64:(h + 1) * 64], zqT[:, kc, cols], kc == 0, kc == 1, None) for kc in range(2)])
                    tt('dve', qt1[0:64, :], p1[0:64, :], cosT[:, cols], ALU.mult)
                    tt('dve', qt2[0:64, :], p2[0:64, :], sinT[:, cols], ALU.mult)
                    tt('pool', qr_[0:64, h, :], qt1[0:64, :], qt2[0:64, :], ALU.add)

            def head(qb, h):
                qn_ = qn[qb % 2]
                qr_ = qrT[qb % 2]
                at_ = attn[qb % 2]
                po = PS[:, (4 + 2 * (h % 2)) * 512:(5 + 2 * (h % 2)) * 512]
                pr = PS[:, (5 + 2 * (h % 2)) * 512:(6 + 2 * (h % 2)) * 512]
                pend = None
                nkt = 16
                for kt in range(nkt + 1):
                    cur = None
                    if kt < nkt:
                        pst = psb(lo=0, hi=4)
                        ks = slice(kt * 128, (kt + 1) * 128)
                        mms([(pst, knT[:, h, ks], qn_[:, h, :], True, False, None),
                             (pst, krT[:, ks], qr_[:, h, :], False, True, None)])
                        ptile = PT[kt % 3]
                        act(ptile, pst, AF.Exp, scale=SCALE)
                        cur = (kt, ptile)
                    if pend is not None:
                        k0, p0 = pend
                        mms([(po, Vt[:, k0, h * 128:(h + 1) * 128], p0, k0 == 0, k0 == nkt - 1, None),
                             (pr, onesb[:], p0, k0 == 0, k0 == nkt - 1, None)])
                    pend = cur
                ri = rinv[h % 2]
                recip('dve', ri, pr)
                tt('dve', at_[:, h, :], po, ri, ALU.mult)

            def epilogue(qb):
                cols = slice(qb * 512, (qb + 1) * 512)
                at_ = attn[qb % 2]
                fm_norm([at_[:, j, :] for j in range(4)], 512, sq3, sd3, rs3, pl=(0, 4))
                tt('pool', tn3, at_, rs3.unsqueeze(1).to_broadcast([128, 4, 512]), ALU.mult)
                tt('dve', gA[:, :, cols], tn3, gA[:, :, cols], ALU.mult)

            qproj(0)
            for qb in range(4):
                if qb + 1 < 4:
                    qproj(qb + 1)
                for h in range(4):
                    head(qb, h)
                    if h == 0 and qb > 0:
                        epilogue(qb - 1)
            epilogue(3)
            if s == 0:
                dump("attn", gA)
            if stop_after <= 4:
                continue

            ar.reset(TMP0)
            xt4 = [ar.f32(1024), ar.f32(1024)]
            t4 = [ar.f32(1024), ar.f32(1024)]
            o4 = [ar.f32(1024), ar.f32(1024)]
            junk4 = ar.bf(1024)
            for mt in range(2):
                for i in range(8):
                    k_ = (mt * 8 + i) % 2
                    c0 = mt * 1024 + i * 128
                    dma(xt4[k_], xv[mt, i])
                    pm = psb(2)
                    lst = []
                    for nb in range(2):
                        for kc in range(8):
                            src = gA[:, kc, c0:c0 + 128] if kc < 4 else gS[:, kc - 4, c0:c0 + 128]
                            lst.append((pm[:, nb * 512:(nb + 1) * 512], src, Wout_b[:, kc, nb * 512:(nb + 1) * 512], kc == 0, kc == 7, None))
                    mms(lst)
                    ss, sdv, rstd = stat3()
                    ssb, _u1, _u2 = stat3()
                    act(junk4[:, 0:512], pm[:, 0:512], AF.Square, accum=ss)
                    act(junk4[:, 512:1024], pm[:, 512:1024], AF.Square, accum=ssb)
                    tt('dve', ss, ss, ssb, ALU.add)
                    act(sdv, ss, AF.Sqrt, scale=1.0 / D, bias=epsc)
                    recip('dve', rstd, sdv)
                    for nb in range(2):
                        stt('dve', t4[k_][:, nb * 512:(nb + 1) * 512], pm[:, nb * 512:(nb + 1) * 512], rstd, gpost_b[:, nb * 512:(nb + 1) * 512], ALU.mult, ALU.mult)
                    tt('pool', o4[k_], t4[k_], xt4[k_], ALU.add)
                    FINAL_OPS.append(dma(yv[mt, i], o4[k_]))

        Sd.analyze()
        Sd.emit(final_wait_ops=FINAL_OPS)
    return nc


def _consts():
    ident = np.eye(128, dtype=np.float32)
    ii = np.arange(128) // 16
    maskf = (ii[None, :] >= ii[:, None]).astype(np.float32)
    maskb = (ii[:, None] >= ii[None, :]).astype(np.float32)
    cst = np.stack([ident, maskf, maskb], axis=1).copy()
    kvals = np.tile(np.arange(-7, 9, dtype=np.float32)[None, :], (128, 1)).copy()
    mvals = np.tile(np.arange(256, dtype=np.float32)[None, :], (128, 1)).copy()
    invf = (np.float32(10000.0) ** (-np.arange(0, 64, 2, dtype=np.float32) / np.float32(64))).astype(np.float32)
    invf64 = np.concatenate([invf, invf])
    return cst, kvals, mvals, invf64


def _prep_shared(inp):
    f = np.float32
    cst, kvals, mvals, invf64 = _consts()
    smallv = np.zeros((128, 64), f)
    smallv[:, 0:8] = np.asarray(inp["pre_norm_g"], f).reshape(8, 128).T
    smallv[:, 8:10] = np.asarray(inp["q_norm_g"], f).reshape(2, 128).T
    smallv[:, 10:11] = np.asarray(inp["kv_norm_g"], f).reshape(1, 128).T
    gout = np.concatenate([np.asarray(inp["attn_out_g"], f).reshape(-1), np.asarray(inp["ssm_out_g"], f).reshape(-1)])
    smallv[:, 11:19] = gout.reshape(8, 128).T
    smallv[:, 19:23] = np.asarray(inp["b_glu"], f).reshape(4, 128).T
    smallv[0:64, 23] = invf64
    smallv[:, 24] = EPS

    def pm(a):
        a = np.asarray(a, f).reshape(2, 16, 2, 64)
        return a.transpose(2, 3, 0, 1).reshape(128, 32)

    logdt = np.asarray(inp["s5_log_dt"], f).reshape(2, 16, 2)
    logdt_t = np.broadcast_to(logdt.transpose(2, 0, 1)[:, None, :, :], (2, 64, 2, 16)).reshape(128, 32)
    dsgn = np.concatenate([np.ones((128, 16), f), -np.ones((128, 16), f)], axis=1)
    s5p = np.stack([pm(inp["s5_lam_re"][0]), pm(inp["s5_lam_im"][0]), logdt_t, dsgn], axis=1).astype(f).copy()

    def pb(a):
        a = np.asarray(a, f).reshape(2, 16, 2, 64, 16)
        return a.transpose(2, 3, 0, 1, 4).reshape(128, 512)

    def pc(a):
        a = np.asarray(a, f).reshape(2, 16, 2, 16, 64)
        return a.transpose(2, 4, 0, 1, 3).reshape(128, 512)

    s5bc = np.stack([pb(inp["s5_b_re"][0]), pb(inp["s5_b_im"][0]), pc(inp["s5_c_re"][0]), pc(inp["s5_c_im"][0])], axis=1).astype(f).copy()
    dsk = np.asarray(inp["s5_d"], f).reshape(32, 16)
    dvec = np.broadcast_to(dsk.T[None, :, :], (8, 16, 32)).reshape(128, 32).astype(f).copy()
    shared = {
        "w_in": np.ascontiguousarray(np.asarray(inp["w_in"], f)[0]),
        "w_uq": np.ascontiguousarray(np.asarray(inp["w_uq"], f)[0]),
        "w_ukv": np.ascontiguousarray(np.asarray(inp["w_ukv"], f)[0]),
        "w_glu": np.ascontiguousarray(np.asarray(inp["w_glu"], f)[0]),
        "w_out": np.ascontiguousarray(np.asarray(inp["w_out"], f)[0]),
        "smallv": smallv,
        "gpost": np.asarray(inp["post_norm_g"], f).reshape(1, D).copy(),
        "s5p": s5p, "s5bc": s5bc, "dvec": dvec, "cst": cst, "kvals": kvals, "mvals": mvals,
    }
    return shared


def _perm_pos(p):
    n = p.shape[0]
    return np.ascontiguousarray(p.reshape(n, 2, 128, 8).transpose(0, 1, 3, 2).reshape(n, S)).astype(np.int32)


_NC_CACHE = {}


def kernel(**inputs):
    x = np.asarray(inputs["x"], np.float32)
    positions = np.asarray(inputs["positions"], np.int32)
    B = x.shape[0]
    ncores = 8
    nseq = B // ncores
    shared = _prep_shared(inputs)
    if nseq not in _NC_CACHE:
        _NC_CACHE[nseq] = build(nseq)
    nc = _NC_CACHE[nseq]
    in_maps = []
    for c in range(ncores):
        m = dict(shared)
        m["x"] = np.ascontiguousarray(x[c * nseq:(c + 1) * nseq])
        m["pos"] = _perm_pos(positions[c * nseq:(c + 1) * nseq])
        in_maps.append(m)
    res = run_bass_kernel_spmd(nc, in_maps, core_ids=list(range(ncores)))
    out = np.concatenate([np.asarray(r["y"]) for r in res.results], axis=0)
    return out.astype(np.float32)
```

```python
import numpy as np
import concourse.bass as bass
import concourse.mybir as mybir

F32 = mybir.dt.float32
BF16 = mybir.dt.bfloat16
I32 = mybir.dt.int32
AF = mybir.ActivationFunctionType
ALU = mybir.AluOpType
AX = mybir.AxisListType

_DTSIZE = {}


def dtsize(dt):
    s = str(dt)
    if s in _DTSIZE:
        return _DTSIZE[s]
    if '32' in s:
        v = 4
    elif '16' in s:
        v = 2
    elif '64' in s:
        v = 8
    else:
        v = 1
    _DTSIZE[s] = v
    return v


GRAN = 16


def region_of(ap):
    t = ap.tensor
    name = t.name
    es = dtsize(ap.dtype)
    dims = ap.ap
    off = ap.offset
    space = str(ap.space)
    if 'DRAM' in space.upper() or 'HBM' in space.upper() or 'dram' in space.lower():
        lo = off
        hi = off
        for st, cnt in dims:
            if st >= 0:
                hi += st * (cnt - 1)
            else:
                lo += st * (cnt - 1)
        return (name, 'dram', 0, 1, lo * es, (hi + 1) * es)
    pstep, pcnt = dims[0]
    p0 = off // pstep if pstep > 0 else 0
    rem = off - p0 * pstep if pstep > 0 else off
    lo = rem
    hi = rem
    for st, cnt in dims[1:]:
        if st >= 0:
            hi += st * (cnt - 1)
        else:
            lo += st * (cnt - 1)
    return (name, 'sb', p0, p0 + pcnt, lo * es, (hi + 1) * es)


class Op:
    __slots__ = ('eng', 'fn', 'reads', 'writes', 'idx', 'deps', 'has_dep', 'ord', 'sem', 'semval', 'is_dma', 'waits', 'pre_wait')

    def __init__(self, eng, fn, reads, writes, is_dma):
        self.eng = eng
        self.fn = fn
        self.reads = reads
        self.writes = writes
        self.is_dma = is_dma
        self.deps = set()
        self.has_dep = False
        self.ord = None
        self.sem = None
        self.semval = None
        self.waits = []
        self.pre_wait = None


ENGS = ['pe', 'act', 'dve', 'pool', 'sp']


class Sched:
    def __init__(self, nc, n_dma_sems=12, sem_limit=4000):
        self.nc = nc
        self.ops = []
        self.n_dma_sems = n_dma_sems
        self.sem_limit = sem_limit
        self.track = {}

    def add(self, eng, fn, reads=(), writes=(), dma=False):
        rr = [region_of(a) if not isinstance(a, tuple) else a for a in reads]
        ww = [region_of(a) if not isinstance(a, tuple) else a for a in writes]
        op = Op(eng, fn, rr, ww, dma)
        op.idx = len(self.ops)
        self.ops.append(op)
        return op

    def _arr(self, name, b1):
        ng = (b1 + GRAN - 1) // GRAN + 1
        t = self.track.get(name)
        if t is None:
            t = {'w': np.full((4, ng), -1, np.int64), 'r': np.full((len(ENGS) + 1, 4, ng), -1, np.int64)}
            self.track[name] = t
        elif t['w'].shape[1] < ng:
            old = t['w'].shape[1]
            w = np.full((4, ng), -1, np.int64)
            w[:, :old] = t['w']
            r = np.full((len(ENGS) + 1, 4, ng), -1, np.int64)
            r[:, :, :old] = t['r']
            t['w'] = w
            t['r'] = r
        return t

    def analyze(self):
        ops = self.ops
        for op in ops:
            slot = len(ENGS) if op.is_dma else ENGS.index(op.eng)
            deps = set()
            for (name, sp, p0, p1, b0, b1) in op.reads:
                t = self._arr(name, b1)
                q0, q1 = p0 // 32, (p1 - 1) // 32 + 1
                g0, g1 = b0 // GRAN, (b1 - 1) // GRAN + 1
                w = t['w'][q0:q1, g0:g1]
                for d in np.unique(w):
                    if d >= 0:
                        deps.add(int(d))
            for (name, sp, p0, p1, b0, b1) in op.writes:
                t = self._arr(name, b1)
                q0, q1 = p0 // 32, (p1 - 1) // 32 + 1
                g0, g1 = b0 // GRAN, (b1 - 1) // GRAN + 1
                w = t['w'][q0:q1, g0:g1]
                for d in np.unique(w):
                    if d >= 0:
                        deps.add(int(d))
                r = t['r'][:, q0:q1, g0:g1]
                for d in np.unique(r):
                    if d >= 0:
                        deps.add(int(d))
            for (name, sp, p0, p1, b0, b1) in op.reads:
                t = self.track[name]
                q0, q1 = p0 // 32, (p1 - 1) // 32 + 1
                g0, g1 = b0 // GRAN, (b1 - 1) // GRAN + 1
                if op.is_dma:
                    prev = t['r'][slot, q0:q1, g0:g1]
                    for d in np.unique(prev):
                        if d >= 0:
                            deps.add(int(d))
                t['r'][slot, q0:q1, g0:g1] = op.idx
            for (name, sp, p0, p1, b0, b1) in op.writes:
                t = self.track[name]
                q0, q1 = p0 // 32, (p1 - 1) // 32 + 1
                g0, g1 = b0 // GRAN, (b1 - 1) // GRAN + 1
                t['w'][q0:q1, g0:g1] = op.idx
                t['r'][:, q0:q1, g0:g1] = -1
            deps.discard(op.idx)
            fdeps = set()
            for d in deps:
                o = ops[d]
                if (not o.is_dma) and (not op.is_dma) and o.eng == op.eng:
                    if op.eng == 'pe':
                        continue
                    raw = False
                    for (n1, _, p0, p1, b0, b1) in op.reads:
                        for (n2, _, P0, P1, B0, B1) in o.writes:
                            if n1 == n2 and p0 < P1 and P0 < p1 and b0 < B1 and B0 < b1:
                                raw = True
                    if not raw:
                        continue
                fdeps.add(d)
            op.deps = fdeps
            for d in fdeps:
                ops[d].has_dep = True

    def emit(self, final_wait_ops=()):
        nc = self.nc
        ops = self.ops
        for o in final_wait_ops:
            o.has_dep = True
        for o in ops:
            if o.is_dma:
                o.has_dep = True
        counts = {e: 0 for e in ENGS}
        for op in ops:
            if op.has_dep and not op.is_dma:
                counts[op.eng] += 1
        nsem_eng = {e: max(1, (counts[e] + self.sem_limit - 1) // self.sem_limit) for e in ENGS}
        import contextlib
        with contextlib.ExitStack() as es:
            eng_sems = {e: [es.enter_context(nc.semaphore(f"c_{e}_{i}")) for i in range(nsem_eng[e])] for e in ENGS}
            dma_sems = [es.enter_context(nc.semaphore(f"d_{i}")) for i in range(self.n_dma_sems)]
            cnt = {e: 0 for e in ENGS}
            dma_cnt = [0] * self.n_dma_sems
            dma_rr = 0
            for op in ops:
                if not op.has_dep:
                    continue
                if op.is_dma:
                    s = dma_rr % self.n_dma_sems
                    dma_rr += 1
                    if dma_cnt[s] > 0:
                        op.pre_wait = (dma_sems[s], 16 * dma_cnt[s])
                    dma_cnt[s] += 1
                    op.sem = dma_sems[s]
                    op.semval = 16 * dma_cnt[s]
                else:
                    k = cnt[op.eng]
                    cnt[op.eng] += 1
                    op.sem = eng_sems[op.eng][k // self.sem_limit]
                    op.semval = (k % self.sem_limit) + 1
            waited = {e: {} for e in ENGS}
            for op in ops:
                need = {}
                for d in op.deps:
                    o = ops[d]
                    key = id(o.sem)
                    if key not in need or need[key][1] < o.semval:
                        need[key] = (o.sem, o.semval)
                if op.pre_wait is not None:
                    key = id(op.pre_wait[0])
                    if key not in need or need[key][1] < op.pre_wait[1]:
                        need[key] = op.pre_wait
                wl = []
                for key, (s, v) in need.items():
                    if waited[op.eng].get(key, 0) >= v:
                        continue
                    waited[op.eng][key] = v
                    wl.append((s, v))
                op.waits = wl
            self.n_waits = sum(len(o.waits) for o in ops)
            print('SCHED ops', len(ops), 'incs', dict(cnt), 'dma', sum(dma_cnt), 'waits', self.n_waits, flush=True)
            finals = {}
            for o in final_wait_ops:
                finals[id(o.sem)] = (o.sem, max(o.semval, finals.get(id(o.sem), (None, 0))[1]))
            with nc.Block() as block:
                def run(engname, eng):
                    for op in ops:
                        if op.eng != engname:
                            continue
                        for (s, v) in op.waits:
                            eng.wait_ge(s, v)
                        ins = op.fn(eng)
                        if op.has_dep:
                            ins.then_inc(op.sem, 16 if op.is_dma else 1)
                    if engname == 'sp':
                        for (s, v) in finals.values():
                            eng.wait_ge(s, v)

                @block.tensor
                def _(eng):
                    run('pe', eng)

                @block.scalar
                def _(eng):
                    run('act', eng)

                @block.vector
                def _(eng):
                    run('dve', eng)

                @block.gpsimd
                def _(eng):
                    run('pool', eng)

                @block.sync
                def _(eng):
                    run('sp', eng)

import contextlib
import math
from concourse.bass_utils import run_bass_kernel_spmd

D = 1024
S = 2048
DIN = 1984
C_ZQ, C_ZKV, C_ZKR, C_GA, C_U, C_GS = 0, 256, 384, 448, 960, 1472
EPS = 1e-6
TWO_PI = 2.0 * math.pi
SCALE = 192.0 ** -0.5
NSEQ_CORE = 4


def build(nseq, stop_after=99, dbg=None):
    nc = bass.Bass("TRN2", target_bir_lowering=False)

    def din(name, shape, dt=F32):
        return nc.dram_tensor(name, shape, dt, kind="ExternalInput").ap()

    x = din("x", [nseq, S, D])
    pos = din("pos", [nseq, S], I32)
    w_in = din("w_in", [D, DIN])
    w_uq = din("w_uq", [256, 768])
    w_ukv = din("w_ukv", [128, 1024])
    w_glu = din("w_glu", [512, 512])
    w_out = din("w_out", [D, D])
    smallv = din("smallv", [128, 64])
    gpost = din("gpost", [1, D])
    s5p = din("s5p", [128, 4, 32])
    s5bc = din("s5bc", [128, 4, 512])
    dvec = din("dvec", [128, 32])
    cst = din("cst", [128, 3, 128])
    kvals = din("kvals", [128, 16])
    mvals = din("mvals", [128, 256])
    y = nc.dram_tensor("y", [nseq, S, D], F32, kind="ExternalOutput").ap()
    s5w = nc.dram_tensor("s5w", [16, 128, 1280], BF16, kind="Internal").ap()
    s5t = nc.dram_tensor("s5t", [32, 128, 1536], F32, kind="Internal").ap()

    es = contextlib.ExitStack()
    with es:
        def sb(name, shape, dt):
            return es.enter_context(nc.sbuf_tensor(name, shape, dt))

        Win_b = sb("Win_b", [128, 8, DIN], BF16)
        Wkrrot = sb("Wkrrot", [128, 8, 64], BF16)
        Wuq_b = sb("Wuq_b", [128, 2, 768], BF16)
        Wuqrot = sb("Wuqrot", [128, 2, 256], BF16)
        Wukv_b = sb("Wukv_b", [128, 1024], BF16)
        Wglu_b = sb("Wglu_b", [128, 4, 512], BF16)
        Wv_b = sb("Wv_b", [128, 512], BF16)
        Wout_b = sb("Wout_b", [128, 8, D], BF16)
        identb = sb("identb", [128, 128], BF16)
        onesb = sb("onesb", [128, 128], BF16)
        cstf = sb("cstf", [128, 3, 128], F32)
        gpost_b = sb("gpost_b", [128, D], F32)
        sv = sb("sv", [128, 64], F32)
        RHO8 = sb("RHO8", [128, 32], F32)
        cosT = sb("cosT", [64, S], BF16)
        sinT = sb("sinT", [64, S], BF16)
        stat = sb("stat", [128, 64], F32)
        Xs = sb("Xs", [128, 2, 2, 2, 256], BF16)
        BIG = sb("BIG", [128, 33024], F32)
        PS = es.enter_context(nc.psum_tensor("PS", [128, 4096], F32))

        identf = cstf[:, 0, :]
        maskf = cstf[:, 1, :]
        maskb = cstf[:, 2, :]
        gpre = sv[:, 0:8]
        gq = sv[:, 8:10]
        gkv = sv[:, 10:11]
        gout = sv[:, 11:19]
        bglu = sv[:, 19:23]
        invf = sv[0:64, 23:24]
        epsc = sv[:, 24:25]

        Sd = Sched(nc)
        A = Sd.add
        FINAL_OPS = []

        class Arena:
            def __init__(self):
                self.off = 0

            def reset(self, off=0):
                self.off = off

            def f32(self, n):
                a = BIG[:, self.off:self.off + n]
                self.off += n
                assert self.off <= 33024, self.off
                return a

            def bf(self, n):
                assert n % 2 == 0
                return self.f32(n // 2).bitcast(BF16)

            def i32(self, n):
                return self.f32(n).bitcast(I32)

        ar = Arena()
        ps_rr = {}

        def psb(n=1, lo=0, hi=8):
            k = ps_rr.get((lo, hi), lo)
            if k + n > hi:
                k = lo
            ps_rr[(lo, hi)] = k + n
            return PS[:, k * 512:(k + n) * 512]

        def scan(out, rho, in_):
            return A('dve', lambda e: e.tensor_tensor_scan(out=out, data0=rho, data1=in_, initial=0.0, op0=ALU.mult, op1=ALU.add),
                     [rho, in_], [out])

        DBG = {}

        def dump(name, ap):
            if dbg is None or name not in dbg:
                return
            t = nc.dram_tensor("dbg_" + name, list(ap.shape), ap.dtype, kind="ExternalOutput").ap()
            FINAL_OPS.append(dma(t, ap))

        def dma(out, in_, eng='sp'):
            return A(eng, lambda e: e.dma_start(out=out, in_=in_), [in_], [out], dma=True)

        def tt(eng, out, in0, in1, op):
            return A(eng, lambda e: e.tensor_tensor(out=out, in0=in0, in1=in1, op=op), [in0, in1], [out])

        def ts(eng, out, in0, s1, op0, s2=None, op1=None):
            rd = [in0] + [s for s in (s1, s2) if not isinstance(s, (int, float, type(None)))]
            if op1 is None:
                return A(eng, lambda e: e.tensor_scalar(out=out, in0=in0, scalar1=s1, scalar2=None, op0=op0), rd, [out])
            return A(eng, lambda e: e.tensor_scalar(out=out, in0=in0, scalar1=s1, scalar2=s2, op0=op0, op1=op1), rd, [out])

        def stt(eng, out, in0, sc, in1, op0, op1):
            rd = [in0, in1] + ([] if isinstance(sc, (int, float)) else [sc])
            return A('dve', lambda e: e.scalar_tensor_tensor(out=out, in0=in0, scalar=sc, in1=in1, op0=op0, op1=op1), rd, [out])

        def cp(eng, out, in_):
            if eng == 'act':
                return A('act', lambda e: e.copy(out=out, in_=in_), [in_], [out])
            return A(eng, lambda e: e.tensor_copy(out=out, in_=in_), [in_], [out])

        def act(out, in_, func, scale=1.0, bias=None, accum=None):
            rd = [in_] + ([] if bias is None else [bias]) + ([] if isinstance(scale, (int, float)) else [scale])
            wr = [out] + ([] if accum is None else [accum])
            kw = {}
            if bias is not None:
                kw['bias'] = bias
            if accum is not None:
                kw['accum_out'] = accum
            return A('act', lambda e: e.activation(out=out, in_=in_, func=func, scale=scale, **kw), rd, wr)

        def recip(eng, out, in_):
            return A('dve', lambda e: e.reciprocal(out=out, in_=in_), [in_], [out])

        def mms(lst, extra_reads=()):
            def fn(e):
                ins = None
                for (o, l, r, st, sp_, tp) in lst:
                    if tp is None:
                        ins = e.matmul(o, l, r, start=st, stop=sp_)
                    else:
                        ins = e.matmul(o, l, r, start=st, stop=sp_, tile_position=tp)
                return ins
            rd = []
            wr = []
            for (o, l, r, st, sp_, tp) in lst:
                rd += [l, r]
                wr.append(o)
            return A('pe', fn, rd + list(extra_reads), wr)

        def trs(lst):
            def fn(e):
                ins = None
                for (o, i_, idn) in lst:
                    ins = e.transpose(o, i_, idn)
                return ins
            rd = []
            wr = []
            for (o, i_, idn) in lst:
                rd += [i_, idn]
                wr.append(o)
            return A('pe', fn, rd, wr)

        def memset(eng, out, val):
            return A(eng, lambda e: e.memset(out, val), [], [out])

        dma(sv[:], smallv)
        dma(cstf[:], cst)
        dma(gpost_b[:], gpost.to_broadcast([128, D]))
        cp('dve', identb[:], identf)
        memset('pool', onesb[:], 1.0)
        memset('pool', Xs[:], 0.0)

        ar.reset()
        wst = [ar.f32(2048), ar.f32(2048)]
        engs3 = ['dve', 'pool', 'dve']

        def scale_cast(eng, out, in_, scal):
            if eng == 'act':
                return A('act', lambda e: e.activation(out=out, in_=in_, func=AF.Copy, scale=scal), [in_, scal], [out])
            return ts(eng, out, in_, scal, ALU.mult)

        k = 0
        for c in range(8):
            st_ = wst[k % 2]
            dma(st_[:, 0:DIN], w_in[c * 128:(c + 1) * 128, :])
            scale_cast(engs3[k % 3], Win_b[:, c, :], st_[:, 0:DIN], gpre[:, c:c + 1])
            k += 1
        for c in range(8):
            st_ = wst[k % 2]
            dma(st_[:, 0:D], w_out[c * 128:(c + 1) * 128, :])
            scale_cast(engs3[k % 3], Wout_b[:, c, :], st_[:, 0:D], gout[:, c:c + 1])
            k += 1
        for c in range(2):
            st_ = wst[k % 2]
            dma(st_[:, 0:768], w_uq[c * 128:(c + 1) * 128, :])
            scale_cast(engs3[k % 3], Wuq_b[:, c, :], st_[:, 0:768], gq[:, c:c + 1])
            k += 1
        st_ = wst[k % 2]
        dma(st_[:, 0:1024], w_ukv)
        scale_cast(engs3[k % 3], Wukv_b[:], st_[:, 0:1024], gkv)
        k += 1
        st_ = wst[k % 2]
        dma(st_[:].rearrange("p (c n) -> p c n", c=4), w_glu.rearrange("(c p) n -> p c n", p=128))
        cp(engs3[k % 3], Wglu_b[:].rearrange("p c n -> p (c n)"), st_[:])
        k += 1
        cp('pool', Wv_b[:].rearrange("p (h f) -> p h f", h=4), Wukv_b[:].rearrange("p (h f) -> p h f", h=4)[:, :, 128:256])
        ts('dve', Wkrrot[:, :, 0:32], Win_b[:, :, C_ZKR + 32:C_ZKR + 64], -1.0, ALU.mult)
        cp('dve', Wkrrot[:, :, 32:64], Win_b[:, :, C_ZKR:C_ZKR + 32])
        wq4 = Wuq_b[:].rearrange("p c (h f) -> p c h f", h=4)
        wr4 = Wuqrot[:].rearrange("p c (h f) -> p c h f", h=4)
        for c in range(2):
            ts('dve', wr4[:, c, :, 0:32], wq4[:, c, :, 160:192], -1.0, ALU.mult)
            cp('dve', wr4[:, c, :, 32:64], wq4[:, c, :, 128:160])

        def sincos(tu_ap, it_ap, itf_ap, fr_ap, s_out, c_out, s_scale=TWO_PI, eng='dve', s_out2=None):
            cp(eng, it_ap, tu_ap)
            cp(eng, itf_ap, it_ap)
            tt(eng, fr_ap, tu_ap, itf_ap, ALU.subtract)
            act(s_out, fr_ap, AF.Sin, scale=s_scale)
            if s_out2 is not None:
                act(s_out2, fr_ap, AF.Sin, scale=-s_scale)
            ts(eng, fr_ap, tu_ap, 0.25, ALU.add)
            cp(eng, it_ap, fr_ap)
            cp(eng, itf_ap, it_ap)
            tt(eng, fr_ap, fr_ap, itf_ap, ALU.subtract)
            act(c_out, fr_ap, AF.Sin, scale=TWO_PI)


        import os as _os
        S5CUT = int(_os.environ.get('S5CUT', '99'))

        def s5_setup():
            ar.reset(4096)
            P5 = ar.f32(128).rearrange("p (a c) -> p a c", a=4)
            BC = ar.f32(2048).rearrange("p (a c h) -> p a c h", a=4, c=32)
            KV = ar.f32(16)
            MV = ar.f32(256)
            DV = ar.f32(32)
            dma(P5, s5p)
            dma(BC.rearrange("p a c h -> p a (c h)"), s5bc)
            dma(KV, kvals)
            dma(MV, mvals)
            dma(DV, dvec)
            LAMRE, LAMIM, LOGDT, DSGN = P5[:, 0, :], P5[:, 1, :], P5[:, 2, :], P5[:, 3, :]
            BRE, BIM, CRE, CIM = BC[:, 0], BC[:, 1], BC[:, 2], BC[:, 3]

            def t32():
                return ar.f32(32)

            def t512():
                return ar.f32(512).rearrange("p (c k) -> p c k", c=32)

            lr, dt_, lrdt, th, den, rden, ca, cb, cr, ci, q1, q2, ff = [t32() for _ in range(13)]
            ts('dve', lr, LAMRE, -1e-4, ALU.min)
            act(dt_, LOGDT, AF.Exp)
            tt('dve', lrdt, lr, dt_, ALU.mult)
            tt('dve', th, LAMIM, dt_, ALU.mult)
            ang, lnm, mag, tu, itf, fr, sinv, cosv, PWr, PWi = [t512() for _ in range(10)]
            iti = ar.i32(512).rearrange("p (c k) -> p c k", c=32)
            thb = th.unsqueeze(2).to_broadcast([128, 32, 16])
            lrb = lrdt.unsqueeze(2).to_broadcast([128, 32, 16])
            kvb = KV.unsqueeze(1).to_broadcast([128, 32, 16])
            tt('dve', ang, thb, kvb, ALU.mult)
            tt('dve', lnm, lrb, kvb, ALU.mult)
            act(mag, lnm, AF.Exp)

            if S5CUT <= 1:
                return
            ts('dve', tu, ang, 1.0 / TWO_PI, ALU.mult)
            sincos(tu, iti, itf, fr, sinv, cosv)
            tt('dve', PWr, mag, cosv, ALU.mult)
            tt('dve', PWi, mag, sinv, ALU.mult)
            cp('dve', RHO8[:], mag[:, :, 15])
            if S5CUT <= 2:
                return
            ts('dve', ca, PWr[:, :, 8], -1.0, ALU.add)
            cp('dve', cb, PWi[:, :, 8])
            tt('dve', q1, lr, lr, ALU.mult)
            tt('dve', q2, LAMIM, LAMIM, ALU.mult)
            tt('dve', den, q1, q2, ALU.add)
            recip('dve', rden, den)
            tt('dve', q1, ca, lr, ALU.mult)
            tt('dve', q2, cb, LAMIM, ALU.mult)
            tt('dve', q1, q1, q2, ALU.add)
            tt('dve', cr, q1, rden, ALU.mult)
            tt('dve', q1, cb, lr, ALU.mult)
            tt('dve', q2, ca, LAMIM, ALU.mult)
            tt('dve', q1, q1, q2, ALU.subtract)
            tt('dve', ci, q1, rden, ALU.mult)
            Bbr, Bbi, w1, w2 = [t512() for _ in range(4)]
            crb = cr.unsqueeze(2).to_broadcast([128, 32, 16])
            cib = ci.unsqueeze(2).to_broadcast([128, 32, 16])
            tt('dve', w1, crb, BRE, ALU.mult)
            tt('dve', w2, cib, BIM, ALU.mult)
            tt('dve', Bbr, w1, w2, ALU.subtract)
            tt('dve', w1, crb, BIM, ALU.mult)
            tt('dve', w2, cib, BRE, ALU.mult)
            tt('dve', Bbi, w1, w2, ALU.add)
            ts('dve', ff, th, 8.0 / TWO_PI, ALU.mult)
            fi_ = ar.i32(32)
            fif = t32()
            cp('dve', fi_, ff)
            cp('dve', fif, fi_)
            tt('dve', ff, ff, fif, ALU.subtract)
            tt('dve', ff, ff, DSGN, ALU.mult)
            mark0 = ar.off

            if S5CUT <= 3:
                return
            TABST = ar.f32(4 * 1536).rearrange("p (c t x) -> p c t x", c=4, t=3)
            tq = ar.f32(1024).rearrange("p (c m) -> p c m", c=4)
            tqi = ar.i32(1024).rearrange("p (c m) -> p c m", c=4)
            tqf = ar.f32(1024).rearrange("p (c m) -> p c m", c=4)
            tqr = ar.f32(1024).rearrange("p (c m) -> p c m", c=4)
            mvb = MV.unsqueeze(1).to_broadcast([128, 4, 256])
            for bt in range(8):
                c0 = bt * 4
                eng = 'dve'
                fb = ff[:, c0:c0 + 4].unsqueeze(2).to_broadcast([128, 4, 256])
                tt(eng, tq, fb, mvb, ALU.mult)
                sincos(tq, tqi, tqf, tqr, TABST[:, :, 1, 0:256], TABST[:, :, 0, 0:256], eng=eng, s_out2=TABST[:, :, 1, 256:512])
                cp(eng, TABST[:, :, 0, 256:512], TABST[:, :, 0, 0:256])
                cp(eng, TABST[:, :, 2, 0:256], TABST[:, :, 1, 256:512])
                cp(eng, TABST[:, :, 2, 256:512], TABST[:, :, 1, 0:256])
                dma(s5t[c0:c0 + 4].rearrange("c p x -> p c x"), TABST.rearrange("p c t x -> p c (t x)"))

            if S5CUT <= 4:
                return
            ar.reset(mark0)
            TACC = ar.f32(32 * 128).rearrange("p (g x) -> p g x", g=32)
            TST = ar.bf(32 * 128).rearrange("p (g x) -> p g x", g=32)

            def q4():
                return ar.f32(8 * 128).rearrange("p (g i h) -> p g i h", g=8, i=8)

            def q4b():
                return ar.bf(8 * 128).rearrange("p (g i h) -> p g i h", g=8, i=8)

            Etr, Eti, Ttr, Tti, Gtr, Gti = [q4b() for _ in range(6)]
            v1, v2 = q4(), q4()
            EST = ar.bf(8 * 256).rearrange("p (g r x) -> p g r x", g=8, r=2)
            GST = ar.bf(8 * 256).rearrange("p (g r x) -> p g r x", g=8, r=2)

            def bc_pw(pw, c0, sl):
                return pw[:, c0:c0 + 8, sl].unsqueeze(3).to_broadcast([128, 8, 8, 16])

            def bc_x(xx, c0):
                return xx[:, c0:c0 + 8, :].unsqueeze(2).to_broadcast([128, 8, 8, 16])

            def cmul(eng, outr, outi, pr_, pi_, xr_, xi_, neg_im=False):
                tt(eng, v1, pr_, xr_, ALU.mult)
                tt(eng, v2, pi_, xi_, ALU.mult)
                tt(eng, outr, v1, v2, ALU.subtract)
                tt(eng, v1, pr_, xi_, ALU.mult)
                tt(eng, v2, pi_, xr_, ALU.mult)
                if neg_im:
                    stt(eng, outi, v1, -1.0, v2, ALU.mult, ALU.subtract)
                else:
                    tt(eng, outi, v1, v2, ALU.add)

            SL_E = [slice(14, 6, -1), slice(7, 15)]
            SL_G = [slice(8, 16), slice(15, 7, -1)]
            SL_TE = [slice(7, None, -1), slice(7, 15)]
            SL_TG = [slice(7, 15), slice(7, None, -1)]
            for d in range(2):
                for hf in range(2):
                    c0 = d * 16 + hf * 8
                    gp0 = hf * 8
                    eng = 'dve'
                    cmul(eng, Etr, Eti, bc_pw(PWr, c0, SL_E[d]), bc_pw(PWi, c0, SL_E[d]), bc_x(Bbr, c0), bc_x(Bbi, c0))
                    gr_v = GST[:, :, 0, :].rearrange("p g (j h) -> p g j h", j=8)
                    gi_v = GST[:, :, 1, :].rearrange("p g (j h) -> p g j h", j=8)
                    cmul(eng, gr_v, gi_v, bc_pw(PWr, c0, SL_G[d]), bc_pw(PWi, c0, SL_G[d]), bc_x(CRE, c0), bc_x(CIM, c0), neg_im=True)
                    dma(s5w[gp0:gp0 + 8, :, 512 + d * 256:512 + d * 256 + 256].rearrange("g p x -> p g x"),
                        GST.rearrange("p g r x -> p g (r x)"))
                    if d == 0:
                        cmul(eng, Ttr, Tti, bc_pw(PWr, c0, SL_TE[d]), bc_pw(PWi, c0, SL_TE[d]), bc_x(Bbr, c0), bc_x(Bbi, c0))
                        te_r, te_i = Ttr, Tti
                    else:
                        te_r, te_i = Etr, Eti
                    cmul(eng, Gtr, Gti, bc_pw(PWr, c0, SL_TG[d]), bc_pw(PWi, c0, SL_TG[d]), bc_x(CRE, c0), bc_x(CIM, c0), neg_im=True)
                    for gl in range(8 if S5CUT > 5 else 0):
                        pe_ = psb().bitcast(BF16)
                        trs([(pe_[:, 0:128], Etr[:, gl].rearrange("p i h -> p (i h)"), identb[:]),
                             (pe_[:, 128:256], Eti[:, gl].rearrange("p i h -> p (i h)"), identb[:])])
                        cp('act', EST[:, gl].rearrange("p r x -> p (r x)"), pe_[:, 0:256])
                        if S5CUT <= 6:
                            continue
                        pt_ = psb(2)
                        lst = []
                        for g2 in range(2):
                            lo, hi = g2 * 64, g2 * 64 + 64
                            lst.append((pt_[:, g2 * 512:g2 * 512 + 128], te_r[lo:hi, gl].rearrange("p i h -> p (i h)"),
                                        Gtr[lo:hi, gl].rearrange("p i h -> p (i h)"), True, False, None))
                            lst.append((pt_[:, g2 * 512:g2 * 512 + 128], te_i[lo:hi, gl].rearrange("p i h -> p (i h)"),
                                        Gti[lo:hi, gl].rearrange("p i h -> p (i h)"), False, True, None))
                        mms(lst)
                        gp = gp0 + gl
                        msk = (maskf if d == 0 else maskb).unsqueeze(1).to_broadcast([128, 2, 128])
                        pv = pt_.rearrange("p (a x) -> p a x", a=2)[:, :, 0:128]
                        if d == 0:
                            tt('dve', TACC[:, 2 * gp:2 * gp + 2, :], pv, msk, ALU.mult)
                        else:
                            tmp = v1[:, 0:2].rearrange("p a i h -> p a (i h)")
                            tt('dve', tmp, pv, msk, ALU.mult)
                            tt('pool', TACC[:, 2 * gp:2 * gp + 2, :], TACC[:, 2 * gp:2 * gp + 2, :], tmp, ALU.add)
                            for g2 in range(2):
                                g = 2 * gp + g2
                                stt('pool', TST[:, g, :], identf, DV[:, g:g + 1], TACC[:, g, :], ALU.mult, ALU.add)
                    dma(s5w[gp0:gp0 + 8, :, d * 256:d * 256 + 256].rearrange("g p x -> p g x"),
                        EST.rearrange("p g r x -> p g (r x)"))
            dma(s5w[:, :, 1024:1280].rearrange("g p x -> p g x"), TST.rearrange("p (g a) x -> p g (a x)", a=2))


        if stop_after >= 0:
            s5_setup()

        ar.reset(0)
        BUFA = ar.bf(8192)
        BUFB = ar.bf(8192)
        gS = ar.bf(8192).rearrange("p (c n) -> p c n", c=4)
        gA = ar.bf(8192).rearrange("p (c n) -> p c n", c=4)
        zqT = ar.bf(4096).rearrange("p (c n) -> p c n", c=2)
        zkvT = ar.bf(2048)
        krT = ar.bf(2048)
        TMP0 = ar.off
        U_g = BUFA.rearrange("p (g m) -> p g m", g=32)
        geluT = BUFA.rearrange("p (c n) -> p c n", c=4)
        knT = BUFA.rearrange("p (h n) -> p h n", h=4)
        Ytok = BUFB.rearrange("p (mt i c) -> p mt i c", mt=2, i=8)
        Vt = BUFB.rearrange("p (kt c) -> p kt c", kt=16)
        statk = [0]

        def stat3():
            k = statk[0] % 20
            statk[0] += 1
            return stat[:, 3 * k:3 * k + 1], stat[:, 3 * k + 1:3 * k + 2], stat[:, 3 * k + 2:3 * k + 3]

        def fm_norm(src_list, nfeat, sq, sd, rs, pl=(2, 8)):
            n = len(src_list)
            for j, sap in enumerate(src_list):
                act(sq[:, j, :], sap, AF.Square)
            pq = psb(lo=pl[0], hi=pl[1])
            mms([(pq, onesb[:], sq[:, j, :], j == 0, j == n - 1, None) for j in range(n)])
            act(sd, pq, AF.Sqrt, scale=1.0 / nfeat, bias=epsc)
            recip('dve', rs, sd)

        for s in range(nseq if stop_after >= 1 else 0):
            xv = x[s].rearrange("(mt m i) d -> mt i m d", mt=2, m=128, i=8)
            yv = y[s].rearrange("(mt m i) d -> mt i m d", mt=2, m=128, i=8)
            ar.reset(TMP0)
            posi = ar.i32(512)
            posf, rtu, ritf, rfr = [ar.f32(512) for _ in range(4)]
            riti = ar.i32(512)
            for cb in range(4):
                cols = slice(cb * 512, (cb + 1) * 512)
                dma(posi[0:64, :], pos[s:s + 1, cols].to_broadcast([64, 512]))
                cp('dve', posf[0:64, :], posi[0:64, :])
                ts('dve', rtu[0:64, :], posf[0:64, :], invf, ALU.mult, 1.0 / TWO_PI, ALU.mult)
                sincos(rtu[0:64, :], riti[0:64, :], ritf[0:64, :], rfr[0:64, :], sinT[:, cols], cosT[:, cols])

            ar.reset(TMP0)
            xt = [ar.f32(1024), ar.f32(1024)]
            hb = [ar.bf(1024) for _ in range(4)]
            hT = [ar.bf(4096).rearrange("p (c n) -> p c n", c=8) for _ in range(2)]
            utile = ar.bf(4096).rearrange("p (g i h) -> p g i h", g=32, i=8)
            sq1 = ar.bf(1024).rearrange("p (c n) -> p c n", c=2)
            sd1 = ar.f32(512)
            rs1 = ar.f32(512)
            rt1 = ar.bf(512)
            rt2 = ar.bf(512)
            memset('pool', krT[64:128, :], 0.0)
            tcnt = [0]

            def prepA(b, only=None):
                mt, q = divmod(b, 2)
                for ii in (range(4) if only is None else [only]):
                    i = 4 * q + ii
                    sl = tcnt[0] % 2
                    tcnt[0] += 1
                    dma(xt[sl], xv[mt, i])
                    ss, sdv, rstd = stat3()
                    act(hb[ii], xt[sl], AF.Square, accum=ss)
                    act(sdv, ss, AF.Sqrt, scale=1.0 / D, bias=epsc)
                    recip('dve', rstd, sdv)
                    ts('dve', hb[ii], xt[sl], rstd, ALU.mult)

            def prepB(b):
                hs = hT[b % 2]
                for ii in range(4):
                    pt = psb(lo=0, hi=2).bitcast(BF16)
                    trs([(pt[:, c * 128:(c + 1) * 128], hb[ii][:, c * 128:(c + 1) * 128], identb[:]) for c in range(8)])
                    cp('act' if ii % 2 == 0 else 'dve', hs[:, :, ii * 128:(ii + 1) * 128], pt.rearrange("p (c m) -> p c m", c=8))

            def proj(b, nxt=None):
                mt, q = divmod(b, 2)

                def hook(k_):
                    if nxt is not None:
                        prepA(nxt, only=k_)

                hs = hT[b % 2]
                cols = slice(b * 512, (b + 1) * 512)

                def proj_fm(wcols, M=128, wsrc=None):
                    pp = psb(lo=2, hi=8)
                    if wsrc is None:
                        lst = [(pp[0:M, :], Win_b[:, c, wcols], hs[:, c, :], c == 0, c == 7, None) for c in range(8)]
                    else:
                        lst = [(pp[0:M, :], wsrc[:, c, :], hs[:, c, :], c == 0, c == 7, None) for c in range(8)]
                    mms(lst)
                    return pp

                pz = [proj_fm(slice(C_ZQ + j * 128, C_ZQ + (j + 1) * 128)) for j in range(2)]
                fm_norm(pz, 256, sq1, sd1, rs1)
                for j in range(2):
                    tt('dve', zqT[:, j, cols], pz[j], rs1, ALU.mult)
                hook(0)
                pk = proj_fm(slice(C_ZKV, C_ZKV + 128))
                fm_norm([pk], 128, sq1, sd1, rs1)
                tt('dve', zkvT[:, cols], pk, rs1, ALU.mult)
                hook(1)
                pr1 = proj_fm(slice(C_ZKR, C_ZKR + 64), M=64)
                pr2 = proj_fm(None, M=64, wsrc=Wkrrot)
                tt('dve', rt1[0:64, :], pr1[0:64, :], cosT[:, cols], ALU.mult)
                tt('dve', rt2[0:64, :], pr2[0:64, :], sinT[:, cols], ALU.mult)
                tt('pool', krT[0:64, cols], rt1[0:64, :], rt2[0:64, :], ALU.add)
                hook(2)
                for j in range(4):
                    pg = proj_fm(slice(C_GA + j * 128, C_GA + (j + 1) * 128))
                    act(gA[:, j, cols], pg, AF.Silu)
                hook(3)
                for j in range(4):
                    pg = proj_fm(slice(C_GS + j * 128, C_GS + (j + 1) * 128))
                    act(gS[:, j, cols], pg, AF.Silu)
                for ii in range(4):
                    i = 4 * q + ii
                    pu = psb(lo=2, hi=8)
                    mms([(pu, hs[:, c, ii * 128:(ii + 1) * 128], Win_b[:, c, C_U:C_U + 512], c == 0, c == 7, None) for c in range(8)])
                    cp('dve' if ii % 2 == 0 else 'act', utile[:, :, i, :], pu.rearrange("p (g h) -> p g h", g=32))
                if q == 1:
                    for g4 in range(8):
                        pt = psb(lo=0, hi=2).bitcast(BF16)
                        trs([(pt[:, k_ * 128:(k_ + 1) * 128], utile[:, g4 * 4 + k_].rearrange("p i h -> p (i h)"), identb[:]) for k_ in range(4)])
                        cp('dve' if g4 % 2 == 0 else 'act', U_g[:, g4 * 4:(g4 + 1) * 4, mt * 128:(mt + 1) * 128],
                           pt[:, 0:512].rearrange("p (k m) -> p k m", k=4))

            prepA(0)
            prepB(0)
            for b in range(4):
                proj(b, nxt=(b + 1 if b + 1 < 4 else None))
                if b + 1 < 4:
                    prepB(b + 1)
            if s == 0:
                dump("U_g", U_g); dump("zqT", zqT); dump("zkvT", zkvT); dump("krT", krT[0:64, :]); dump("gA", gA); dump("gS", gS)
            if stop_after <= 1:
                continue

            ar.reset(TMP0)
            W5 = [ar.bf(1280) for _ in range(3)]
            TABm = [ar.f32(1024).rearrange("p (t x) -> p t x", t=2) for _ in range(2)]
            TABd = [ar.f32(1024).rearrange("p (t x) -> p t x", t=2) for _ in range(2)]
            tA = [ar.f32(512) for _ in range(2)]
            tB = [ar.f32(512) for _ in range(2)]
            a2 = [ar.f32(512) for _ in range(2)]
            b2 = [ar.f32(512) for _ in range(2)]
            Xp = [ar.f32(512) for _ in range(2)]
            rr = [ar.f32(512) for _ in range(2)]

            def h2(ap):
                return ap.rearrange("p (a x) -> p a x", a=2)

            def sw(ap):
                return h2(ap)[:, ::-1, :]

            def w5views(gp):
                w5 = W5[gp % 3]
                Ev = w5[:, 0:512].rearrange("p (d r x) -> p d r x", d=2, r=2)
                Gv = w5[:, 512:1024].rearrange("p (d r x) -> p d r x", d=2, r=2)
                Tv = w5[:, 1024:1280].rearrange("p (a x) -> p a x", a=2)
                return w5, Ev, Gv, Tv

            def st1(idx):
                gp, d = divmod(idx, 2)
                w5, Ev, Gv, Tv = w5views(gp)
                if d == 0:
                    dma(w5, s5w[gp])
                c = d * 16 + gp
                k2 = idx % 2
                tab = TABm[k2]
                dma(tab.rearrange("p t x -> p (t x)"), s5t[c, :, 0:1024])
                px = psb()
                lst = []
                for g2 in range(2):
                    for r_ in range(2):
                        lst.append((px[g2 * 64:(g2 + 1) * 64, r_ * 256:(r_ + 1) * 256], Ev[:, d, r_, g2 * 64:(g2 + 1) * 64],
                                    U_g[:, 2 * gp + g2, :], True, True, (0, 64 * g2) if g2 else None))
                mms(lst)
                tt('dve', h2(tA[k2]), h2(px), h2(tab[:, 0, :]), ALU.mult)
                tt('dve', h2(tB[k2]), sw(px), h2(tab[:, 1, :]), ALU.mult)

            def st2(idx):
                k2 = idx % 2
                tt('pool', Xp[k2], tA[k2], tB[k2], ALU.add)

            def st3(idx):
                gp, d = divmod(idx, 2)
                c = d * 16 + gp
                k2 = idx % 2
                rho = RHO8[:, c:c + 1].to_broadcast([128, 256])
                for r_ in range(2):
                    o_ = rr[k2][:, r_ * 256:(r_ + 1) * 256]
                    i_ = Xp[k2][:, r_ * 256:(r_ + 1) * 256]
                    if d == 1:
                        o_ = o_[:, ::-1]
                        i_ = i_[:, ::-1]
                    scan(o_, rho, i_)

            def st4(idx):
                gp, d = divmod(idx, 2)
                c = d * 16 + gp
                k2 = idx % 2
                tab = TABd[k2]
                dma(tab.rearrange("p t x -> p (t x)"), s5t[c, :, 0:1024])
                tt('pool', h2(a2[k2]), h2(rr[k2]), h2(tab[:, 0, :]), ALU.mult)
                tt('pool', h2(b2[k2]), sw(rr[k2]), h2(tab[:, 1, :]), ALU.mult)

            def st5(idx):
                gp, d = divmod(idx, 2)
                k2 = idx % 2
                xs = Xs[:, gp % 2]
                w5, Ev, Gv, Tv = w5views(gp)
                if d == 0:
                    tt('dve', xs[:, 0, :, 1:256], h2(a2[k2])[:, :, 0:255], h2(b2[k2])[:, :, 0:255], ALU.subtract)
                else:
                    tt('dve', xs[:, 1, :, 0:255], h2(a2[k2])[:, :, 1:256], h2(b2[k2])[:, :, 1:256], ALU.subtract)
                    for mt in range(2):
                        py = psb(2)
                        lst = []
                        for g2 in range(2):
                            oc = py[:, g2 * 512:g2 * 512 + 128]
                            lo, hi = g2 * 64, g2 * 64 + 64
                            lst.append((oc, U_g[:, 2 * gp + g2, mt * 128:(mt + 1) * 128], Tv[:, g2, :], True, False, None))
                            for d_ in range(2):
                                for r_ in range(2):
                                    lst.append((oc, xs[lo:hi, d_, r_, mt * 128:(mt + 1) * 128], Gv[lo:hi, d_, r_, :], False, (d_ == 1 and r_ == 1), None))
                        mms(lst)
                        act(Ytok[:, mt, :, gp * 32:(gp + 1) * 32].rearrange("p j (a h) -> p a j h", a=2),
                            py.rearrange("p (a x) -> p a x", a=2)[:, :, 0:128].rearrange("p a (j h) -> p a j h", j=8), AF.Gelu)

            stages = [st1, st2, st3, st4, st5]
            for t_ in range(32 + 4):
                for k_, f_ in enumerate(stages):
                    if 0 <= t_ - k_ < 32:
                        f_(t_ - k_)
            if s == 0:
                dump("Ytok", Ytok)
            if stop_after <= 2:
                continue

            ar.reset(TMP0)
            sg = ar.bf(2048).rearrange("p (c n) -> p c n", c=4)
            ssm2 = ar.bf(2048).rearrange("p (c n) -> p c n", c=4)
            sq2 = ar.bf(2048).rearrange("p (c n) -> p c n", c=4)
            tn2 = ar.bf(2048).rearrange("p (c n) -> p c n", c=4)
            sd2 = ar.f32(512)
            rs2 = ar.f32(512)
            for mt in range(2):
                for i in range(8):
                    pt = psb().bitcast(BF16)
                    trs([(pt[:, c * 128:(c + 1) * 128], Ytok[:, mt, i, c * 128:(c + 1) * 128], identb[:]) for c in range(4)])
                    c0 = mt * 1024 + i * 128
                    cp('dve' if i % 2 == 0 else 'act', geluT[:, :, c0:c0 + 128], pt[:, 0:512].rearrange("p (c m) -> p c m", c=4))
            for b in range(4):
                cols = slice(b * 512, (b + 1) * 512)
                for mo in range(4):
                    pg = psb()
                    mms([(pg, Wglu_b[:, kc, mo * 128:(mo + 1) * 128], geluT[:, kc, cols], kc == 0, kc == 3, None) for kc in range(4)])
                    act(sg[:, mo, :], pg, AF.Sigmoid, bias=bglu[:, mo:mo + 1])
                tt('dve', ssm2, geluT[:, :, cols], sg, ALU.mult)
                fm_norm([ssm2[:, j, :] for j in range(4)], 512, sq2, sd2, rs2)
                tt('pool', tn2, ssm2, rs2.unsqueeze(1).to_broadcast([128, 4, 512]), ALU.mult)
                tt('dve', gS[:, :, cols], tn2, gS[:, :, cols], ALU.mult)
            if s == 0:
                dump("ssm", gS)
            if stop_after <= 3:
                continue

            ar.reset(TMP0)
            qn = [ar.bf(2048).rearrange("p (h n) -> p h n", h=4) for _ in range(2)]
            qrT = [ar.bf(2048).rearrange("p (h n) -> p h n", h=4) for _ in range(2)]
            PT = [ar.bf(512) for _ in range(3)]
            rinv = [ar.f32(512) for _ in range(2)]
            attn = [ar.bf(2048).rearrange("p (h n) -> p h n", h=4) for _ in range(2)]
            sq3 = ar.bf(2048).rearrange("p (h n) -> p h n", h=4)
            tn3 = ar.bf(2048).rearrange("p (h n) -> p h n", h=4)
            sd3 = ar.f32(512)
            rs3 = ar.f32(512)
            qt1 = ar.f32(512)
            qt2 = ar.f32(512)
            wkv4 = Wukv_b[:].rearrange("p (h f) -> p h f", h=4)
            for h in range(4):
                for b in range(4):
                    cols = slice(b * 512, (b + 1) * 512)
                    pk = psb()
                    mms([(pk, wkv4[:, h, 0:128], zkvT[:, cols], True, True, None)])
                    cp('act' if b % 2 else 'dve', knT[:, h, cols], pk)
            for kt in range(16):
                pv = psb()
                mms([(pv, zkvT[:, kt * 128:(kt + 1) * 128], Wv_b[:], True, True, None)])
                cp('act' if kt % 2 else 'dve', Vt[:, kt, :], pv)
            for q_ in qrT:
                memset('pool', q_[64:128], 0.0)

            def qproj(qb):
                cols = slice(qb * 512, (qb + 1) * 512)
                qn_ = qn[qb % 2]
                qr_ = qrT[qb % 2]
                for h in range(4):
                    pq = psb(lo=0, hi=4)
                    mms([(pq, Wuq_b[:, kc, h * 192:h * 192 + 128], zqT[:, kc, cols], kc == 0, kc == 1, None) for kc in range(2)])
                    cp('act' if h % 2 else 'dve', qn_[:, h, :], pq)
                    p1 = psb(lo=0, hi=4)
                    mms([(p1[0:64, :], Wuq_b[:, kc, h * 192 + 128:h * 192 + 192], zqT[:, kc, cols], kc == 0, kc == 1, None) for kc in range(2)])
                    p2 = psb(lo=0, hi=4)
                    mms([(p2[0:64, :], Wuqrot[:, kc, h * 64:(h + 1) * 64], zqT[:, kc, cols], kc == 0, kc == 1, None) for kc in range(2)])
                    tt('dve', qt1[0:64, :], p1[0:64, :], cosT[:, cols], ALU.mult)
                    tt('dve', qt2[0:64, :], p2[0:64, :], sinT[:, cols], ALU.mult)
                    tt('pool', qr_[0:64, h, :], qt1[0:64, :], qt2[0:64, :], ALU.add)

            def head(qb, h):
                qn_ = qn[qb % 2]
                qr_ = qrT[qb % 2]
                at_ = attn[qb % 2]
                po = PS[:, (4 + 2 * (h % 2)) * 512:(5 + 2 * (h % 2)) * 512]
                pr = PS[:, (5 + 2 * (h % 2)) * 512:(6 + 2 * (h % 2)) * 512]
                pend = None
                nkt = 16
                for kt in range(nkt + 1):
                    cur = None
                    if kt < nkt:
                        pst = psb(lo=0, hi=4)
                        ks = slice(kt * 128, (kt + 1) * 128)
                        mms([(pst, knT[:, h, ks], qn_[:, h, :], True, False, None),
                             (pst, krT[:, ks], qr_[:, h, :], False, True, None)])
                        ptile = PT[kt % 3]
                        act(ptile, pst, AF.Exp, scale=SCALE)
                        cur = (kt, ptile)
                    if pend is not None:
                        k0, p0 = pend
                        mms([(po, Vt[:, k0, h * 128:(h + 1) * 128], p0, k0 == 0, k0 == nkt - 1, None),
                             (pr, onesb[:], p0, k0 == 0, k0 == nkt - 1, None)])
                    pend = cur
                ri = rinv[h % 2]
                recip('dve', ri, pr)
                tt('dve', at_[:, h, :], po, ri, ALU.mult)

            def epilogue(qb):
                cols = slice(qb * 512, (qb + 1) * 512)
                at_ = attn[qb % 2]
                fm_norm([at_[:, j, :] for j in range(4)], 512, sq3, sd3, rs3, pl=(0, 4))
                tt('pool', tn3, at_, rs3.unsqueeze(1).to_broadcast([128, 4, 512]), ALU.mult)
                tt('dve', gA[:, :, cols], tn3, gA[:, :, cols], ALU.mult)

            qproj(0)
            for qb in range(4):
                if qb + 1 < 4:
                    qproj(qb + 1)
                for h in range(4):
                    head(qb, h)
                    if h == 0 and qb > 0:
                        epilogue(qb - 1)
            epilogue(3)
            if s == 0:
                dump("attn", gA)
            if stop_after <= 4:
                continue

            ar.reset(TMP0)
            xt4 = [ar.f32(1024), ar.f32(1024)]
            t4 = [ar.f32(1024), ar.f32(1024)]
            o4 = [ar.f32(1024), ar.f32(1024)]
            junk4 = ar.bf(1024)
            for mt in range(2):
                for i in range(8):
                    k_ = (mt * 8 + i) % 2
                    c0 = mt * 1024 + i * 128
                    dma(xt4[k_], xv[mt, i])
                    pm = psb(2)
                    lst = []
                    for nb in range(2):
                        for kc in range(8):
                            src = gA[:, kc, c0:c0 + 128] if kc < 4 else gS[:, kc - 4, c0:c0 + 128]
                            lst.append((pm[:, nb * 512:(nb + 1) * 512], src, Wout_b[:, kc, nb * 512:(nb + 1) * 512], kc == 0, kc == 7, None))
                    mms(lst)
                    ss, sdv, rstd = stat3()
                    ssb, _u1, _u2 = stat3()
                    act(junk4[:, 0:512], pm[:, 0:512], AF.Square, accum=ss)
                    act(junk4[:, 512:1024], pm[:, 512:1024], AF.Square, accum=ssb)
                    tt('dve', ss, ss, ssb, ALU.add)
                    act(sdv, ss, AF.Sqrt, scale=1.0 / D, bias=epsc)
                    recip('dve', rstd, sdv)
                    for nb in range(2):
                        stt('dve', t4[k_][:, nb * 512:(nb + 1) * 512], pm[:, nb * 512:(nb + 1) * 512], rstd, gpost_b[:, nb * 512:(nb + 1) * 512], ALU.mult, ALU.mult)
                    tt('pool', o4[k_], t4[k_], xt4[k_], ALU.add)
                    FINAL_OPS.append(dma(yv[mt, i], o4[k_]))

        Sd.analyze()
        Sd.emit(final_wait_ops=FINAL_OPS)
    return nc


def _consts():
    ident = np.eye(128, dtype=np.float32)
    ii = np.arange(128) // 16
    maskf = (ii[None, :] >= ii[:, None]).astype(np.float32)
    maskb = (ii[:, None] >= ii[None, :]).astype(np.float32)
    cst = np.stack([ident, maskf, maskb], axis=1).copy()
    kvals = np.tile(np.arange(-7, 9, dtype=np.float32)[None, :], (128, 1)).copy()
    mvals = np.tile(np.arange(256, dtype=np.float32)[None, :], (128, 1)).copy()
    invf = (np.float32(10000.0) ** (-np.arange(0, 64, 2, dtype=np.float32) / np.float32(64))).astype(np.float32)
    invf64 = np.concatenate([invf, invf])
    return cst, kvals, mvals, invf64


def _prep_shared(inp):
    f = np.float32
    cst, kvals, mvals, invf64 = _consts()
    smallv = np.zeros((128, 64), f)
    smallv[:, 0:8] = np.asarray(inp["pre_norm_g"], f).reshape(8, 128).T
    smallv[:, 8:10] = np.asarray(inp["q_norm_g"], f).reshape(2, 128).T
    smallv[:, 10:11] = np.asarray(inp["kv_norm_g"], f).reshape(1, 128).T
    gout = np.concatenate([np.asarray(inp["attn_out_g"], f).reshape(-1), np.asarray(inp["ssm_out_g"], f).reshape(-1)])
    smallv[:, 11:19] = gout.reshape(8, 128).T
    smallv[:, 19:23] = np.asarray(inp["b_glu"], f).reshape(4, 128).T
    smallv[0:64, 23] = invf64
    smallv[:, 24] = EPS

    def pm(a):
        a = np.asarray(a, f).reshape(2, 16, 2, 64)
        return a.transpose(2, 3, 0, 1).reshape(128, 32)

    logdt = np.asarray(inp["s5_log_dt"], f).reshape(2, 16, 2)
    logdt_t = np.broadcast_to(logdt.transpose(2, 0, 1)[:, None, :, :], (2, 64, 2, 16)).reshape(128, 32)
    dsgn = np.concatenate([np.ones((128, 16), f), -np.ones((128, 16), f)], axis=1)
    s5p = np.stack([pm(inp["s5_lam_re"][0]), pm(inp["s5_lam_im"][0]), logdt_t, dsgn], axis=1).astype(f).copy()

    def pb(a):
        a = np.asarray(a, f).reshape(2, 16, 2, 64, 16)
        return a.transpose(2, 3, 0, 1, 4).reshape(128, 512)

    def pc(a):
        a = np.asarray(a, f).reshape(2, 16, 2, 16, 64)
        return a.transpose(2, 4, 0, 1, 3).reshape(128, 512)

    s5bc = np.stack([pb(inp["s5_b_re"][0]), pb(inp["s5_b_im"][0]), pc(inp["s5_c_re"][0]), pc(inp["s5_c_im"][0])], axis=1).astype(f).copy()
    dsk = np.asarray(inp["s5_d"], f).reshape(32, 16)
    dvec = np.broadcast_to(dsk.T[None, :, :], (8, 16, 32)).reshape(128, 32).astype(f).copy()
    shared = {
        "w_in": np.ascontiguousarray(np.asarray(inp["w_in"], f)[0]),
        "w_uq": np.ascontiguousarray(np.asarray(inp["w_uq"], f)[0]),
        "w_ukv": np.ascontiguousarray(np.asarray(inp["w_ukv"], f)[0]),
        "w_glu": np.ascontiguousarray(np.asarray(inp["w_glu"], f)[0]),
        "w_out": np.ascontiguousarray(np.asarray(inp["w_out"], f)[0]),
        "smallv": smallv,
        "gpost": np.asarray(inp["post_norm_g"], f).reshape(1, D).copy(),
        "s5p": s5p, "s5bc": s5bc, "dvec": dvec, "cst": cst, "kvals": kvals, "mvals": mvals,
    }
    return shared


def _perm_pos(p):
    n = p.shape[0]
    return np.ascontiguousarray(p.reshape(n, 2, 128, 8).transpose(0, 1, 3, 2).reshape(n, S)).astype(np.int32)


_NC_CACHE = {}


def kernel(**inputs):
    x = np.asarray(inputs["x"], np.float32)
    positions = np.asarray(inputs["positions"], np.int32)
    B = x.shape[0]
    ncores = 8
    nseq = B // ncores
    shared = _prep_shared(inputs)
    if nseq not in _NC_CACHE:
        _NC_CACHE[nseq] = build(nseq)
    nc = _NC_CACHE[nseq]
    in_maps = []
    for c in range(ncores):
        m = dict(shared)
        m["x"] = np.ascontiguousarray(x[c * nseq:(c + 1) * nseq])
        m["pos"] = _perm_pos(positions[c * nseq:(c + 1) * nseq])
        in_maps.append(m)
    res = run_bass_kernel_spmd(nc, in_maps, core_ids=list(range(ncores)))
    out = np.concatenate([np.asarray(r["y"]) for r in res.results], axis=0)
    return out.astype(np.float32)
```

```python
import numpy as np
import concourse.bass as bass
import concourse.mybir as mybir

F32 = mybir.dt.float32
BF16 = mybir.dt.bfloat16
I32 = mybir.dt.int32
AF = mybir.ActivationFunctionType
ALU = mybir.AluOpType
AX = mybir.AxisListType

_DTSIZE = {}


def dtsize(dt):
    s = str(dt)
    if s in _DTSIZE:
        return _DTSIZE[s]
    if '32' in s:
        v = 4
    elif '16' in s:
        v = 2
    elif '64' in s:
        v = 8
    else:
        v = 1
    _DTSIZE[s] = v
    return v


GRAN = 16


def region_of(ap):
    t = ap.tensor
    name = t.name
    es = dtsize(ap.dtype)
    dims = ap.ap
    off = ap.offset
    space = str(ap.space)
    if 'DRAM' in space.upper() or 'HBM' in space.upper() or 'dram' in space.lower():
        lo = off
        hi = off
        for st, cnt in dims:
            if st >= 0:
                hi += st * (cnt - 1)
            else:
                lo += st * (cnt - 1)
        return (name, 'dram', 0, 1, lo * es, (hi + 1) * es)
    pstep, pcnt = dims[0]
    p0 = off // pstep if pstep > 0 else 0
    rem = off - p0 * pstep if pstep > 0 else off
    lo = rem
    hi = rem
    for st, cnt in dims[1:]:
        if st >= 0:
            hi += st * (cnt - 1)
        else:
            lo += st * (cnt - 1)
    return (name, 'sb', p0, p0 + pcnt, lo * es, (hi + 1) * es)


class Op:
    __slots__ = ('eng', 'fn', 'reads', 'writes', 'idx', 'deps', 'has_dep', 'ord', 'sem', 'semval', 'is_dma', 'waits', 'pre_wait')

    def __init__(self, eng, fn, reads, writes, is_dma):
        self.eng = eng
        self.fn = fn
        self.reads = reads
        self.writes = writes
        self.is_dma = is_dma
        self.deps = set()
        self.has_dep = False
        self.ord = None
        self.sem = None
        self.semval = None
        self.waits = []
        self.pre_wait = None


ENGS = ['pe', 'act', 'dve', 'pool', 'sp']


class Sched:
    def __init__(self, nc, n_dma_sems=12, sem_limit=4000):
        self.nc = nc
        self.ops = []
        self.n_dma_sems = n_dma_sems
        self.sem_limit = sem_limit
        self.track = {}

    def add(self, eng, fn, reads=(), writes=(), dma=False):
        rr = [region_of(a) if not isinstance(a, tuple) else a for a in reads]
        ww = [region_of(a) if not isinstance(a, tuple) else a for a in writes]
        op = Op(eng, fn, rr, ww, dma)
        op.idx = len(self.ops)
        self.ops.append(op)
        return op

    def _arr(self, name, b1):
        ng = (b1 + GRAN - 1) // GRAN + 1
        t = self.track.get(name)
        if t is None:
            t = {'w': np.full((4, ng), -1, np.int64), 'r': np.full((len(ENGS) + 1, 4, ng), -1, np.int64)}
            self.track[name] = t
        elif t['w'].shape[1] < ng:
            old = t['w'].shape[1]
            w = np.full((4, ng), -1, np.int64)
            w[:, :old] = t['w']
            r = np.full((len(ENGS) + 1, 4, ng), -1, np.int64)
            r[:, :, :old] = t['r']
            t['w'] = w
            t['r'] = r
        return t

    def analyze(self):
        ops = self.ops
        for op in ops:
            slot = len(ENGS) if op.is_dma else ENGS.index(op.eng)
            deps = set()
            for (name, sp, p0, p1, b0, b1) in op.reads:
                t = self._arr(name, b1)
                q0, q1 = p0 // 32, (p1 - 1) // 32 + 1
                g0, g1 = b0 // GRAN, (b1 - 1) // GRAN + 1
                w = t['w'][q0:q1, g0:g1]
                for d in np.unique(w):
                    if d >= 0:
                        deps.add(int(d))
            for (name, sp, p0, p1, b0, b1) in op.writes:
                t = self._arr(name, b1)
                q0, q1 = p0 // 32, (p1 - 1) // 32 + 1
                g0, g1 = b0 // GRAN, (b1 - 1) // GRAN + 1
                w = t['w'][q0:q1, g0:g1]
                for d in np.unique(w):
                    if d >= 0:
                        deps.add(int(d))
                r = t['r'][:, q0:q1, g0:g1]
                for d in np.unique(r):
                    if d >= 0:
                        deps.add(int(d))
            for (name, sp, p0, p1, b0, b1) in op.reads:
                t = self.track[name]
                q0, q1 = p0 // 32, (p1 - 1) // 32 + 1
                g0, g1 = b0 // GRAN, (b1 - 1) // GRAN + 1
                if op.is_dma:
                    prev = t['r'][slot, q0:q1, g0:g1]
                    for d in np.unique(prev):
                        if d >= 0:
                            deps.add(int(d))
                t['r'][slot, q0:q1, g0:g1] = op.idx
            for (name, sp, p0, p1, b0, b1) in op.writes:
                t = self.track[name]
                q0, q1 = p0 // 32, (p1 - 1) // 32 + 1
                g0, g1 = b0 // GRAN, (b1 - 1) // GRAN + 1
                t['w'][q0:q1, g0:g1] = op.idx
                t['r'][:, q0:q1, g0:g1] = -1
            deps.discard(op.idx)
            fdeps = set()
            for d in deps:
                o = ops[d]
                if (not o.is_dma) and (not op.is_dma) and o.eng == op.eng:
                    if op.eng == 'pe':
                        continue
                    raw = False
                    for (n1, _, p0, p1, b0, b1) in op.reads:
                        for (n2, _, P0, P1, B0, B1) in o.writes:
                            if n1 == n2 and p0 < P1 and P0 < p1 and b0 < B1 and B0 < b1:
                                raw = True
                    if not raw:
                        continue
                fdeps.add(d)
            op.deps = fdeps
            for d in fdeps:
                ops[d].has_dep = True

    def emit(self, final_wait_ops=()):
        nc = self.nc
        ops = self.ops
        for o in final_wait_ops:
            o.has_dep = True
        for o in ops:
            if o.is_dma:
                o.has_dep = True
        counts = {e: 0 for e in ENGS}
        for op in ops:
            if op.has_dep and not op.is_dma:
                counts[op.eng] += 1
        nsem_eng = {e: max(1, (counts[e] + self.sem_limit - 1) // self.sem_limit) for e in ENGS}
        import contextlib
        with contextlib.ExitStack() as es:
            eng_sems = {e: [es.enter_context(nc.semaphore(f"c_{e}_{i}")) for i in range(nsem_eng[e])] for e in ENGS}
            dma_sems = [es.enter_context(nc.semaphore(f"d_{i}")) for i in range(self.n_dma_sems)]
            cnt = {e: 0 for e in ENGS}
            dma_cnt = [0] * self.n_dma_sems
            dma_rr = 0
            for op in ops:
                if not op.has_dep:
                    continue
                if op.is_dma:
                    s = dma_rr % self.n_dma_sems
                    dma_rr += 1
                    if dma_cnt[s] > 0:
                        op.pre_wait = (dma_sems[s], 16 * dma_cnt[s])
                    dma_cnt[s] += 1
                    op.sem = dma_sems[s]
                    op.semval = 16 * dma_cnt[s]
                else:
                    k = cnt[op.eng]
                    cnt[op.eng] += 1
                    op.sem = eng_sems[op.eng][k // self.sem_limit]
                    op.semval = (k % self.sem_limit) + 1
            waited = {e: {} for e in ENGS}
            for op in ops:
                need = {}
                for d in op.deps:
                    o = ops[d]
                    key = id(o.sem)
                    if key not in need or need[key][1] < o.semval:
                        need[key] = (o.sem, o.semval)
                if op.pre_wait is not None:
                    key = id(op.pre_wait[0])
                    if key not in need or need[key][1] < op.pre_wait[1]:
                        need[key] = op.pre_wait
                wl = []
                for key, (s, v) in need.items():
                    if waited[op.eng].get(key, 0) >= v:
                        continue
                    waited[op.eng][key] = v
                    wl.append((s, v))
                op.waits = wl
            self.n_waits = sum(len(o.waits) for o in ops)
            print('SCHED ops', len(ops), 'incs', dict(cnt), 'dma', sum(dma_cnt), 'waits', self.n_waits, flush=True)
            finals = {}
            for o in final_wait_ops:
                finals[id(o.sem)] = (o.sem, max(o.semval, finals.get(id(o.sem), (None, 0))[1]))
            with nc.Block() as block:
                def run(engname, eng):
                    for op in ops:
                        if op.eng != engname:
                            continue
                        for (s, v) in op.waits:
                            eng.wait_ge(s, v)
                        ins = op.fn(eng)
                        if op.has_dep:
                            ins.then_inc(op.sem, 16 if op.is_dma else 1)
                    if engname == 'sp':
                        for (s, v) in finals.values():
                            eng.wait_ge(s, v)

                @block.tensor
                def _(eng):
                    run('pe', eng)

                @block.scalar
                def _(eng):
                    run('act', eng)

                @block.vector
                def _(eng):
                    run('dve', eng)

                @block.gpsimd
                def _(eng):
                    run('pool', eng)

                @block.sync
                def _(eng):
                    run('sp', eng)

import contextlib
import math
from concourse.bass_utils import run_bass_kernel_spmd

D = 1024
S = 2048
DIN = 1984
C_ZQ, C_ZKV, C_ZKR, C_GA, C_U, C_GS = 0, 256, 384, 448, 960, 1472
EPS = 1e-6
TWO_PI = 2.0 * math.pi
SCALE = 192.0 ** -0.5
NSEQ_CORE = 4


def build(nseq, stop_after=99, dbg=None):
    nc = bass.Bass("TRN2", target_bir_lowering=False)

    def din(name, shape, dt=F32):
        return nc.dram_tensor(name, shape, dt, kind="ExternalInput").ap()

    x = din("x", [nseq, S, D])
    pos = din("pos", [nseq, S], I32)
    w_in = din("w_in", [D, DIN])
    w_uq = din("w_uq", [256, 768])
    w_ukv = din("w_ukv", [128, 1024])
    w_glu = din("w_glu", [512, 512])
    w_out = din("w_out", [D, D])
    smallv = din("smallv", [128, 64])
    gpost = din("gpost", [1, D])
    s5p = din("s5p", [128, 4, 32])
    s5bc = din("s5bc", [128, 4, 512])
    dvec = din("dvec", [128, 32])
    cst = din("cst", [128, 3, 128])
    kvals = din("kvals", [128, 16])
    mvals = din("mvals", [128, 256])
    y = nc.dram_tensor("y", [nseq, S, D], F32, kind="ExternalOutput").ap()
    s5w = nc.dram_tensor("s5w", [16, 128, 1280], BF16, kind="Internal").ap()
    s5t = nc.dram_tensor("s5t", [32, 128, 1536], F32, kind="Internal").ap()

    es = contextlib.ExitStack()
    with es:
        def sb(name, shape, dt):
            return es.enter_context(nc.sbuf_tensor(name, shape, dt))

        Win_b = sb("Win_b", [128, 8, DIN], BF16)
        Wkrrot = sb("Wkrrot", [128, 8, 64], BF16)
        Wuq_b = sb("Wuq_b", [128, 2, 768], BF16)
        Wuqrot = sb("Wuqrot", [128, 2, 256], BF16)
        Wukv_b = sb("Wukv_b", [128, 1024], BF16)
        Wglu_b = sb("Wglu_b", [128, 4, 512], BF16)
        Wv_b = sb("Wv_b", [128, 512], BF16)
        Wout_b = sb("Wout_b", [128, 8, D], BF16)
        identb = sb("identb", [128, 128], BF16)
        onesb = sb("onesb", [128, 128], BF16)
        cstf = sb("cstf", [128, 3, 128], F32)
        gpost_b = sb("gpost_b", [128, D], F32)
        sv = sb("sv", [128, 64], F32)
        RHO8 = sb("RHO8", [128, 32], F32)
        cosT = sb("cosT", [64, S], BF16)
        sinT = sb("sinT", [64, S], BF16)
        stat = sb("stat", [128, 64], F32)
        Xs = sb("Xs", [128, 2, 2, 2, 256], BF16)
        BIG = sb("BIG", [128, 33024], F32)
        PS = es.enter_context(nc.psum_tensor("PS", [128, 4096], F32))

        identf = cstf[:, 0, :]
        maskf = cstf[:, 1, :]
        maskb = cstf[:, 2, :]
        gpre = sv[:, 0:8]
        gq = sv[:, 8:10]
        gkv = sv[:, 10:11]
        gout = sv[:, 11:19]
        bglu = sv[:, 19:23]
        invf = sv[0:64, 23:24]
        epsc = sv[:, 24:25]

        Sd = Sched(nc)
        A = Sd.add
        FINAL_OPS = []

        class Arena:
            def __init__(self):
                self.off = 0

            def reset(self, off=0):
                self.off = off

            def f32(self, n):
                a = BIG[:, self.off:self.off + n]
                self.off += n
                assert self.off <= 33024, self.off
                return a

            def bf(self, n):
                assert n % 2 == 0
                return self.f32(n // 2).bitcast(BF16)

            def i32(self, n):
                return self.f32(n).bitcast(I32)

        ar = Arena()
        ps_rr = {}

        def psb(n=1, lo=0, hi=8):
            k = ps_rr.get((lo, hi), lo)
            if k + n > hi:
                k = lo
            ps_rr[(lo, hi)] = k + n
            return PS[:, k * 512:(k + n) * 512]

        def scan(out, rho, in_):
            return A('dve', lambda e: e.tensor_tensor_scan(out=out, data0=rho, data1=in_, initial=0.0, op0=ALU.mult, op1=ALU.add),
                     [rho, in_], [out])

        DBG = {}

        def dump(name, ap):
            if dbg is None or name not in dbg:
                return
            t = nc.dram_tensor("dbg_" + name, list(ap.shape), ap.dtype, kind="ExternalOutput").ap()
            FINAL_OPS.append(dma(t, ap))

        def dma(out, in_, eng='sp'):
            return A(eng, lambda e: e.dma_start(out=out, in_=in_), [in_], [out], dma=True)

        def tt(eng, out, in0, in1, op):
            return A(eng, lambda e: e.tensor_tensor(out=out, in0=in0, in1=in1, op=op), [in0, in1], [out])

        def ts(eng, out, in0, s1, op0, s2=None, op1=None):
            rd = [in0] + [s for s in (s1, s2) if not isinstance(s, (int, float, type(None)))]
            if op1 is None:
                return A(eng, lambda e: e.tensor_scalar(out=out, in0=in0, scalar1=s1, scalar2=None, op0=op0), rd, [out])
            return A(eng, lambda e: e.tensor_scalar(out=out, in0=in0, scalar1=s1, scalar2=s2, op0=op0, op1=op1), rd, [out])

        def stt(eng, out, in0, sc, in1, op0, op1):
            rd = [in0, in1] + ([] if isinstance(sc, (int, float)) else [sc])
            return A('dve', lambda e: e.scalar_tensor_tensor(out=out, in0=in0, scalar=sc, in1=in1, op0=op0, op1=op1), rd, [out])

        def cp(eng, out, in_):
            if eng == 'act':
                return A('act', lambda e: e.copy(out=out, in_=in_), [in_], [out])
            return A(eng, lambda e: e.tensor_copy(out=out, in_=in_), [in_], [out])

        def act(out, in_, func, scale=1.0, bias=None, accum=None):
            rd = [in_] + ([] if bias is None else [bias]) + ([] if isinstance(scale, (int, float)) else [scale])
            wr = [out] + ([] if accum is None else [accum])
            kw = {}
            if bias is not None:
                kw['bias'] = bias
            if accum is not None:
                kw['accum_out'] = accum
            return A('act', lambda e: e.activation(out=out, in_=in_, func=func, scale=scale, **kw), rd, wr)

        def recip(eng, out, in_):
            return A('dve', lambda e: e.reciprocal(out=out, in_=in_), [in_], [out])

        def mms(lst, extra_reads=()):
            def fn(e):
                ins = None
                for (o, l, r, st, sp_, tp) in lst:
                    if tp is None:
                        ins = e.matmul(o, l, r, start=st, stop=sp_)
                    else:
                        ins = e.matmul(o, l, r, start=st, stop=sp_, tile_position=tp)
                return ins
            rd = []
            wr = []
            for (o, l, r, st, sp_, tp) in lst:
                rd += [l, r]
                wr.append(o)
            return A('pe', fn, rd + list(extra_reads), wr)

        def trs(lst):
            def fn(e):
                ins = None
                for (o, i_, idn) in lst:
                    ins = e.transpose(o, i_, idn)
                return ins
            rd = []
            wr = []
            for (o, i_, idn) in lst:
                rd += [i_, idn]
                wr.append(o)
            return A('pe', fn, rd, wr)

        def memset(eng, out, val):
            return A(eng, lambda e: e.memset(out, val), [], [out])

        dma(sv[:], smallv)
        dma(cstf[:], cst)
        dma(gpost_b[:], gpost.to_broadcast([128, D]))
        cp('dve', identb[:], identf)
        memset('pool', onesb[:], 1.0)
        memset('pool', Xs[:], 0.0)

        ar.reset()
        wst = [ar.f32(2048), ar.f32(2048)]
        engs3 = ['dve', 'pool', 'dve']

        def scale_cast(eng, out, in_, scal):
            if eng == 'act':
                return A('act', lambda e: e.activation(out=out, in_=in_, func=AF.Copy, scale=scal), [in_, scal], [out])
            return ts(eng, out, in_, scal, ALU.mult)

        k = 0
        for c in range(8):
            st_ = wst[k % 2]
            dma(st_[:, 0:DIN], w_in[c * 128:(c + 1) * 128, :])
            scale_cast(engs3[k % 3], Win_b[:, c, :], st_[:, 0:DIN], gpre[:, c:c + 1])
            k += 1
        for c in range(8):
            st_ = wst[k % 2]
            dma(st_[:, 0:D], w_out[c * 128:(c + 1) * 128, :])
            scale_cast(engs3[k % 3], Wout_b[:, c, :], st_[:, 0:D], gout[:, c:c + 1])
            k += 1
        for c in range(2):
            st_ = wst[k % 2]
            dma(st_[:, 0:768], w_uq[c * 128:(c + 1) * 128, :])
            scale_cast(engs3[k % 3], Wuq_b[:, c, :], st_[:, 0:768], gq[:, c:c + 1])
            k += 1
        st_ = wst[k % 2]
        dma(st_[:, 0:1024], w_ukv)
        scale_cast(engs3[k % 3], Wukv_b[:], st_[:, 0:1024], gkv)
        k += 1
        st_ = wst[k % 2]
        dma(st_[:].rearrange("p (c n) -> p c n", c=4), w_glu.rearrange("(c p) n -> p c n", p=128))
        cp(engs3[k % 3], Wglu_b[:].rearrange("p c n -> p (c n)"), st_[:])
        k += 1
        cp('pool', Wv_b[:].rearrange("p (h f) -> p h f", h=4), Wukv_b[:].rearrange("p (h f) -> p h f", h=4)[:, :, 128:256])
        ts('dve', Wkrrot[:, :, 0:32], Win_b[:, :, C_ZKR + 32:C_ZKR + 64], -1.0, ALU.mult)
        cp('dve', Wkrrot[:, :, 32:64], Win_b[:, :, C_ZKR:C_ZKR + 32])
        wq4 = Wuq_b[:].rearrange("p c (h f) -> p c h f", h=4)
        wr4 = Wuqrot[:].rearrange("p c (h f) -> p c h f", h=4)
        for c in range(2):
            ts('dve', wr4[:, c, :, 0:32], wq4[:, c, :, 160:192], -1.0, ALU.mult)
            cp('dve', wr4[:, c, :, 32:64], wq4[:, c, :, 128:160])

        def sincos(tu_ap, it_ap, itf_ap, fr_ap, s_out, c_out, s_scale=TWO_PI, eng='dve', s_out2=None):
            cp(eng, it_ap, tu_ap)
            cp(eng, itf_ap, it_ap)
            tt(eng, fr_ap, tu_ap, itf_ap, ALU.subtract)
            act(s_out, fr_ap, AF.Sin, scale=s_scale)
            if s_out2 is not None:
                act(s_out2, fr_ap, AF.Sin, scale=-s_scale)
            ts(eng, fr_ap, tu_ap, 0.25, ALU.add)
            cp(eng, it_ap, fr_ap)
            cp(eng, itf_ap, it_ap)
            tt(eng, fr_ap, fr_ap, itf_ap, ALU.subtract)
            act(c_out, fr_ap, AF.Sin, scale=TWO_PI)


        import os as _os
        S5CUT = int(_os.environ.get('S5CUT', '99'))

        def s5_setup():
            ar.reset(4096)
            P5 = ar.f32(128).rearrange("p (a c) -> p a c", a=4)
            BC = ar.f32(2048).rearrange("p (a c h) -> p a c h", a=4, c=32)
            KV = ar.f32(16)
            MV = ar.f32(256)
            DV = ar.f32(32)
            dma(P5, s5p)
            dma(BC.rearrange("p a c h -> p a (c h)"), s5bc)
            dma(KV, kvals)
            dma(MV, mvals)
            dma(DV, dvec)
            LAMRE, LAMIM, LOGDT, DSGN = P5[:, 0, :], P5[:, 1, :], P5[:, 2, :], P5[:, 3, :]
            BRE, BIM, CRE, CIM = BC[:, 0], BC[:, 1], BC[:, 2], BC[:, 3]

            def t32():
                return ar.f32(32)

            def t512():
                return ar.f32(512).rearrange("p (c k) -> p c k", c=32)

            lr, dt_, lrdt, th, den, rden, ca, cb, cr, ci, q1, q2, ff = [t32() for _ in range(13)]
            ts('dve', lr, LAMRE, -1e-4, ALU.min)
            act(dt_, LOGDT, AF.Exp)
            tt('dve', lrdt, lr, dt_, ALU.mult)
            tt('dve', th, LAMIM, dt_, ALU.mult)
            ang, lnm, mag, tu, itf, fr, sinv, cosv, PWr, PWi = [t512() for _ in range(10)]
            iti = ar.i32(512).rearrange("p (c k) -> p c k", c=32)
            thb = th.unsqueeze(2).to_broadcast([128, 32, 16])
            lrb = lrdt.unsqueeze(2).to_broadcast([128, 32, 16])
            kvb = KV.unsqueeze(1).to_broadcast([128, 32, 16])
            tt('dve', ang, thb, kvb, ALU.mult)
            tt('dve', lnm, lrb, kvb, ALU.mult)
            act(mag, lnm, AF.Exp)

            if S5CUT <= 1:
                return
            ts('dve', tu, ang, 1.0 / TWO_PI, ALU.mult)
            sincos(tu, iti, itf, fr, sinv, cosv)
            tt('dve', PWr, mag, cosv, ALU.mult)
            tt('dve', PWi, mag, sinv, ALU.mult)
            cp('dve', RHO8[:], mag[:, :, 15])
            if S5CUT <= 2:
                return
            ts('dve', ca, PWr[:, :, 8], -1.0, ALU.add)
            cp('dve', cb, PWi[:, :, 8])
            tt('dve', q1, lr, lr, ALU.mult)
            tt('dve', q2, LAMIM, LAMIM, ALU.mult)
            tt('dve', den, q1, q2, ALU.add)
            recip('dve', rden, den)
            tt('dve', q1, ca, lr, ALU.mult)
            tt('dve', q2, cb, LAMIM, ALU.mult)
            tt('dve', q1, q1, q2, ALU.add)
            tt('dve', cr, q1, rden, ALU.mult)
            tt('dve', q1, cb, lr, ALU.mult)
            tt('dve', q2, ca, LAMIM, ALU.mult)
            tt('dve', q1, q1, q2, ALU.subtract)
            tt('dve', ci, q1, rden, ALU.mult)
            Bbr, Bbi, w1, w2 = [t512() for _ in range(4)]
            crb = cr.unsqueeze(2).to_broadcast([128, 32, 16])
            cib = ci.unsqueeze(2).to_broadcast([128, 32, 16])
            tt('dve', w1, crb, BRE, ALU.mult)
            tt('dve', w2, cib, BIM, ALU.mult)
            tt('dve', Bbr, w1, w2, ALU.subtract)
            tt('dve', w1, crb, BIM, ALU.mult)
            tt('dve', w2, cib, BRE, ALU.mult)
            tt('dve', Bbi, w1, w2, ALU.add)
            ts('dve', ff, th, 8.0 / TWO_PI, ALU.mult)
            fi_ = ar.i32(32)
            fif = t32()
            cp('dve', fi_, ff)
            cp('dve', fif, fi_)
            tt('dve', ff, ff, fif, ALU.subtract)
            tt('dve', ff, ff, DSGN, ALU.mult)
            mark0 = ar.off

            if S5CUT <= 3:
                return
            TABST = ar.f32(4 * 1536).rearrange("p (c t x) -> p c t x", c=4, t=3)
            tq = ar.f32(1024).rearrange("p (c m) -> p c m", c=4)
            tqi = ar.i32(1024).rearrange("p (c m) -> p c m", c=4)
            tqf = ar.f32(1024).rearrange("p (c m) -> p c m", c=4)
            tqr = ar.f32(1024).rearrange("p (c m) -> p c m", c=4)
            mvb = MV.unsqueeze(1).to_broadcast([128, 4, 256])
            for bt in range(8):
                c0 = bt * 4
                eng = 'dve'
                fb = ff[:, c0:c0 + 4].unsqueeze(2).to_broadcast([128, 4, 256])
                tt(eng, tq, fb, mvb, ALU.mult)
                sincos(tq, tqi, tqf, tqr, TABST[:, :, 1, 0:256], TABST[:, :, 0, 0:256], eng=eng, s_out2=TABST[:, :, 1, 256:512])
                cp(eng, TABST[:, :, 0, 256:512], TABST[:, :, 0, 0:256])
                cp(eng, TABST[:, :, 2, 0:256], TABST[:, :, 1, 256:512])
                cp(eng, TABST[:, :, 2, 256:512], TABST[:, :, 1, 0:256])
                dma(s5t[c0:c0 + 4].rearrange("c p x -> p c x"), TABST.rearrange("p c t x -> p c (t x)"))

            if S5CUT <= 4:
                return
            ar.reset(mark0)
            TACC = ar.f32(32 * 128).rearrange("p (g x) -> p g x", g=32)
            TST = ar.bf(32 * 128).rearrange("p (g x) -> p g x", g=32)

            def q4():
                return ar.f32(8 * 128).rearrange("p (g i h) -> p g i h", g=8, i=8)

            def q4b():
                return ar.bf(8 * 128).rearrange("p (g i h) -> p g i h", g=8, i=8)

            Etr, Eti, Ttr, Tti, Gtr, Gti = [q4b() for _ in range(6)]
            v1, v2 = q4(), q4()
            EST = ar.bf(8 * 256).rearrange("p (g r x) -> p g r x", g=8, r=2)
            GST = ar.bf(8 * 256).rearrange("p (g r x) -> p g r x", g=8, r=2)

            def bc_pw(pw, c0, sl):
                return pw[:, c0:c0 + 8, sl].unsqueeze(3).to_broadcast([128, 8, 8, 16])

            def bc_x(xx, c0):
                return xx[:, c0:c0 + 8, :].unsqueeze(2).to_broadcast([128, 8, 8, 16])

            def cmul(eng, outr, outi, pr_, pi_, xr_, xi_, neg_im=False):
                tt(eng, v1, pr_, xr_, ALU.mult)
                tt(eng, v2, pi_, xi_, ALU.mult)
                tt(eng, outr, v1, v2, ALU.subtract)
                tt(eng, v1, pr_, xi_, ALU.mult)
                tt(eng, v2, pi_, xr_, ALU.mult)
                if neg_im:
                    stt(eng, outi, v1, -1.0, v2, ALU.mult, ALU.subtract)
                else:
                    tt(eng, outi, v1, v2, ALU.add)

            SL_E = [slice(14, 6, -1), slice(7, 15)]
            SL_G = [slice(8, 16), slice(15, 7, -1)]
            SL_TE = [slice(7, None, -1), slice(7, 15)]
            SL_TG = [slice(7, 15), slice(7, None, -1)]
            for d in range(2):
                for hf in range(2):
                    c0 = d * 16 + hf * 8
                    gp0 = hf * 8
                    eng = 'dve'
                    cmul(eng, Etr, Eti, bc_pw(PWr, c0, SL_E[d]), bc_pw(PWi, c0, SL_E[d]), bc_x(Bbr, c0), bc_x(Bbi, c0))
                    gr_v = GST[:, :, 0, :].rearrange("p g (j h) -> p g j h", j=8)
                    gi_v = GST[:, :, 1, :].rearrange("p g (j h) -> p g j h", j=8)
                    cmul(eng, gr_v, gi_v, bc_pw(PWr, c0, SL_G[d]), bc_pw(PWi, c0, SL_G[d]), bc_x(CRE, c0), bc_x(CIM, c0), neg_im=True)
                    dma(s5w[gp0:gp0 + 8, :, 512 + d * 256:512 + d * 256 + 256].rearrange("g p x -> p g x"),
                        GST.rearrange("p g r x -> p g (r x)"))
                    if d == 0:
                        cmul(eng, Ttr, Tti, bc_pw(PWr, c0, SL_TE[d]), bc_pw(PWi, c0, SL_TE[d]), bc_x(Bbr, c0), bc_x(Bbi, c0))
                        te_r, te_i = Ttr, Tti
                    else:
                        te_r, te_i = Etr, Eti
                    cmul(eng, Gtr, Gti, bc_pw(PWr, c0, SL_TG[d]), bc_pw(PWi, c0, SL_TG[d]), bc_x(CRE, c0), bc_x(CIM, c0), neg_im=True)
                    for gl in range(8 if S5CUT > 5 else 0):
                        pe_ = psb().bitcast(BF16)
                        trs([(pe_[:, 0:128], Etr[:, gl].rearrange("p i h -> p (i h)"), identb[:]),
                             (pe_[:, 128:256], Eti[:, gl].rearrange("p i h -> p (i h)"), identb[:])])
                        cp('act', EST[:, gl].rearrange("p r x -> p (r x)"), pe_[:, 0:256])
                        if S5CUT <= 6:
                            continue
                        pt_ = psb(2)
                        lst = []
                        for g2 in range(2):
                            lo, hi = g2 * 64, g2 * 64 + 64
                            lst.append((pt_[:, g2 * 512:g2 * 512 + 128], te_r[lo:hi, gl].rearrange("p i h -> p (i h)"),
                                        Gtr[lo:hi, gl].rearrange("p i h -> p (i h)"), True, False, None))
                            lst.append((pt_[:, g2 * 512:g2 * 512 + 128], te_i[lo:hi, gl].rearrange("p i h -> p (i h)"),
                                        Gti[lo:hi, gl].rearrange("p i h -> p (i h)"), False, True, None))
                        mms(lst)
                        gp = gp0 + gl
                        msk = (maskf if d == 0 else maskb).unsqueeze(1).to_broadcast([128, 2, 128])
                        pv = pt_.rearrange("p (a x) -> p a x", a=2)[:, :, 0:128]
                        if d == 0:
                            tt('dve', TACC[:, 2 * gp:2 * gp + 2, :], pv, msk, ALU.mult)
                        else:
                            tmp = v1[:, 0:2].rearrange("p a i h -> p a (i h)")
                            tt('dve', tmp, pv, msk, ALU.mult)
                            tt('pool', TACC[:, 2 * gp:2 * gp + 2, :], TACC[:, 2 * gp:2 * gp + 2, :], tmp, ALU.add)
                            for g2 in range(2):
                                g = 2 * gp + g2
                                stt('pool', TST[:, g, :], identf, DV[:, g:g + 1], TACC[:, g, :], ALU.mult, ALU.add)
                    dma(s5w[gp0:gp0 + 8, :, d * 256:d * 256 + 256].rearrange("g p x -> p g x"),
                        EST.rearrange("p g r x -> p g (r x)"))
            dma(s5w[:, :, 1024:1280].rearrange("g p x -> p g x"), TST.rearrange("p (g a) x -> p g (a x)", a=2))


        if stop_after >= 0:
            s5_setup()

        ar.reset(0)
        BUFA = ar.bf(8192)
        BUFB = ar.bf(8192)
        gS = ar.bf(8192).rearrange("p (c n) -> p c n", c=4)
        gA = ar.bf(8192).rearrange("p (c n) -> p c n", c=4)
        zqT = ar.bf(4096).rearrange("p (c n) -> p c n", c=2)
        zkvT = ar.bf(2048)
        krT = ar.bf(2048)
        TMP0 = ar.off
        U_g = BUFA.rearrange("p (g m) -> p g m", g=32)
        geluT = BUFA.rearrange("p (c n) -> p c n", c=4)
        knT = BUFA.rearrange("p (h n) -> p h n", h=4)
        Ytok = BUFB.rearrange("p (mt i c) -> p mt i c", mt=2, i=8)
        Vt = BUFB.rearrange("p (kt c) -> p kt c", kt=16)
        statk = [0]

        def stat3():
            k = statk[0] % 20
            statk[0] += 1
            return stat[:, 3 * k:3 * k + 1], stat[:, 3 * k + 1:3 * k + 2], stat[:, 3 * k + 2:3 * k + 3]

        def fm_norm(src_list, nfeat, sq, sd, rs, pl=(2, 8)):
            n = len(src_list)
            for j, sap in enumerate(src_list):
                act(sq[:, j, :], sap, AF.Square)
            pq = psb(lo=pl[0], hi=pl[1])
            mms([(pq, onesb[:], sq[:, j, :], j == 0, j == n - 1, None) for j in range(n)])
            act(sd, pq, AF.Ln, scale=1.0 / nfeat, bias=epsc)
            act(rs, sd, AF.Exp, scale=-0.5)

        for s in range(nseq if stop_after >= 1 else 0):
            xv = x[s].rearrange("(mt m i) d -> mt i m d", mt=2, m=128, i=8)
            yv = y[s].rearrange("(mt m i) d -> mt i m d", mt=2, m=128, i=8)
            ar.reset(TMP0)
            posi = ar.i32(512)
            posf, rtu, ritf, rfr = [ar.f32(512) for _ in range(4)]
            riti = ar.i32(512)
            for cb in range(4):
                cols = slice(cb * 512, (cb + 1) * 512)
                dma(posi[0:64, :], pos[s:s + 1, cols].to_broadcast([64, 512]))
                cp('dve', posf[0:64, :], posi[0:64, :])
                ts('dve', rtu[0:64, :], posf[0:64, :], invf, ALU.mult, 1.0 / TWO_PI, ALU.mult)
                sincos(rtu[0:64, :], riti[0:64, :], ritf[0:64, :], rfr[0:64, :], sinT[:, cols], cosT[:, cols])

            ar.reset(TMP0)
            xt = [ar.f32(1024), ar.f32(1024)]
            hb = [ar.bf(1024) for _ in range(4)]
            hT = [ar.bf(4096).rearrange("p (c n) -> p c n", c=8) for _ in range(2)]
            utile = ar.bf(4096).rearrange("p (g i h) -> p g i h", g=32, i=8)
            sq1 = ar.bf(1024).rearrange("p (c n) -> p c n", c=2)
            sd1 = ar.f32(512)
            rs1 = ar.f32(512)
            rt1 = ar.bf(512)
            rt2 = ar.bf(512)
            memset('pool', krT[64:128, :], 0.0)
            tcnt = [0]

            def prepA(b, only=None):
                mt, q = divmod(b, 2)
                for ii in (range(4) if only is None else [only]):
                    i = 4 * q + ii
                    sl = tcnt[0] % 2
                    tcnt[0] += 1
                    dma(xt[sl], xv[mt, i])
                    ss, sdv, rstd = stat3()
                    act(hb[ii], xt[sl], AF.Square, accum=ss)
                    act(sdv, ss, AF.Sqrt, scale=1.0 / D, bias=epsc)
                    recip('dve', rstd, sdv)
                    ts('dve', hb[ii], xt[sl], rstd, ALU.mult)

            def prepB(b):
                hs = hT[b % 2]
                for ii in range(4):
                    pt = psb(lo=0, hi=2).bitcast(BF16)
                    trs([(pt[:, c * 128:(c + 1) * 128], hb[ii][:, c * 128:(c + 1) * 128], identb[:]) for c in range(8)])
                    cp('act', hs[:, :, ii * 128:(ii + 1) * 128], pt.rearrange("p (c m) -> p c m", c=8))

            def proj(b, nxt=None):
                mt, q = divmod(b, 2)

                def hook(k_):
                    if nxt is not None:
                        prepA(nxt, only=k_)

                hs = hT[b % 2]
                cols = slice(b * 512, (b + 1) * 512)

                def proj_fm(wcols, M=128, wsrc=None):
                    pp = psb(lo=2, hi=8)
                    if wsrc is None:
                        lst = [(pp[0:M, :], Win_b[:, c, wcols], hs[:, c, :], c == 0, c == 7, None) for c in range(8)]
                    else:
                        lst = [(pp[0:M, :], wsrc[:, c, :], hs[:, c, :], c == 0, c == 7, None) for c in range(8)]
                    mms(lst)
                    return pp

                pz = [proj_fm(slice(C_ZQ + j * 128, C_ZQ + (j + 1) * 128)) for j in range(2)]
                fm_norm(pz, 256, sq1, sd1, rs1)
                for j in range(2):
                    tt('dve', zqT[:, j, cols], pz[j], rs1, ALU.mult)
                hook(0)
                pk = proj_fm(slice(C_ZKV, C_ZKV + 128))
                fm_norm([pk], 128, sq1, sd1, rs1)
                tt('dve', zkvT[:, cols], pk, rs1, ALU.mult)
                hook(1)
                pr1 = proj_fm(slice(C_ZKR, C_ZKR + 64), M=64)
                pr2 = proj_fm(None, M=64, wsrc=Wkrrot)
                tt('dve', rt1[0:64, :], pr1[0:64, :], cosT[:, cols], ALU.mult)
                tt('dve', rt2[0:64, :], pr2[0:64, :], sinT[:, cols], ALU.mult)
                tt('dve', krT[0:64, cols], rt1[0:64, :], rt2[0:64, :], ALU.add)
                hook(2)
                for j in range(4):
                    pg = proj_fm(slice(C_GA + j * 128, C_GA + (j + 1) * 128))
                    act(gA[:, j, cols], pg, AF.Silu)
                hook(3)
                for j in range(4):
                    pg = proj_fm(slice(C_GS + j * 128, C_GS + (j + 1) * 128))
                    act(gS[:, j, cols], pg, AF.Silu)
                for ii in range(4):
                    i = 4 * q + ii
                    pu = psb(lo=2, hi=8)
                    mms([(pu, hs[:, c, ii * 128:(ii + 1) * 128], Win_b[:, c, C_U:C_U + 512], c == 0, c == 7, None) for c in range(8)])
                    cp('act', utile[:, :, i, :], pu.rearrange("p (g h) -> p g h", g=32))
                if q == 1:
                    for g4 in range(8):
                        pt = psb(lo=0, hi=2).bitcast(BF16)
                        trs([(pt[:, k_ * 128:(k_ + 1) * 128], utile[:, g4 * 4 + k_].rearrange("p i h -> p (i h)"), identb[:]) for k_ in range(4)])
                        cp('act', U_g[:, g4 * 4:(g4 + 1) * 4, mt * 128:(mt + 1) * 128],
                           pt[:, 0:512].rearrange("p (k m) -> p k m", k=4))

            prepA(0)
            prepB(0)
            for b in range(4):
                proj(b, nxt=(b + 1 if b + 1 < 4 else None))
                if b + 1 < 4:
                    prepB(b + 1)
            if s == 0:
                dump("U_g", U_g); dump("zqT", zqT); dump("zkvT", zkvT); dump("krT", krT[0:64, :]); dump("gA", gA); dump("gS", gS)
            if stop_after <= 1:
                continue

            ar.reset(TMP0)
            W5 = [ar.bf(1280) for _ in range(3)]
            TABm = [ar.f32(1024).rearrange("p (t x) -> p t x", t=2) for _ in range(2)]
            TABd = [ar.f32(1024).rearrange("p (t x) -> p t x", t=2) for _ in range(2)]
            tA = [ar.f32(512) for _ in range(2)]
            tB = [ar.f32(512) for _ in range(2)]
            a2 = [ar.f32(512) for _ in range(2)]
            b2 = [ar.f32(512) for _ in range(2)]
            Xp = [ar.f32(512) for _ in range(2)]
            rr = [ar.f32(512) for _ in range(2)]

            def h2(ap):
                return ap.rearrange("p (a x) -> p a x", a=2)

            def sw(ap):
                return h2(ap)[:, ::-1, :]

            def w5views(gp):
                w5 = W5[gp % 3]
                Ev = w5[:, 0:512].rearrange("p (d r x) -> p d r x", d=2, r=2)
                Gv = w5[:, 512:1024].rearrange("p (d r x) -> p d r x", d=2, r=2)
                Tv = w5[:, 1024:1280].rearrange("p (a x) -> p a x", a=2)
                return w5, Ev, Gv, Tv

            def st1(idx):
                gp, d = divmod(idx, 2)
                w5, Ev, Gv, Tv = w5views(gp)
                if d == 0:
                    dma(w5, s5w[gp])
                c = d * 16 + gp
                k2 = idx % 2
                tab = TABm[k2]
                dma(tab.rearrange("p t x -> p (t x)"), s5t[c, :, 0:1024])
                px = psb()
                lst = []
                for g2 in range(2):
                    for r_ in range(2):
                        lst.append((px[g2 * 64:(g2 + 1) * 64, r_ * 256:(r_ + 1) * 256], Ev[:, d, r_, g2 * 64:(g2 + 1) * 64],
                                    U_g[:, 2 * gp + g2, :], True, True, (0, 64 * g2) if g2 else None))
                mms(lst)
                tt('dve', h2(tA[k2]), h2(px), h2(tab[:, 0, :]), ALU.mult)
                tt('dve', h2(tB[k2]), sw(px), h2(tab[:, 1, :]), ALU.mult)

            def st2(idx):
                k2 = idx % 2
                tt('dve', Xp[k2], tA[k2], tB[k2], ALU.add)

            def st3(idx):
                gp, d = divmod(idx, 2)
                c = d * 16 + gp
                k2 = idx % 2
                rho = RHO8[:, c:c + 1].to_broadcast([128, 256])
                for r_ in range(2):
                    o_ = rr[k2][:, r_ * 256:(r_ + 1) * 256]
                    i_ = Xp[k2][:, r_ * 256:(r_ + 1) * 256]
                    if d == 1:
                        o_ = o_[:, ::-1]
                        i_ = i_[:, ::-1]
                    scan(o_, rho, i_)

            def st4(idx):
                gp, d = divmod(idx, 2)
                c = d * 16 + gp
                k2 = idx % 2
                tab = TABd[k2]
                dma(tab.rearrange("p t x -> p (t x)"), s5t[c, :, 0:1024])
                tt('dve', h2(a2[k2]), h2(rr[k2]), h2(tab[:, 0, :]), ALU.mult)
                tt('dve', h2(b2[k2]), sw(rr[k2]), h2(tab[:, 1, :]), ALU.mult)

            def st5(idx):
                gp, d = divmod(idx, 2)
                k2 = idx % 2
                xs = Xs[:, gp % 2]
                w5, Ev, Gv, Tv = w5views(gp)
                if d == 0:
                    tt('dve', xs[:, 0, :, 1:256], h2(a2[k2])[:, :, 0:255], h2(b2[k2])[:, :, 0:255], ALU.subtract)
                else:
                    tt('dve', xs[:, 1, :, 0:255], h2(a2[k2])[:, :, 1:256], h2(b2[k2])[:, :, 1:256], ALU.subtract)
                    for mt in range(2):
                        py = psb(2)
                        lst = []
                        for g2 in range(2):
                            oc = py[:, g2 * 512:g2 * 512 + 128]
                            lo, hi = g2 * 64, g2 * 64 + 64
                            lst.append((oc, U_g[:, 2 * gp + g2, mt * 128:(mt + 1) * 128], Tv[:, g2, :], True, False, None))
                            for d_ in range(2):
                                for r_ in range(2):
                                    lst.append((oc, xs[lo:hi, d_, r_, mt * 128:(mt + 1) * 128], Gv[lo:hi, d_, r_, :], False, (d_ == 1 and r_ == 1), None))
                        mms(lst)
                        act(Ytok[:, mt, :, gp * 32:(gp + 1) * 32].rearrange("p j (a h) -> p a j h", a=2),
                            py.rearrange("p (a x) -> p a x", a=2)[:, :, 0:128].rearrange("p a (j h) -> p a j h", j=8), AF.Gelu)

            stages = [st1, st2, st3, st4, st5]
            for t_ in range(32 + 4):
                for k_, f_ in enumerate(stages):
                    if 0 <= t_ - k_ < 32:
                        f_(t_ - k_)
            if s == 0:
                dump("Ytok", Ytok)
            if stop_after <= 2:
                continue

            ar.reset(TMP0)
            sg = ar.bf(2048).rearrange("p (c n) -> p c n", c=4)
            ssm2 = ar.bf(2048).rearrange("p (c n) -> p c n", c=4)
            sq2 = ar.bf(2048).rearrange("p (c n) -> p c n", c=4)
            tn2 = ar.bf(2048).rearrange("p (c n) -> p c n", c=4)
            sd2 = ar.f32(512)
            rs2 = ar.f32(512)
            for mt in range(2):
                for i in range(8):
                    pt = psb().bitcast(BF16)
                    trs([(pt[:, c * 128:(c + 1) * 128], Ytok[:, mt, i, c * 128:(c + 1) * 128], identb[:]) for c in range(4)])
                    c0 = mt * 1024 + i * 128
                    cp('act', geluT[:, :, c0:c0 + 128], pt[:, 0:512].rearrange("p (c m) -> p c m", c=4))
            for b in range(4):
                cols = slice(b * 512, (b + 1) * 512)
                for mo in range(4):
                    pg = psb()
                    mms([(pg, Wglu_b[:, kc, mo * 128:(mo + 1) * 128], geluT[:, kc, cols], kc == 0, kc == 3, None) for kc in range(4)])
                    act(sg[:, mo, :], pg, AF.Sigmoid, bias=bglu[:, mo:mo + 1])
                tt('dve', ssm2, geluT[:, :, cols], sg, ALU.mult)
                fm_norm([ssm2[:, j, :] for j in range(4)], 512, sq2, sd2, rs2)
                tt('dve', tn2, ssm2, rs2.unsqueeze(1).to_broadcast([128, 4, 512]), ALU.mult)
                tt('dve', gS[:, :, cols], tn2, gS[:, :, cols], ALU.mult)
            if s == 0:
                dump("ssm", gS)
            if stop_after <= 3:
                continue

            ar.reset(TMP0)
            qn = [ar.bf(2048).rearrange("p (h n) -> p h n", h=4) for _ in range(2)]
            qrT = [ar.bf(2048).rearrange("p (h n) -> p h n", h=4) for _ in range(2)]
            PT = [ar.bf(512) for _ in range(3)]
            rinv = [ar.f32(512) for _ in range(2)]
            attn = [ar.bf(2048).rearrange("p (h n) -> p h n", h=4) for _ in range(2)]
            sq3 = ar.bf(2048).rearrange("p (h n) -> p h n", h=4)
            tn3 = ar.bf(2048).rearrange("p (h n) -> p h n", h=4)
            sd3 = ar.f32(512)
            rs3 = ar.f32(512)
            qt1 = ar.f32(512)
            qt2 = ar.f32(512)
            wkv4 = Wukv_b[:].rearrange("p (h f) -> p h f", h=4)
            for h in range(4):
                for b in range(4):
                    cols = slice(b * 512, (b + 1) * 512)
                    pk = psb()
                    mms([(pk, wkv4[:, h, 0:128], zkvT[:, cols], True, True, None)])
                    cp('act', knT[:, h, cols], pk)
            for kt in range(16):
                pv = psb()
                mms([(pv, zkvT[:, kt * 128:(kt + 1) * 128], Wv_b[:], True, True, None)])
                cp('act', Vt[:, kt, :], pv)
            for q_ in qrT:
                memset('pool', q_[64:128], 0.0)

            def qproj(qb):
                cols = slice(qb * 512, (qb + 1) * 512)
                qn_ = qn[qb % 2]
                qr_ = qrT[qb % 2]
                for h in range(4):
                    pq = psb(lo=0, hi=4)
                    mms([(pq, Wuq_b[:, kc, h * 192:h * 192 + 128], zqT[:, kc, cols], kc == 0, kc == 1, None) for kc in range(2)])
                    cp('act', qn_[:, h, :], pq)
                    p1 = psb(lo=0, hi=4)
                    mms([(p1[0:64, :], Wuq_b[:, kc, h * 192 + 128:h * 192 + 192], zqT[:, kc, cols], kc == 0, kc == 1, None) for kc in range(2)])
                    p2 = psb(lo=0, hi=4)
                    mms([(p2[0:64, :], Wuqrot[:, kc, h * 64:(h + 1) * 64], zqT[:, kc, cols], kc == 0, kc == 1, None) for kc in range(2)])
                    tt('dve', qt1[0:64, :], p1[0:64, :], cosT[:, cols], ALU.mult)
                    tt('dve', qt2[0:64, :], p2[0:64, :], sinT[:, cols], ALU.mult)
                    tt('dve', qr_[0:64, h, :], qt1[0:64, :], qt2[0:64, :], ALU.add)

            def head(qb, h):
                qn_ = qn[qb % 2]
                qr_ = qrT[qb % 2]
                at_ = attn[qb % 2]
                po = PS[:, (4 + 2 * (h % 2)) * 512:(5 + 2 * (h % 2)) * 512]
                pr = PS[:, (5 + 2 * (h % 2)) * 512:(6 + 2 * (h % 2)) * 512]
                pend = None
                nkt = 16
                for kt in range(nkt + 1):
                    cur = None
                    if kt < nkt:
                        pst = psb(lo=0, hi=4)
                        ks = slice(kt * 128, (kt + 1) * 128)
                        mms([(pst, knT[:, h, ks], qn_[:, h, :], True, False, None),
                             (pst, krT[:, ks], qr_[:, h, :], False, True, None)])
                        ptile = PT[kt % 3]
                        act(ptile, pst, AF.Exp, scale=SCALE)
                        cur = (kt, ptile)
                    if pend is not None:
                        k0, p0 = pend
                        mms([(po, Vt[:, k0, h * 128:(h + 1) * 128], p0, k0 == 0, k0 == nkt - 1, None),
                             (pr, onesb[:], p0, k0 == 0, k0 == nkt - 1, None)])
                    pend = cur
                ri = rinv[h % 2]
                act(ri, pr, AF.Ln)
                act(ri, ri, AF.Exp, scale=-1.0)
                tt('dve', at_[:, h, :], po, ri, ALU.mult)

            def epilogue(qb):
                cols = slice(qb * 512, (qb + 1) * 512)
                at_ = attn[qb % 2]
                fm_norm([at_[:, j, :] for j in range(4)], 512, sq3, sd3, rs3, pl=(0, 4))
                tt('dve', tn3, at_, rs3.unsqueeze(1).to_broadcast([128, 4, 512]), ALU.mult)
                tt('dve', gA[:, :, cols], tn3, gA[:, :, cols], ALU.mult)

            qproj(0)
            for qb in range(4):
                if qb + 1 < 4:
                    qproj(qb + 1)
                for h in range(4):
                    head(qb, h)
                    if h == 0 and qb > 0:
                        epilogue(qb - 1)
            epilogue(3)
            if s == 0:
                dump("attn", gA)
            if stop_after <= 4:
                continue

            ar.reset(TMP0)
            xt4 = [ar.f32(1024), ar.f32(1024)]
            t4 = [ar.f32(1024), ar.f32(1024)]
            o4 = [ar.f32(1024), ar.f32(1024)]
            junk4 = ar.bf(1024)
            for mt in range(2):
                for i in range(8):
                    k_ = (mt * 8 + i) % 2
                    c0 = mt * 1024 + i * 128
                    dma(xt4[k_], xv[mt, i])
                    pm = psb(2)
                    lst = []
                    for nb in range(2):
                        for kc in range(8):
                            src = gA[:, kc, c0:c0 + 128] if kc < 4 else gS[:, kc - 4, c0:c0 + 128]
                            lst.append((pm[:, nb * 512:(nb + 1) * 512], src, Wout_b[:, kc, nb * 512:(nb + 1) * 512], kc == 0, kc == 7, None))
                    mms(lst)
                    ss, sdv, rstd = stat3()
                    ssb, _u1, _u2 = stat3()
                    act(junk4[:, 0:512], pm[:, 0:512], AF.Square, accum=ss)
                    act(junk4[:, 512:1024], pm[:, 512:1024], AF.Square, accum=ssb)
                    tt('dve', ss, ss, ssb, ALU.add)
                    act(sdv, ss, AF.Sqrt, scale=1.0 / D, bias=epsc)
                    recip('dve', rstd, sdv)
                    for nb in range(2):
                        stt('dve', t4[k_][:, nb * 512:(nb + 1) * 512], pm[:, nb * 512:(nb + 1) * 512], rstd, gpost_b[:, nb * 512:(nb + 1) * 512], ALU.mult, ALU.mult)
                    tt('dve', o4[k_], t4[k_], xt4[k_], ALU.add)
                    FINAL_OPS.append(dma(yv[mt, i], o4[k_]))

        Sd.analyze()
        Sd.emit(final_wait_ops=FINAL_OPS)
    return nc


def _consts():
    ident = np.eye(128, dtype=np.float32)
    ii = np.arange(128) // 16
    maskf = (ii[None, :] >= ii[:, None]).astype(np.float32)
    maskb = (ii[:, None] >= ii[None, :]).astype(np.float32)
    cst = np.stack([ident, maskf, maskb], axis=1).copy()
    kvals = np.tile(np.arange(-7, 9, dtype=np.float32)[None, :], (128, 1)).copy()
    mvals = np.tile(np.arange(256, dtype=np.float32)[None, :], (128, 1)).copy()
    invf = (np.float32(10000.0) ** (-np.arange(0, 64, 2, dtype=np.float32) / np.float32(64))).astype(np.float32)
    invf64 = np.concatenate([invf, invf])
    return cst, kvals, mvals, invf64


def _prep_shared(inp):
    f = np.float32
    cst, kvals, mvals, invf64 = _consts()
    smallv = np.zeros((128, 64), f)
    smallv[:, 0:8] = np.asarray(inp["pre_norm_g"], f).reshape(8, 128).T
    smallv[:, 8:10] = np.asarray(inp["q_norm_g"], f).reshape(2, 128).T
    smallv[:, 10:11] = np.asarray(inp["kv_norm_g"], f).reshape(1, 128).T
    gout = np.concatenate([np.asarray(inp["attn_out_g"], f).reshape(-1), np.asarray(inp["ssm_out_g"], f).reshape(-1)])
    smallv[:, 11:19] = gout.reshape(8, 128).T
    smallv[:, 19:23] = np.asarray(inp["b_glu"], f).reshape(4, 128).T
    smallv[0:64, 23] = invf64
    smallv[:, 24] = EPS

    def pm(a):
        a = np.asarray(a, f).reshape(2, 16, 2, 64)
        return a.transpose(2, 3, 0, 1).reshape(128, 32)

    logdt = np.asarray(inp["s5_log_dt"], f).reshape(2, 16, 2)
    logdt_t = np.broadcast_to(logdt.transpose(2, 0, 1)[:, None, :, :], (2, 64, 2, 16)).reshape(128, 32)
    dsgn = np.concatenate([np.ones((128, 16), f), -np.ones((128, 16), f)], axis=1)
    s5p = np.stack([pm(inp["s5_lam_re"][0]), pm(inp["s5_lam_im"][0]), logdt_t, dsgn], axis=1).astype(f).copy()

    def pb(a):
        a = np.asarray(a, f).reshape(2, 16, 2, 64, 16)
        return a.transpose(2, 3, 0, 1, 4).reshape(128, 512)

    def pc(a):
        a = np.asarray(a, f).reshape(2, 16, 2, 16, 64)
        return a.transpose(2, 4, 0, 1, 3).reshape(128, 512)

    s5bc = np.stack([pb(inp["s5_b_re"][0]), pb(inp["s5_b_im"][0]), pc(inp["s5_c_re"][0]), pc(inp["s5_c_im"][0])], axis=1).astype(f).copy()
    dsk = np.asarray(inp["s5_d"], f).reshape(32, 16)
    dvec = np.broadcast_to(dsk.T[None, :, :], (8, 16, 32)).reshape(128, 32).astype(f).copy()
    shared = {
        "w_in": np.ascontiguousarray(np.asarray(inp["w_in"], f)[0]),
        "w_uq": np.ascontiguousarray(np.asarray(inp["w_uq"], f)[0]),
        "w_ukv": np.ascontiguousarray(np.asarray(inp["w_ukv"], f)[0]),
        "w_glu": np.ascontiguousarray(np.asarray(inp["w_glu"], f)[0]),
        "w_out": np.ascontiguousarray(np.asarray(inp["w_out"], f)[0]),
        "smallv": smallv,
        "gpost": np.asarray(inp["post_norm_g"], f).reshape(1, D).copy(),
        "s5p": s5p, "s5bc": s5bc, "dvec": dvec, "cst": cst, "kvals": kvals, "mvals": mvals,
    }
    return shared


def _perm_pos(p):
    n = p.shape[0]
    return np.ascontiguousarray(p.reshape(n, 2, 128, 8).transpose(0, 1, 3, 2).reshape(n, S)).astype(np.int32)


_NC_CACHE = {}


def kernel(**inputs):
    x = np.asarray(inputs["x"], np.float32)
    positions = np.asarray(inputs["positions"], np.int32)
    B = x.shape[0]
    ncores = 8
    nseq = B // ncores
    shared = _prep_shared(inputs)
    if nseq not in _NC_CACHE:
        _NC_CACHE[nseq] = build(nseq)
    nc = _NC_CACHE[nseq]
    in_maps = []
    for c in range(ncores):
        m = dict(shared)
        m["x"] = np.ascontiguousarray(x[c * nseq:(c + 1) * nseq])
        m["pos"] = _perm_pos(positions[c * nseq:(c + 1) * nseq])
        in_maps.append(m)
    res = run_bass_kernel_spmd(nc, in_maps, core_ids=list(range(ncores)))
    out = np.concatenate([np.asarray(r["y"]) for r in res.results], axis=0)
    return out.astype(np.float32)
```

```python
import numpy as np
import concourse.bass as bass
import concourse.mybir as mybir

F32 = mybir.dt.float32
BF16 = mybir.dt.bfloat16
I32 = mybir.dt.int32
AF = mybir.ActivationFunctionType
ALU = mybir.AluOpType
AX = mybir.AxisListType

_DTSIZE = {}


def dtsize(dt):
    s = str(dt)
    if s in _DTSIZE:
        return _DTSIZE[s]
    if '32' in s:
        v = 4
    elif '16' in s:
        v = 2
    elif '64' in s:
        v = 8
    else:
        v = 1
    _DTSIZE[s] = v
    return v


GRAN = 16


def region_of(ap):
    t = ap.tensor
    name = t.name
    es = dtsize(ap.dtype)
    dims = ap.ap
    off = ap.offset
    space = str(ap.space)
    if 'DRAM' in space.upper() or 'HBM' in space.upper() or 'dram' in space.lower():
        lo = off
        hi = off
        for st, cnt in dims:
            if st >= 0:
                hi += st * (cnt - 1)
            else:
                lo += st * (cnt - 1)
        return (name, 'dram', 0, 1, lo * es, (hi + 1) * es)
    pstep, pcnt = dims[0]
    p0 = off // pstep if pstep > 0 else 0
    rem = off - p0 * pstep if pstep > 0 else off
    lo = rem
    hi = rem
    for st, cnt in dims[1:]:
        if st >= 0:
            hi += st * (cnt - 1)
        else:
            lo += st * (cnt - 1)
    return (name, 'sb', p0, p0 + pcnt, lo * es, (hi + 1) * es)


class Op:
    __slots__ = ('eng', 'fn', 'reads', 'writes', 'idx', 'deps', 'has_dep', 'ord', 'sem', 'semval', 'is_dma', 'waits', 'pre_wait')

    def __init__(self, eng, fn, reads, writes, is_dma):
        self.eng = eng
        self.fn = fn
        self.reads = reads
        self.writes = writes
        self.is_dma = is_dma
        self.deps = set()
        self.has_dep = False
        self.ord = None
        self.sem = None
        self.semval = None
        self.waits = []
        self.pre_wait = None


ENGS = ['pe', 'act', 'dve', 'pool', 'sp']


class Sched:
    def __init__(self, nc, n_dma_sems=12, sem_limit=4000):
        self.nc = nc
        self.ops = []
        self.n_dma_sems = n_dma_sems
        self.sem_limit = sem_limit
        self.track = {}

    def add(self, eng, fn, reads=(), writes=(), dma=False):
        rr = [region_of(a) if not isinstance(a, tuple) else a for a in reads]
        ww = [region_of(a) if not isinstance(a, tuple) else a for a in writes]
        op = Op(eng, fn, rr, ww, dma)
        op.idx = len(self.ops)
        self.ops.append(op)
        return op

    def _arr(self, name, b1):
        ng = (b1 + GRAN - 1) // GRAN + 1
        t = self.track.get(name)
        if t is None:
            t = {'w': np.full((4, ng), -1, np.int64), 'r': np.full((len(ENGS) + 1, 4, ng), -1, np.int64)}
            self.track[name] = t
        elif t['w'].shape[1] < ng:
            old = t['w'].shape[1]
            w = np.full((4, ng), -1, np.int64)
            w[:, :old] = t['w']
            r = np.full((len(ENGS) + 1, 4, ng), -1, np.int64)
            r[:, :, :old] = t['r']
            t['w'] = w
            t['r'] = r
        return t

    def analyze(self):
        ops = self.ops
        for op in ops:
            slot = len(ENGS) if op.is_dma else ENGS.index(op.eng)
            deps = set()
            for (name, sp, p0, p1, b0, b1) in op.reads:
                t = self._arr(name, b1)
                q0, q1 = p0 // 32, (p1 - 1) // 32 + 1
                g0, g1 = b0 // GRAN, (b1 - 1) // GRAN + 1
                w = t['w'][q0:q1, g0:g1]
                for d in np.unique(w):
                    if d >= 0:
                        deps.add(int(d))
            for (name, sp, p0, p1, b0, b1) in op.writes:
                t = self._arr(name, b1)
                q0, q1 = p0 // 32, (p1 - 1) // 32 + 1
                g0, g1 = b0 // GRAN, (b1 - 1) // GRAN + 1
                w = t['w'][q0:q1, g0:g1]
                for d in np.unique(w):
                    if d >= 0:
                        deps.add(int(d))
                r = t['r'][:, q0:q1, g0:g1]
                for d in np.unique(r):
                    if d >= 0:
                        deps.add(int(d))
            for (name, sp, p0, p1, b0, b1) in op.reads:
                t = self.track[name]
                q0, q1 = p0 // 32, (p1 - 1) // 32 + 1
                g0, g1 = b0 // GRAN, (b1 - 1) // GRAN + 1
                if op.is_dma:
                    prev = t['r'][slot, q0:q1, g0:g1]
                    for d in np.unique(prev):
                        if d >= 0:
                            deps.add(int(d))
                t['r'][slot, q0:q1, g0:g1] = op.idx
            for (name, sp, p0, p1, b0, b1) in op.writes:
                t = self.track[name]
                q0, q1 = p0 // 32, (p1 - 1) // 32 + 1
                g0, g1 = b0 // GRAN, (b1 - 1) // GRAN + 1
                t['w'][q0:q1, g0:g1] = op.idx
                t['r'][:, q0:q1, g0:g1] = -1
            deps.discard(op.idx)
            fdeps = set()
            for d in deps:
                o = ops[d]
                if (not o.is_dma) and (not op.is_dma) and o.eng == op.eng:
                    if op.eng == 'pe':
                        continue
                    raw = False
                    for (n1, _, p0, p1, b0, b1) in op.reads:
                        for (n2, _, P0, P1, B0, B1) in o.writes:
                            if n1 == n2 and p0 < P1 and P0 < p1 and b0 < B1 and B0 < b1:
                                raw = True
                    if not raw:
                        continue
                fdeps.add(d)
            op.deps = fdeps
            for d in fdeps:
                ops[d].has_dep = True

    def emit(self, final_wait_ops=()):
        nc = self.nc
        ops = self.ops
        for o in final_wait_ops:
            o.has_dep = True
        for o in ops:
            if o.is_dma:
                o.has_dep = True
        counts = {e: 0 for e in ENGS}
        for op in ops:
            if op.has_dep and not op.is_dma:
                counts[op.eng] += 1
        nsem_eng = {e: max(1, (counts[e] + self.sem_limit - 1) // self.sem_limit) for e in ENGS}
        import contextlib
        with contextlib.ExitStack() as es:
            eng_sems = {e: [es.enter_context(nc.semaphore(f"c_{e}_{i}")) for i in range(nsem_eng[e])] for e in ENGS}
            dma_sems = [es.enter_context(nc.semaphore(f"d_{i}")) for i in range(self.n_dma_sems)]
            cnt = {e: 0 for e in ENGS}
            dma_cnt = [0] * self.n_dma_sems
            dma_rr = 0
            for op in ops:
                if not op.has_dep:
                    continue
                if op.is_dma:
                    s = dma_rr % self.n_dma_sems
                    dma_rr += 1
                    if dma_cnt[s] > 0:
                        op.pre_wait = (dma_sems[s], 16 * dma_cnt[s])
                    dma_cnt[s] += 1
                    op.sem = dma_sems[s]
                    op.semval = 16 * dma_cnt[s]
                else:
                    k = cnt[op.eng]
                    cnt[op.eng] += 1
                    op.sem = eng_sems[op.eng][k // self.sem_limit]
                    op.semval = (k % self.sem_limit) + 1
            waited = {e: {} for e in ENGS}
            for op in ops:
                need = {}
                for d in op.deps:
                    o = ops[d]
                    key = id(o.sem)
                    if key not in need or need[key][1] < o.semval:
                        need[key] = (o.sem, o.semval)
                if op.pre_wait is not None:
                    key = id(op.pre_wait[0])
                    if key not in need or need[key][1] < op.pre_wait[1]:
                        need[key] = op.pre_wait
                wl = []
                for key, (s, v) in need.items():
                    if waited[op.eng].get(key, 0) >= v:
                        continue
                    waited[op.eng][key] = v
                    wl.append((s, v))
                op.waits = wl
            self.n_waits = sum(len(o.waits) for o in ops)
            print('SCHED ops', len(ops), 'incs', dict(cnt), 'dma', sum(dma_cnt), 'waits', self.n_waits, flush=True)
            finals = {}
            for o in final_wait_ops:
                finals[id(o.sem)] = (o.sem, max(o.semval, finals.get(id(o.sem), (None, 0))[1]))
            with nc.Block() as block:
                def run(engname, eng):
                    for op in ops:
                        if op.eng != engname:
                            continue
                        for (s, v) in op.waits:
                            eng.wait_ge(s, v)
                        ins = op.fn(eng)
                        if op.has_dep:
                            ins.then_inc(op.sem, 16 if op.is_dma else 1)
                    if engname == 'sp':
                        for (s, v) in finals.values():
                            eng.wait_ge(s, v)

                @block.tensor
                def _(eng):
                    run('pe', eng)

                @block.scalar
                def _(eng):
                    run('act', eng)

                @block.vector
                def _(eng):
                    run('dve', eng)

                @block.gpsimd
                def _(eng):
                    run('pool', eng)

                @block.sync
                def _(eng):
                    run('sp', eng)

import contextlib
import math
from concourse.bass_utils import run_bass_kernel_spmd

D = 1024
S = 2048
DIN = 1984
C_ZQ, C_ZKV, C_ZKR, C_GA, C_U, C_GS = 0, 256, 384, 448, 960, 1472
EPS = 1e-6
TWO_PI = 2.0 * math.pi
SCALE = 192.0 ** -0.5
NSEQ_CORE = 4


def build(nseq, stop_after=99, dbg=None):
    nc = bass.Bass("TRN2", target_bir_lowering=False)

    def din(name, shape, dt=F32):
        return nc.dram_tensor(name, shape, dt, kind="ExternalInput").ap()

    x = din("x", [nseq, S, D])
    pos = din("pos", [nseq, S], I32)
    w_in = din("w_in", [D, DIN])
    w_uq = din("w_uq", [256, 768])
    w_ukv = din("w_ukv", [128, 1024])
    w_glu = din("w_glu", [512, 512])
    w_out = din("w_out", [D, D])
    smallv = din("smallv", [128, 64])
    gpost = din("gpost", [1, D])
    s5p = din("s5p", [128, 4, 32])
    s5bc = din("s5bc", [128, 4, 512])
    dvec = din("dvec", [128, 32])
    cst = din("cst", [128, 3, 128])
    kvals = din("kvals", [128, 16])
    mvals = din("mvals", [128, 256])
    y = nc.dram_tensor("y", [nseq, S, D], F32, kind="ExternalOutput").ap()
    s5w = nc.dram_tensor("s5w", [16, 128, 1280], BF16, kind="Internal").ap()
    s5t = nc.dram_tensor("s5t", [32, 128, 1536], F32, kind="Internal").ap()

    es = contextlib.ExitStack()
    with es:
        def sb(name, shape, dt):
            return es.enter_context(nc.sbuf_tensor(name, shape, dt))

        Win_b = sb("Win_b", [128, 8, DIN], BF16)
        Wkrrot = sb("Wkrrot", [128, 8, 64], BF16)
        Wuq_b = sb("Wuq_b", [128, 2, 768], BF16)
        Wuqrot = sb("Wuqrot", [128, 2, 256], BF16)
        Wukv_b = sb("Wukv_b", [128, 1024], BF16)
        Wglu_b = sb("Wglu_b", [128, 4, 512], BF16)
        Wv_b = sb("Wv_b", [128, 512], BF16)
        Wout_b = sb("Wout_b", [128, 8, D], BF16)
        identb = sb("identb", [128, 128], BF16)
        onesb = sb("onesb", [128, 128], BF16)
        cstf = sb("cstf", [128, 3, 128], F32)
        gpost_b = sb("gpost_b", [128, D], F32)
        sv = sb("sv", [128, 64], F32)
        RHO8 = sb("RHO8", [128, 32], F32)
        cosT = sb("cosT", [64, S], BF16)
        sinT = sb("sinT", [64, S], BF16)
        stat = sb("stat", [128, 64], F32)
        Xs = sb("Xs", [128, 2, 2, 2, 256], BF16)
        BIG = sb("BIG", [128, 33024], F32)
        PS = es.enter_context(nc.psum_tensor("PS", [128, 4096], F32))

        identf = cstf[:, 0, :]
        maskf = cstf[:, 1, :]
        maskb = cstf[:, 2, :]
        gpre = sv[:, 0:8]
        gq = sv[:, 8:10]
        gkv = sv[:, 10:11]
        gout = sv[:, 11:19]
        bglu = sv[:, 19:23]
        invf = sv[0:64, 23:24]
        epsc = sv[:, 24:25]

        Sd = Sched(nc)
        A = Sd.add
        FINAL_OPS = []

        class Arena:
            def __init__(self):
                self.off = 0

            def reset(self, off=0):
                self.off = off

            def f32(self, n):
                a = BIG[:, self.off:self.off + n]
                self.off += n
                assert self.off <= 33024, self.off
                return a

            def bf(self, n):
                assert n % 2 == 0
                return self.f32(n // 2).bitcast(BF16)

            def i32(self, n):
                return self.f32(n).bitcast(I32)

        ar = Arena()
        ps_rr = {}

        def psb(n=1, lo=0, hi=8):
            k = ps_rr.get((lo, hi), lo)
            if k + n > hi:
                k = lo
            ps_rr[(lo, hi)] = k + n
            return PS[:, k * 512:(k + n) * 512]

        def scan(out, rho, in_):
            return A('dve', lambda e: e.tensor_tensor_scan(out=out, data0=rho, data1=in_, initial=0.0, op0=ALU.mult, op1=ALU.add),
                     [rho, in_], [out])

        DBG = {}

        def dump(name, ap):
            if dbg is None or name not in dbg:
                return
            t = nc.dram_tensor("dbg_" + name, list(ap.shape), ap.dtype, kind="ExternalOutput").ap()
            FINAL_OPS.append(dma(t, ap))

        def dma(out, in_, eng='sp'):
            return A(eng, lambda e: e.dma_start(out=out, in_=in_), [in_], [out], dma=True)

        def tt(eng, out, in0, in1, op):
            return A(eng, lambda e: e.tensor_tensor(out=out, in0=in0, in1=in1, op=op), [in0, in1], [out])

        def ts(eng, out, in0, s1, op0, s2=None, op1=None):
            rd = [in0] + [s for s in (s1, s2) if not isinstance(s, (int, float, type(None)))]
            if op1 is None:
                return A(eng, lambda e: e.tensor_scalar(out=out, in0=in0, scalar1=s1, scalar2=None, op0=op0), rd, [out])
            return A(eng, lambda e: e.tensor_scalar(out=out, in0=in0, scalar1=s1, scalar2=s2, op0=op0, op1=op1), rd, [out])

        def stt(eng, out, in0, sc, in1, op0, op1):
            rd = [in0, in1] + ([] if isinstance(sc, (int, float)) else [sc])
            return A('dve', lambda e: e.scalar_tensor_tensor(out=out, in0=in0, scalar=sc, in1=in1, op0=op0, op1=op1), rd, [out])

        def cp(eng, out, in_):
            if eng == 'act':
                return A('act', lambda e: e.copy(out=out, in_=in_), [in_], [out])
            return A(eng, lambda e: e.tensor_copy(out=out, in_=in_), [in_], [out])

        def act(out, in_, func, scale=1.0, bias=None, accum=None):
            rd = [in_] + ([] if bias is None else [bias]) + ([] if isinstance(scale, (int, float)) else [scale])
            wr = [out] + ([] if accum is None else [accum])
            kw = {}
            if bias is not None:
                kw['bias'] = bias
            if accum is not None:
                kw['accum_out'] = accum
            return A('act', lambda e: e.activation(out=out, in_=in_, func=func, scale=scale, **kw), rd, wr)

        def recip(eng, out, in_):
            return A('dve', lambda e: e.reciprocal(out=out, in_=in_), [in_], [out])

        def mms(lst, extra_reads=()):
            def fn(e):
                ins = None
                for (o, l, r, st, sp_, tp) in lst:
                    if tp is None:
                        ins = e.matmul(o, l, r, start=st, stop=sp_)
                    else:
                        ins = e.matmul(o, l, r, start=st, stop=sp_, tile_position=tp)
                return ins
            rd = []
            wr = []
            for (o, l, r, st, sp_, tp) in lst:
                rd += [l, r]
                wr.append(o)
            return A('pe', fn, rd + list(extra_reads), wr)

        def trs(lst):
            def fn(e):
                ins = None
                for (o, i_, idn) in lst:
                    ins = e.transpose(o, i_, idn)
                return ins
            rd = []
            wr = []
            for (o, i_, idn) in lst:
                rd += [i_, idn]
                wr.append(o)
            return A('pe', fn, rd, wr)

        def memset(eng, out, val):
            return A(eng, lambda e: e.memset(out, val), [], [out])

        dma(sv[:], smallv)
        dma(cstf[:], cst)
        dma(gpost_b[:], gpost.to_broadcast([128, D]))
        cp('dve', identb[:], identf)
        memset('pool', onesb[:], 1.0)
        memset('pool', Xs[:], 0.0)

        ar.reset()
        wst = [ar.f32(2048), ar.f32(2048)]
        engs3 = ['dve', 'pool', 'dve']

        def scale_cast(eng, out, in_, scal):
            if eng == 'act':
                return A('act', lambda e: e.activation(out=out, in_=in_, func=AF.Copy, scale=scal), [in_, scal], [out])
            return ts(eng, out, in_, scal, ALU.mult)

        k = 0
        for c in range(8):
            st_ = wst[k % 2]
            dma(st_[:, 0:DIN], w_in[c * 128:(c + 1) * 128, :])
            scale_cast(engs3[k % 3], Win_b[:, c, :], st_[:, 0:DIN], gpre[:, c:c + 1])
            k += 1
        for c in range(8):
            st_ = wst[k % 2]
            dma(st_[:, 0:D], w_out[c * 128:(c + 1) * 128, :])
            scale_cast(engs3[k % 3], Wout_b[:, c, :], st_[:, 0:D], gout[:, c:c + 1])
            k += 1
        for c in range(2):
            st_ = wst[k % 2]
            dma(st_[:, 0:768], w_uq[c * 128:(c + 1) * 128, :])
            scale_cast(engs3[k % 3], Wuq_b[:, c, :], st_[:, 0:768], gq[:, c:c + 1])
            k += 1
        st_ = wst[k % 2]
        dma(st_[:, 0:1024], w_ukv)
        scale_cast(engs3[k % 3], Wukv_b[:], st_[:, 0:1024], gkv)
        k += 1
        st_ = wst[k % 2]
        dma(st_[:].rearrange("p (c n) -> p c n", c=4), w_glu.rearrange("(c p) n -> p c n", p=128))
        cp(engs3[k % 3], Wglu_b[:].rearrange("p c n -> p (c n)"), st_[:])
        k += 1
        cp('pool', Wv_b[:].rearrange("p (h f) -> p h f", h=4), Wukv_b[:].rearrange("p (h f) -> p h f", h=4)[:, :, 128:256])
        ts('dve', Wkrrot[:, :, 0:32], Win_b[:, :, C_ZKR + 32:C_ZKR + 64], -1.0, ALU.mult)
        cp('dve', Wkrrot[:, :, 32:64], Win_b[:, :, C_ZKR:C_ZKR + 32])
        wq4 = Wuq_b[:].rearrange("p c (h f) -> p c h f", h=4)
        wr4 = Wuqrot[:].rearrange("p c (h f) -> p c h f", h=4)
        for c in range(2):
            ts('dve', wr4[:, c, :, 0:32], wq4[:, c, :, 160:192], -1.0, ALU.mult)
            cp('dve', wr4[:, c, :, 32:64], wq4[:, c, :, 128:160])

        def sincos(tu_ap, it_ap, itf_ap, fr_ap, s_out, c_out, s_scale=TWO_PI, eng='dve', s_out2=None):
            cp(eng, it_ap, tu_ap)
            cp(eng, itf_ap, it_ap)
            tt(eng, fr_ap, tu_ap, itf_ap, ALU.subtract)
            act(s_out, fr_ap, AF.Sin, scale=s_scale)
            if s_out2 is not None:
                act(s_out2, fr_ap, AF.Sin, scale=-s_scale)
            ts(eng, fr_ap, tu_ap, 0.25, ALU.add)
            cp(eng, it_ap, fr_ap)
            cp(eng, itf_ap, it_ap)
            tt(eng, fr_ap, fr_ap, itf_ap, ALU.subtract)
            act(c_out, fr_ap, AF.Sin, scale=TWO_PI)


        import os as _os
        S5CUT = int(_os.environ.get('S5CUT', '99'))

        def s5_setup():
            ar.reset(4096)
            P5 = ar.f32(128).rearrange("p (a c) -> p a c", a=4)
            BC = ar.f32(2048).rearrange("p (a c h) -> p a c h", a=4, c=32)
            KV = ar.f32(16)
            MV = ar.f32(256)
            DV = ar.f32(32)
            dma(P5, s5p)
            dma(BC.rearrange("p a c h -> p a (c h)"), s5bc)
            dma(KV, kvals)
            dma(MV, mvals)
            dma(DV, dvec)
            LAMRE, LAMIM, LOGDT, DSGN = P5[:, 0, :], P5[:, 1, :], P5[:, 2, :], P5[:, 3, :]
            BRE, BIM, CRE, CIM = BC[:, 0], BC[:, 1], BC[:, 2], BC[:, 3]

            def t32():
                return ar.f32(32)

            def t512():
                return ar.f32(512).rearrange("p (c k) -> p c k", c=32)

            lr, dt_, lrdt, th, den, rden, ca, cb, cr, ci, q1, q2, ff = [t32() for _ in range(13)]
            ts('dve', lr, LAMRE, -1e-4, ALU.min)
            act(dt_, LOGDT, AF.Exp)
            tt('dve', lrdt, lr, dt_, ALU.mult)
            tt('dve', th, LAMIM, dt_, ALU.mult)
            ang, lnm, mag, tu, itf, fr, sinv, cosv, PWr, PWi = [t512() for _ in range(10)]
            iti = ar.i32(512).rearrange("p (c k) -> p c k", c=32)
            thb = th.unsqueeze(2).to_broadcast([128, 32, 16])
            lrb = lrdt.unsqueeze(2).to_broadcast([128, 32, 16])
            kvb = KV.unsqueeze(1).to_broadcast([128, 32, 16])
            tt('dve', ang, thb, kvb, ALU.mult)
            tt('dve', lnm, lrb, kvb, ALU.mult)
            act(mag, lnm, AF.Exp)

            if S5CUT <= 1:
                return
            ts('dve', tu, ang, 1.0 / TWO_PI, ALU.mult)
            sincos(tu, iti, itf, fr, sinv, cosv)
            tt('dve', PWr, mag, cosv, ALU.mult)
            tt('dve', PWi, mag, sinv, ALU.mult)
            cp('dve', RHO8[:], mag[:, :, 15])
            if S5CUT <= 2:
                return
            ts('dve', ca, PWr[:, :, 8], -1.0, ALU.add)
            cp('dve', cb, PWi[:, :, 8])
            tt('dve', q1, lr, lr, ALU.mult)
            tt('dve', q2, LAMIM, LAMIM, ALU.mult)
            tt('dve', den, q1, q2, ALU.add)
            recip('dve', rden, den)
            tt('dve', q1, ca, lr, ALU.mult)
            tt('dve', q2, cb, LAMIM, ALU.mult)
            tt('dve', q1, q1, q2, ALU.add)
            tt('dve', cr, q1, rden, ALU.mult)
            tt('dve', q1, cb, lr, ALU.mult)
            tt('dve', q2, ca, LAMIM, ALU.mult)
            tt('dve', q1, q1, q2, ALU.subtract)
            tt('dve', ci, q1, rden, ALU.mult)
            Bbr, Bbi, w1, w2 = [t512() for _ in range(4)]
            crb = cr.unsqueeze(2).to_broadcast([128, 32, 16])
            cib = ci.unsqueeze(2).to_broadcast([128, 32, 16])
            tt('dve', w1, crb, BRE, ALU.mult)
            tt('dve', w2, cib, BIM, ALU.mult)
            tt('dve', Bbr, w1, w2, ALU.subtract)
            tt('dve', w1, crb, BIM, ALU.mult)
            tt('dve', w2, cib, BRE, ALU.mult)
            tt('dve', Bbi, w1, w2, ALU.add)
            ts('dve', ff, th, 8.0 / TWO_PI, ALU.mult)
            fi_ = ar.i32(32)
            fif = t32()
            cp('dve', fi_, ff)
            cp('dve', fif, fi_)
            tt('dve', ff, ff, fif, ALU.subtract)
            tt('dve', ff, ff, DSGN, ALU.mult)
            mark0 = ar.off

            if S5CUT <= 3:
                return
            TABST = ar.f32(4 * 1536).rearrange("p (c t x) -> p c t x", c=4, t=3)
            tq = ar.f32(1024).rearrange("p (c m) -> p c m", c=4)
            tqi = ar.i32(1024).rearrange("p (c m) -> p c m", c=4)
            tqf = ar.f32(1024).rearrange("p (c m) -> p c m", c=4)
            tqr = ar.f32(1024).rearrange("p (c m) -> p c m", c=4)
            mvb = MV.unsqueeze(1).to_broadcast([128, 4, 256])
            for bt in range(8):
                c0 = bt * 4
                eng = 'dve'
                fb = ff[:, c0:c0 + 4].unsqueeze(2).to_broadcast([128, 4, 256])
                tt(eng, tq, fb, mvb, ALU.mult)
                sincos(tq, tqi, tqf, tqr, TABST[:, :, 1, 0:256], TABST[:, :, 0, 0:256], eng=eng, s_out2=TABST[:, :, 1, 256:512])
                cp(eng, TABST[:, :, 0, 256:512], TABST[:, :, 0, 0:256])
                cp(eng, TABST[:, :, 2, 0:256], TABST[:, :, 1, 256:512])
                cp(eng, TABST[:, :, 2, 256:512], TABST[:, :, 1, 0:256])
                dma(s5t[c0:c0 + 4].rearrange("c p x -> p c x"), TABST.rearrange("p c t x -> p c (t x)"))

            if S5CUT <= 4:
                return
            ar.reset(mark0)
            TACC = ar.f32(32 * 128).rearrange("p (g x) -> p g x", g=32)
            TST = ar.bf(32 * 128).rearrange("p (g x) -> p g x", g=32)

            def q4():
                return ar.f32(8 * 128).rearrange("p (g i h) -> p g i h", g=8, i=8)

            def q4b():
                return ar.bf(8 * 128).rearrange("p (g i h) -> p g i h", g=8, i=8)

            Etr, Eti, Ttr, Tti, Gtr, Gti = [q4b() for _ in range(6)]
            v1, v2 = q4(), q4()
            EST = ar.bf(8 * 256).rearrange("p (g r x) -> p g r x", g=8, r=2)
            GST = ar.bf(8 * 256).rearrange("p (g r x) -> p g r x", g=8, r=2)

            def bc_pw(pw, c0, sl):
                return pw[:, c0:c0 + 8, sl].unsqueeze(3).to_broadcast([128, 8, 8, 16])

            def bc_x(xx, c0):
                return xx[:, c0:c0 + 8, :].unsqueeze(2).to_broadcast([128, 8, 8, 16])

            def cmul(eng, outr, outi, pr_, pi_, xr_, xi_, neg_im=False):
                tt(eng, v1, pr_, xr_, ALU.mult)
                tt(eng, v2, pi_, xi_, ALU.mult)
                tt(eng, outr, v1, v2, ALU.subtract)
                tt(eng, v1, pr_, xi_, ALU.mult)
                tt(eng, v2, pi_, xr_, ALU.mult)
                if neg_im:
                    stt(eng, outi, v1, -1.0, v2, ALU.mult, ALU.subtract)
                else:
                    tt(eng, outi, v1, v2, ALU.add)

            SL_E = [slice(14, 6, -1), slice(7, 15)]
            SL_G = [slice(8, 16), slice(15, 7, -1)]
            SL_TE = [slice(7, None, -1), slice(7, 15)]
            SL_TG = [slice(7, 15), slice(7, None, -1)]
            for d in range(2):
                for hf in range(2):
                    c0 = d * 16 + hf * 8
                    gp0 = hf * 8
                    eng = 'dve'
                    cmul(eng, Etr, Eti, bc_pw(PWr, c0, SL_E[d]), bc_pw(PWi, c0, SL_E[d]), bc_x(Bbr, c0), bc_x(Bbi, c0))
                    gr_v = GST[:, :, 0, :].rearrange("p g (j h) -> p g j h", j=8)
                    gi_v = GST[:, :, 1, :].rearrange("p g (j h) -> p g j h", j=8)
                    cmul(eng, gr_v, gi_v, bc_pw(PWr, c0, SL_G[d]), bc_pw(PWi, c0, SL_G[d]), bc_x(CRE, c0), bc_x(CIM, c0), neg_im=True)
                    dma(s5w[gp0:gp0 + 8, :, 512 + d * 256:512 + d * 256 + 256].rearrange("g p x -> p g x"),
                        GST.rearrange("p g r x -> p g (r x)"))
                    if d == 0:
                        cmul(eng, Ttr, Tti, bc_pw(PWr, c0, SL_TE[d]), bc_pw(PWi, c0, SL_TE[d]), bc_x(Bbr, c0), bc_x(Bbi, c0))
                        te_r, te_i = Ttr, Tti
                    else:
                        te_r, te_i = Etr, Eti
                    cmul(eng, Gtr, Gti, bc_pw(PWr, c0, SL_TG[d]), bc_pw(PWi, c0, SL_TG[d]), bc_x(CRE, c0), bc_x(CIM, c0), neg_im=True)
                    for gl in range(8 if S5CUT > 5 else 0):
                        pe_ = psb().bitcast(BF16)
                        trs([(pe_[:, 0:128], Etr[:, gl].rearrange("p i h -> p (i h)"), identb[:]),
                             (pe_[:, 128:256], Eti[:, gl].rearrange("p i h -> p (i h)"), identb[:])])
                        cp('act', EST[:, gl].rearrange("p r x -> p (r x)"), pe_[:, 0:256])
                        if S5CUT <= 6:
                            continue
                        pt_ = psb(2)
                        lst = []
                        for g2 in range(2):
                            lo, hi = g2 * 64, g2 * 64 + 64
                            lst.append((pt_[:, g2 * 512:g2 * 512 + 128], te_r[lo:hi, gl].rearrange("p i h -> p (i h)"),
                                        Gtr[lo:hi, gl].rearrange("p i h -> p (i h)"), True, False, None))
                            lst.append((pt_[:, g2 * 512:g2 * 512 + 128], te_i[lo:hi, gl].rearrange("p i h -> p (i h)"),
                                        Gti[lo:hi, gl].rearrange("p i h -> p (i h)"), False, True, None))
                        mms(lst)
                        gp = gp0 + gl
                        msk = (maskf if d == 0 else maskb).unsqueeze(1).to_broadcast([128, 2, 128])
                        pv = pt_.rearrange("p (a x) -> p a x", a=2)[:, :, 0:128]
                        if d == 0:
                            tt('dve', TACC[:, 2 * gp:2 * gp + 2, :], pv, msk, ALU.mult)
                        else:
                            tmp = v1[:, 0:2].rearrange("p a i h -> p a (i h)")
                            tt('dve', tmp, pv, msk, ALU.mult)
                            tt('pool', TACC[:, 2 * gp:2 * gp + 2, :], TACC[:, 2 * gp:2 * gp + 2, :], tmp, ALU.add)
                            for g2 in range(2):
                                g = 2 * gp + g2
                                stt('pool', TST[:, g, :], identf, DV[:, g:g + 1], TACC[:, g, :], ALU.mult, ALU.add)
                    dma(s5w[gp0:gp0 + 8, :, d * 256:d * 256 + 256].rearrange("g p x -> p g x"),
                        EST.rearrange("p g r x -> p g (r x)"))
            dma(s5w[:, :, 1024:1280].rearrange("g p x -> p g x"), TST.rearrange("p (g a) x -> p g (a x)", a=2))


        if stop_after >= 0:
            s5_setup()

        ar.reset(0)
        BUFA = ar.bf(8192)
        BUFB = ar.bf(8192)
        gS = ar.bf(8192).rearrange("p (c n) -> p c n", c=4)
        gA = ar.bf(8192).rearrange("p (c n) -> p c n", c=4)
        zqT = ar.bf(4096).rearrange("p (c n) -> p c n", c=2)
        zkvT = ar.bf(2048)
        krT = ar.bf(2048)
        TMP0 = ar.off
        U_g = BUFA.rearrange("p (g m) -> p g m", g=32)
        geluT = BUFA.rearrange("p (c n) -> p c n", c=4)
        knT = BUFA.rearrange("p (h n) -> p h n", h=4)
        Ytok = BUFB.rearrange("p (mt i c) -> p mt i c", mt=2, i=8)
        Vt = BUFB.rearrange("p (kt c) -> p kt c", kt=16)
        statk = [0]

        def stat3():
            k = statk[0] % 20
            statk[0] += 1
            return stat[:, 3 * k:3 * k + 1], stat[:, 3 * k + 1:3 * k + 2], stat[:, 3 * k + 2:3 * k + 3]

        def fm_norm(src_list, nfeat, sq, sd, rs, pl=(2, 8)):
            n = len(src_list)
            for j, sap in enumerate(src_list):
                act(sq[:, j, :], sap, AF.Square)
            pq = psb(lo=pl[0], hi=pl[1])
            mms([(pq, onesb[:], sq[:, j, :], j == 0, j == n - 1, None) for j in range(n)])
            act(sd, pq, AF.Ln, scale=1.0 / nfeat, bias=epsc)
            act(rs, sd, AF.Exp, scale=-0.5)

        for s in range(nseq if stop_after >= 1 else 0):
            xv = x[s].rearrange("(mt m i) d -> mt i m d", mt=2, m=128, i=8)
            yv = y[s].rearrange("(mt m i) d -> mt i m d", mt=2, m=128, i=8)
            ar.reset(TMP0)
            posi = ar.i32(512)
            posf, rtu, ritf, rfr = [ar.f32(512) for _ in range(4)]
            riti = ar.i32(512)
            for cb in range(4):
                cols = slice(cb * 512, (cb + 1) * 512)
                dma(posi[0:64, :], pos[s:s + 1, cols].to_broadcast([64, 512]))
                cp('dve', posf[0:64, :], posi[0:64, :])
                ts('dve', rtu[0:64, :], posf[0:64, :], invf, ALU.mult, 1.0 / TWO_PI, ALU.mult)
                sincos(rtu[0:64, :], riti[0:64, :], ritf[0:64, :], rfr[0:64, :], sinT[:, cols], cosT[:, cols])

            ar.reset(TMP0)
            xt = [ar.f32(1024), ar.f32(1024)]
            hb = [ar.bf(1024) for _ in range(4)]
            hT = [ar.bf(4096).rearrange("p (c n) -> p c n", c=8) for _ in range(2)]
            utile = ar.bf(4096).rearrange("p (g i h) -> p g i h", g=32, i=8)
            sq1 = ar.bf(1024).rearrange("p (c n) -> p c n", c=2)
            sd1 = ar.f32(512)
            rs1 = ar.f32(512)
            rt1 = ar.bf(512)
            rt2 = ar.bf(512)
            memset('pool', krT[64:128, :], 0.0)
            tcnt = [0]

            def prepA(b, only=None):
                mt, q = divmod(b, 2)
                for ii in (range(4) if only is None else [only]):
                    i = 4 * q + ii
                    sl = tcnt[0] % 2
                    tcnt[0] += 1
                    dma(xt[sl], xv[mt, i])
                    ss, sdv, rstd = stat3()
                    act(hb[ii], xt[sl], AF.Square, accum=ss)
                    act(sdv, ss, AF.Sqrt, scale=1.0 / D, bias=epsc)
                    recip('dve', rstd, sdv)
                    ts('dve', hb[ii], xt[sl], rstd, ALU.mult)

            def prepB(b):
                hs = hT[b % 2]
                for ii in range(4):
                    pt = psb(lo=0, hi=2).bitcast(BF16)
                    trs([(pt[:, c * 128:(c + 1) * 128], hb[ii][:, c * 128:(c + 1) * 128], identb[:]) for c in range(8)])
                    cp('act', hs[:, :, ii * 128:(ii + 1) * 128], pt.rearrange("p (c m) -> p c m", c=8))

            def proj(b, nxt=None):
                mt, q = divmod(b, 2)

                def hook(k_):
                    if nxt is not None:
                        prepA(nxt, only=k_)

                hs = hT[b % 2]
                cols = slice(b * 512, (b + 1) * 512)

                def proj_fm(wcols, M=128, wsrc=None):
                    pp = psb(lo=2, hi=8)
                    if wsrc is None:
                        lst = [(pp[0:M, :], Win_b[:, c, wcols], hs[:, c, :], c == 0, c == 7, None) for c in range(8)]
                    else:
                        lst = [(pp[0:M, :], wsrc[:, c, :], hs[:, c, :], c == 0, c == 7, None) for c in range(8)]
                    mms(lst)
                    return pp

                pz = [proj_fm(slice(C_ZQ + j * 128, C_ZQ + (j + 1) * 128)) for j in range(2)]
                fm_norm(pz, 256, sq1, sd1, rs1)
                for j in range(2):
                    tt('dve', zqT[:, j, cols], pz[j], rs1, ALU.mult)
                hook(0)
                pk = proj_fm(slice(C_ZKV, C_ZKV + 128))
                fm_norm([pk], 128, sq1, sd1, rs1)
                tt('dve', zkvT[:, cols], pk, rs1, ALU.mult)
                hook(1)
                pr1 = proj_fm(slice(C_ZKR, C_ZKR + 64), M=64)
                pr2 = proj_fm(None, M=64, wsrc=Wkrrot)
                tt('dve', rt1[0:64, :], pr1[0:64, :], cosT[:, cols], ALU.mult)
                tt('dve', rt2[0:64, :], pr2[0:64, :], sinT[:, cols], ALU.mult)
                tt('dve', krT[0:64, cols], rt1[0:64, :], rt2[0:64, :], ALU.add)
                hook(2)
                for j in range(4):
                    pg = proj_fm(slice(C_GA + j * 128, C_GA + (j + 1) * 128))
                    act(gA[:, j, cols], pg, AF.Silu)
                hook(3)
                for j in range(4):
                    pg = proj_fm(slice(C_GS + j * 128, C_GS + (j + 1) * 128))
                    act(gS[:, j, cols], pg, AF.Silu)
                for ii in range(4):
                    i = 4 * q + ii
                    pu = psb(lo=2, hi=8)
                    mms([(pu, hs[:, c, ii * 128:(ii + 1) * 128], Win_b[:, c, C_U:C_U + 512], c == 0, c == 7, None) for c in range(8)])
                    cp('act', utile[:, :, i, :], pu.rearrange("p (g h) -> p g h", g=32))
                if q == 1:
                    for g4 in range(8):
                        pt = psb(lo=0, hi=2).bitcast(BF16)
                        trs([(pt[:, k_ * 128:(k_ + 1) * 128], utile[:, g4 * 4 + k_].rearrange("p i h -> p (i h)"), identb[:]) for k_ in range(4)])
                        cp('act', U_g[:, g4 * 4:(g4 + 1) * 4, mt * 128:(mt + 1) * 128],
                           pt[:, 0:512].rearrange("p (k m) -> p k m", k=4))

            prepA(0)
            prepB(0)
            for b in range(4):
                proj(b, nxt=(b + 1 if b + 1 < 4 else None))
                if b + 1 < 4:
                    prepB(b + 1)
            if s == 0:
                dump("U_g", U_g); dump("zqT", zqT); dump("zkvT", zkvT); dump("krT", krT[0:64, :]); dump("gA", gA); dump("gS", gS)
            if stop_after <= 1:
                continue

            ar.reset(TMP0)
            W5 = [ar.bf(1280) for _ in range(4)]
            TAB = [ar.f32(768) for _ in range(5)]
            tA = [ar.f32(512) for _ in range(2)]
            tB = [ar.f32(512) for _ in range(2)]
            a2 = [ar.f32(512) for _ in range(2)]
            b2 = [ar.f32(512) for _ in range(2)]
            Xp = [ar.f32(512) for _ in range(2)]
            rr = [ar.f32(512) for _ in range(2)]

            def h2(ap):
                return ap.rearrange("p (a x) -> p a x", a=2)

            def sw(ap):
                return h2(ap)[:, ::-1, :]

            def w5views(gp):
                w5 = W5[gp % 4]
                Ev = w5[:, 0:512].rearrange("p (d r x) -> p d r x", d=2, r=2)
                Gv = w5[:, 512:1024].rearrange("p (d r x) -> p d r x", d=2, r=2)
                Tv = w5[:, 1024:1280].rearrange("p (a x) -> p a x", a=2)
                return w5, Ev, Gv, Tv

            def st1(idx):
                gp, d = divmod(idx, 2)
                w5, Ev, Gv, Tv = w5views(gp)
                c = d * 16 + gp
                k2 = idx % 2
                tab = TAB[idx % 5]
                px = psb()
                lst = []
                for g2 in range(2):
                    for r_ in range(2):
                        lst.append((px[g2 * 64:(g2 + 1) * 64, r_ * 256:(r_ + 1) * 256], Ev[:, d, r_, g2 * 64:(g2 + 1) * 64],
                                    U_g[:, 2 * gp + g2, :], True, True, (0, 64 * g2) if g2 else None))
                mms(lst)
                tt('dve', h2(tA[k2]), h2(px), tab[:, 0:256].unsqueeze(1).to_broadcast([128, 2, 256]), ALU.mult)
                tt('dve', h2(tB[k2]), sw(px), h2(tab[:, 256:768]), ALU.mult)

            def st2(idx):
                k2 = idx % 2
                tt('dve', Xp[k2], tA[k2], tB[k2], ALU.add)

            def st3(idx):
                gp, d = divmod(idx, 2)
                c = d * 16 + gp
                k2 = idx % 2
                rho = RHO8[:, c:c + 1].to_broadcast([128, 256])
                for r_ in range(2):
                    o_ = rr[k2][:, r_ * 256:(r_ + 1) * 256]
                    i_ = Xp[k2][:, r_ * 256:(r_ + 1) * 256]
                    if d == 1:
                        o_ = o_[:, ::-1]
                        i_ = i_[:, ::-1]
                    scan(o_, rho, i_)

            def st4(idx):
                gp, d = divmod(idx, 2)
                c = d * 16 + gp
                k2 = idx % 2
                tab = TAB[idx % 5]
                tt('dve', h2(a2[k2]), h2(rr[k2]), tab[:, 0:256].unsqueeze(1).to_broadcast([128, 2, 256]), ALU.mult)
                tt('dve', h2(b2[k2]), sw(rr[k2]), h2(tab[:, 256:768]), ALU.mult)

            def st5(idx):
                gp, d = divmod(idx, 2)
                k2 = idx % 2
                xs = Xs[:, gp % 2]
                w5, Ev, Gv, Tv = w5views(gp)
                if d == 0:
                    tt('dve', xs[:, 0, :, 1:256], h2(a2[k2])[:, :, 0:255], h2(b2[k2])[:, :, 0:255], ALU.subtract)
                else:
                    tt('dve', xs[:, 1, :, 0:255], h2(a2[k2])[:, :, 1:256], h2(b2[k2])[:, :, 1:256], ALU.subtract)
                    for mt in range(2):
                        py = psb(2)
                        lst = []
                        for g2 in range(2):
                            oc = py[:, g2 * 512:g2 * 512 + 128]
                            lo, hi = g2 * 64, g2 * 64 + 64
                            lst.append((oc, U_g[:, 2 * gp + g2, mt * 128:(mt + 1) * 128], Tv[:, g2, :], True, False, None))
                            for d_ in range(2):
                                for r_ in range(2):
                                    lst.append((oc, xs[lo:hi, d_, r_, mt * 128:(mt + 1) * 128], Gv[lo:hi, d_, r_, :], False, (d_ == 1 and r_ == 1), None))
                        mms(lst)
                        act(Ytok[:, mt, :, gp * 32:(gp + 1) * 32].rearrange("p j (a h) -> p a j h", a=2),
                            py.rearrange("p (a x) -> p a x", a=2)[:, :, 0:128].rearrange("p a (j h) -> p a j h", j=8), AF.Gelu)

            def prefetch(idx):
                if idx >= 32:
                    return
                gp, d = divmod(idx, 2)
                if d == 0:
                    dma(W5[gp % 4], s5w[gp])
                dma(TAB[idx % 5], s5t[d * 16 + gp, :, 256:1024])

            prefetch(0)
            prefetch(1)
            stages = [st1, st2, st3, st4, st5]
            for t_ in range(32 + 4):
                for k_, f_ in enumerate(stages):
                    if 0 <= t_ - k_ < 32:
                        f_(t_ - k_)
                prefetch(t_ + 2)
            if s == 0:
                dump("Ytok", Ytok)
            if stop_after <= 2:
                continue

            ar.reset(TMP0)
            sg = ar.bf(2048).rearrange("p (c n) -> p c n", c=4)
            ssm2 = ar.bf(2048).rearrange("p (c n) -> p c n", c=4)
            sq2 = ar.bf(2048).rearrange("p (c n) -> p c n", c=4)
            tn2 = ar.bf(2048).rearrange("p (c n) -> p c n", c=4)
            sd2 = ar.f32(512)
            rs2 = ar.f32(512)
            for mt in range(2):
                for i in range(8):
                    pt = psb().bitcast(BF16)
                    trs([(pt[:, c * 128:(c + 1) * 128], Ytok[:, mt, i, c * 128:(c + 1) * 128], identb[:]) for c in range(4)])
                    c0 = mt * 1024 + i * 128
                    cp('act', geluT[:, :, c0:c0 + 128], pt[:, 0:512].rearrange("p (c m) -> p c m", c=4))
            for b in range(4):
                cols = slice(b * 512, (b + 1) * 512)
                for mo in range(4):
                    pg = psb()
                    mms([(pg, Wglu_b[:, kc, mo * 128:(mo + 1) * 128], geluT[:, kc, cols], kc == 0, kc == 3, None) for kc in range(4)])
                    act(sg[:, mo, :], pg, AF.Sigmoid, bias=bglu[:, mo:mo + 1])
                tt('dve', ssm2, geluT[:, :, cols], sg, ALU.mult)
                fm_norm([ssm2[:, j, :] for j in range(4)], 512, sq2, sd2, rs2)
                tt('dve', tn2, ssm2, rs2.unsqueeze(1).to_broadcast([128, 4, 512]), ALU.mult)
                tt('dve', gS[:, :, cols], tn2, gS[:, :, cols], ALU.mult)
            if s == 0:
                dump("ssm", gS)
            if stop_after <= 3:
                continue

            ar.reset(TMP0)
            qn = [ar.bf(2048).rearrange("p (h n) -> p h n", h=4) for _ in range(2)]
            qrT = [ar.bf(2048).rearrange("p (h n) -> p h n", h=4) for _ in range(2)]
            PT = [ar.bf(512) for _ in range(3)]
            rinv = [ar.f32(512) for _ in range(2)]
            attn = [ar.bf(2048).rearrange("p (h n) -> p h n", h=4) for _ in range(2)]
            sq3 = ar.bf(2048).rearrange("p (h n) -> p h n", h=4)
            tn3 = ar.bf(2048).rearrange("p (h n) -> p h n", h=4)
            sd3 = ar.f32(512)
            rs3 = ar.f32(512)
            qt1 = ar.f32(512)
            qt2 = ar.f32(512)
            wkv4 = Wukv_b[:].rearrange("p (h f) -> p h f", h=4)
            for h in range(4):
                for b in range(4):
                    cols = slice(b * 512, (b + 1) * 512)
                    pk = psb()
                    mms([(pk, wkv4[:, h, 0:128], zkvT[:, cols], True, True, None)])
                    cp('act', knT[:, h, cols], pk)
            for kt in range(16):
                pv = psb()
                mms([(pv, zkvT[:, kt * 128:(kt + 1) * 128], Wv_b[:], True, True, None)])
                cp('act', Vt[:, kt, :], pv)
            for q_ in qrT:
                memset('pool', q_[64:128], 0.0)

            def qproj(qb):
                cols = slice(qb * 512, (qb + 1) * 512)
                qn_ = qn[qb % 2]
                qr_ = qrT[qb % 2]
                for h in range(4):
                    pq = psb(lo=0, hi=4)
                    mms([(pq, Wuq_b[:, kc, h * 192:h * 192 + 128], zqT[:, kc, cols], kc == 0, kc == 1, None) for kc in range(2)])
                    cp('act', qn_[:, h, :], pq)
                    p1 = psb(lo=0, hi=4)
                    mms([(p1[0:64, :], Wuq_b[:, kc, h * 192 + 128:h * 192 + 192], zqT[:, kc, cols], kc == 0, kc == 1, None) for kc in range(2)])
                    p2 = psb(lo=0, hi=4)
                    mms([(p2[0:64, :], Wuqrot[:, kc, h * 64:(h + 1) * 64], zqT[:, kc, cols], kc == 0, kc == 1, None) for kc in range(2)])
                    tt('dve', qt1[0:64, :], p1[0:64, :], cosT[:, cols], ALU.mult)
                    tt('dve', qt2[0:64, :], p2[0:64, :], sinT[:, cols], ALU.mult)
                    tt('dve', qr_[0:64, h, :], qt1[0:64, :], qt2[0:64, :], ALU.add)

            def head(qb, h):
                qn_ = qn[qb % 2]
                qr_ = qrT[qb % 2]
                at_ = attn[qb % 2]
                po = PS[:, (4 + 2 * (h % 2)) * 512:(5 + 2 * (h % 2)) * 512]
                pr = PS[:, (5 + 2 * (h % 2)) * 512:(6 + 2 * (h % 2)) * 512]
                pend = None
                nkt = 16
                for kt in range(nkt + 1):
                    cur = None
                    if kt < nkt:
                        pst = psb(lo=0, hi=4)
                        ks = slice(kt * 128, (kt + 1) * 128)
                        mms([(pst, knT[:, h, ks], qn_[:, h, :], True, False, None),
                             (pst, krT[:, ks], qr_[:, h, :], False, True, None)])
                        ptile = PT[kt % 3]
                        act(ptile, pst, AF.Exp, scale=SCALE)
                        cur = (kt, ptile)
                    if pend is not None:
                        k0, p0 = pend
                        mms([(po, Vt[:, k0, h * 128:(h + 1) * 128], p0, k0 == 0, k0 == nkt - 1, None),
                             (pr, onesb[:], p0, k0 == 0, k0 == nkt - 1, None)])
                    pend = cur
                ri = rinv[h % 2]
                act(ri, pr, AF.Ln)
                act(ri, ri, AF.Exp, scale=-1.0)
                tt('dve', at_[:, h, :], po, ri, ALU.mult)

            def epilogue(qb):
                cols = slice(qb * 512, (qb + 1) * 512)
                at_ = attn[qb % 2]
                fm_norm([at_[:, j, :] for j in range(4)], 512, sq3, sd3, rs3, pl=(0, 4))
                tt('dve', tn3, at_, rs3.unsqueeze(1).to_broadcast([128, 4, 512]), ALU.mult)
                tt('dve', gA[:, :, cols], tn3, gA[:, :, cols], ALU.mult)

            qproj(0)
            for qb in range(4):
                if qb + 1 < 4:
                    qproj(qb + 1)
                for h in range(4):
                    head(qb, h)
                    if h == 0 and qb > 0:
                        epilogue(qb - 1)
            epilogue(3)
            if s == 0:
                dump("attn", gA)
            if stop_after <= 4:
                continue

            ar.reset(TMP0)
            xt4 = [ar.f32(1024), ar.f32(1024)]
            t4 = [ar.f32(1024), ar.f32(1024)]
            o4 = [ar.f32(1024), ar.f32(1024)]
            junk4 = ar.bf(1024)
            for mt in range(2):
                for i in range(8):
                    k_ = (mt * 8 + i) % 2
                    c0 = mt * 1024 + i * 128
                    dma(xt4[k_], xv[mt, i])
                    pm = psb(2)
                    lst = []
                    for nb in range(2):
                        for kc in range(8):
                            src = gA[:, kc, c0:c0 + 128] if kc < 4 else gS[:, kc - 4, c0:c0 + 128]
                            lst.append((pm[:, nb * 512:(nb + 1) * 512], src, Wout_b[:, kc, nb * 512:(nb + 1) * 512], kc == 0, kc == 7, None))
                    mms(lst)
                    ss, sdv, rstd = stat3()
                    ssb, _u1, _u2 = stat3()
                    act(junk4[:, 0:512], pm[:, 0:512], AF.Square, accum=ss)
                    act(junk4[:, 512:1024], pm[:, 512:1024], AF.Square, accum=ssb)
                    tt('dve', ss, ss, ssb, ALU.add)
                    act(sdv, ss, AF.Sqrt, scale=1.0 / D, bias=epsc)
                    recip('dve', rstd, sdv)
                    for nb in range(2):
                        stt('dve', t4[k_][:, nb * 512:(nb + 1) * 512], pm[:, nb * 512:(nb + 1) * 512], rstd, gpost_b[:, nb * 512:(nb + 1) * 512], ALU.mult, ALU.mult)
                    tt('dve', o4[k_], t4[k_], xt4[k_], ALU.add)
                    FINAL_OPS.append(dma(yv[mt, i], o4[k_]))

        Sd.analyze()
        Sd.emit(final_wait_ops=FINAL_OPS)
    return nc


def _consts():
    ident = np.eye(128, dtype=np.float32)
    ii = np.arange(128) // 16
    maskf = (ii[None, :] >= ii[:, None]).astype(np.float32)
    maskb = (ii[:, None] >= ii[None, :]).astype(np.float32)
    cst = np.stack([ident, maskf, maskb], axis=1).copy()
    kvals = np.tile(np.arange(-7, 9, dtype=np.float32)[None, :], (128, 1)).copy()
    mvals = np.tile(np.arange(256, dtype=np.float32)[None, :], (128, 1)).copy()
    invf = (np.float32(10000.0) ** (-np.arange(0, 64, 2, dtype=np.float32) / np.float32(64))).astype(np.float32)
    invf64 = np.concatenate([invf, invf])
    return cst, kvals, mvals, invf64


def _prep_shared(inp):
    f = np.float32
    cst, kvals, mvals, invf64 = _consts()
    smallv = np.zeros((128, 64), f)
    smallv[:, 0:8] = np.asarray(inp["pre_norm_g"], f).reshape(8, 128).T
    smallv[:, 8:10] = np.asarray(inp["q_norm_g"], f).reshape(2, 128).T
    smallv[:, 10:11] = np.asarray(inp["kv_norm_g"], f).reshape(1, 128).T
    gout = np.concatenate([np.asarray(inp["attn_out_g"], f).reshape(-1), np.asarray(inp["ssm_out_g"], f).reshape(-1)])
    smallv[:, 11:19] = gout.reshape(8, 128).T
    smallv[:, 19:23] = np.asarray(inp["b_glu"], f).reshape(4, 128).T
    smallv[0:64, 23] = invf64
    smallv[:, 24] = EPS

    def pm(a):
        a = np.asarray(a, f).reshape(2, 16, 2, 64)
        return a.transpose(2, 3, 0, 1).reshape(128, 32)

    logdt = np.asarray(inp["s5_log_dt"], f).reshape(2, 16, 2)
    logdt_t = np.broadcast_to(logdt.transpose(2, 0, 1)[:, None, :, :], (2, 64, 2, 16)).reshape(128, 32)
    dsgn = np.concatenate([np.ones((128, 16), f), -np.ones((128, 16), f)], axis=1)
    s5p = np.stack([pm(inp["s5_lam_re"][0]), pm(inp["s5_lam_im"][0]), logdt_t, dsgn], axis=1).astype(f).copy()

    def pb(a):
        a = np.asarray(a, f).reshape(2, 16, 2, 64, 16)
        return a.transpose(2, 3, 0, 1, 4).reshape(128, 512)

    def pc(a):
        a = np.asarray(a, f).reshape(2, 16, 2, 16, 64)
        return a.transpose(2, 4, 0, 1, 3).reshape(128, 512)

    s5bc = np.stack([pb(inp["s5_b_re"][0]), pb(inp["s5_b_im"][0]), pc(inp["s5_c_re"][0]), pc(inp["s5_c_im"][0])], axis=1).astype(f).copy()
    dsk = np.asarray(inp["s5_d"], f).reshape(32, 16)
    dvec = np.broadcast_to(dsk.T[None, :, :], (8, 16, 32)).reshape(128, 32).astype(f).copy()
    shared = {
        "w_in": np.ascontiguousarray(np.asarray(inp["w_in"], f)[0]),
        "w_uq": np.ascontiguousarray(np.asarray(inp["w_uq"], f)[0]),
        "w_ukv": np.ascontiguousarray(np.asarray(inp["w_ukv"], f)[0]),
        "w_glu": np.ascontiguousarray(np.asarray(inp["w_glu"], f)[0]),
        "w_out": np.ascontiguousarray(np.asarray(inp["w_out"], f)[0]),
        "smallv": smallv,
        "gpost": np.asarray(inp["post_norm_g"], f).reshape(1, D).copy(),
        "s5p": s5p, "s5bc": s5bc, "dvec": dvec, "cst": cst, "kvals": kvals, "mvals": mvals,
    }
    return shared


def _perm_pos(p):
    n = p.shape[0]
    return np.ascontiguousarray(p.reshape(n, 2, 128, 8).transpose(0, 1, 3, 2).reshape(n, S)).astype(np.int32)


_NC_CACHE = {}


def kernel(**inputs):
    x = np.asarray(inputs["x"], np.float32)
    positions = np.asarray(inputs["positions"], np.int32)
    B = x.shape[0]
    ncores = 8
    nseq = B // ncores
    shared = _prep_shared(inputs)
    if nseq not in _NC_CACHE:
        _NC_CACHE[nseq] = build(nseq)
    nc = _NC_CACHE[nseq]
    in_maps = []
    for c in range(ncores):
        m = dict(shared)
        m["x"] = np.ascontiguousarray(x[c * nseq:(c + 1) * nseq])
        m["pos"] = _perm_pos(positions[c * nseq:(c + 1) * nseq])
        in_maps.append(m)
    res = run_bass_kernel_spmd(nc, in_maps, core_ids=list(range(ncores)))
    out = np.concatenate([np.asarray(r["y"]) for r in res.results], axis=0)
    return out.astype(np.float32)
```
